# Optimizing a Trainium2 kernel written in Bass

```python
import math
import jax, jax.numpy as jnp
from jax import lax
import numpy as np

D_MODEL = 2048
BATCH = 8
SEQ = 4096
DEPTH = 4
DEC_BATCH = 2
DEC_SEQ = 4096
PAST_LEN = 128

GRID_W = 64
WIN_H = 8
WIN_W = 16
DH = 128
H_A = D_MODEL // 256
D_A = H_A * DH
D_B = D_MODEL // 2
EMB = 33
FO = 64
FAST_DECAY = 0.3
SLOW_DECAY = 1.5
DECAY_TARGET = 1e-2
D_C = D_MODEL
G_C = 8
DG_C = D_C // G_C
CHUNK = 128
D_FF = 4 * D_MODEL
N_EVEN = (DEPTH + 1) // 2
N_ODD = DEPTH // 2
D_IN_AB = 3 * D_A + 3 * D_B
D_MIX_AB = D_A + D_B
EPS = 1e-6

kernel_name = "hybrid_natten_hyena_gmlp_encoder"


def rmsnorm(x, g):
    xf = x.astype(jnp.float32)
    y = xf * lax.rsqrt(jnp.mean(xf * xf, axis=-1, keepdims=True) + EPS) * g.astype(jnp.float32)
    return y.astype(x.dtype)


def neighbourhood_attention(q, k, v, q_gain, k_gain, rpb):
    B, L, H, Dh = q.shape
    rows = L // GRID_W
    kh = min(WIN_H, rows)
    q = (rmsnorm(q, q_gain) * (Dh ** -0.5)).reshape(B, rows, GRID_W, H, Dh)
    k = rmsnorm(k, k_gain).reshape(B, rows, GRID_W, H, Dh)
    v = v.reshape(B, rows, GRID_W, H, Dh)
    col = jnp.arange(GRID_W)
    cs = jnp.clip(col - WIN_W // 2, 0, GRID_W - WIN_W)
    qc, kc = col[:, None], col[None, :]
    col_ok = (kc >= cs[:, None]) & (kc < cs[:, None] + WIN_W)
    dc = jnp.clip(kc - qc, -(WIN_W - 1), WIN_W - 1) + WIN_W - 1
    bias_c = rpb.astype(jnp.float32)[:, :, dc]
    bias_c = jnp.where(col_ok, bias_c, -jnp.inf)

    def row_block(r):
        rs = jnp.clip(r - kh // 2, 0, rows - kh)
        q_r = lax.dynamic_index_in_dim(q, r, axis=1, keepdims=False)
        k_r = lax.dynamic_slice_in_dim(k, rs, kh, axis=1)
        v_r = lax.dynamic_slice_in_dim(v, rs, kh, axis=1)
        dr = rs + jnp.arange(kh) - r + WIN_H - 1
        bias = jnp.take(bias_c, dr, axis=1).transpose(0, 2, 1, 3)
        s = jnp.einsum('bqhd,bikhd->bhqik', q_r, k_r).astype(jnp.float32) + bias
        p = jax.nn.softmax(s.reshape(B, H, GRID_W, kh * GRID_W), axis=-1)
        p = p.reshape(B, H, GRID_W, kh, GRID_W).astype(v.dtype)
        return jnp.einsum('bhqik,bikhd->bqhd', p, v_r)

    o = lax.map(row_block, jnp.arange(rows))
    return o.transpose(1, 0, 2, 3, 4).reshape(B, L, H * Dh)


def short_conv3(z, w, b):
    zp = jnp.pad(z, ((0, 0), (1, 1), (0, 0)))
    return zp[:, :-2] * w[0] + zp[:, 1:-1] * w[1] + zp[:, 2:] * w[2] + b


def hyena_kernel(L, f_w1, f_b1, f_w2, f_b2, f_w3, f_b3, f_wout, f_freq):
    f32 = jnp.float32
    t = jnp.linspace(0.0, 1.0, L, dtype=f32)[:, None]
    bands = (EMB - 1) // 2
    w = 2.0 * math.pi * jnp.arange(L, dtype=f32)[:, None] / L
    f = jnp.linspace(1e-4, bands - 1, bands, dtype=f32)[None, :]
    z = jnp.concatenate([t, jnp.cos(f * w), -jnp.sin(f * w)], axis=-1)
    fr = f_freq.astype(f32)
    h = jnp.sin(fr * (z @ f_w1.astype(f32) + f_b1.astype(f32)))
    h = jnp.sin(fr * (h @ f_w2.astype(f32) + f_b2.astype(f32)))
    h = jnp.sin(fr * (h @ f_w3.astype(f32) + f_b3.astype(f32)))
    h = h @ f_wout.astype(f32)
    deltas = jnp.abs(jnp.linspace(math.log(DECAY_TARGET) / FAST_DECAY,
                                  math.log(DECAY_TARGET) / SLOW_DECAY, D_B, dtype=f32))
    decay = jnp.exp(-t * deltas)
    h_fwd = h[:, :D_B] * decay
    h_bwd = h[:, D_B:] * decay
    kern = jnp.concatenate([h_fwd, jnp.zeros((1, D_B), f32), h_bwd[:0:-1]], axis=0)
    return kern / jnp.sum(jnp.abs(kern), axis=0, keepdims=True)


def hyena(z, kern, h_bias):
    B, L, _ = z.shape
    x0, x1, v = jnp.split(z, 3, axis=-1)
    u = (v * x1).astype(jnp.float32)
    y = jnp.fft.irfft(jnp.fft.rfft(u, n=2 * L, axis=1) * jnp.fft.rfft(kern, axis=0)[None],
                      n=2 * L, axis=1)[:, :L]
    y = y + u * h_bias.astype(jnp.float32)
    return (y * x0.astype(jnp.float32)).astype(z.dtype)


def mixer_ab(h, w_in, sc_w, sc_b, q_gain, k_gain, rpb, f_w1, f_b1, f_w2, f_b2, f_w3, f_b3,
             f_wout, f_freq, h_bias, w_out):
    B, L, _ = h.shape
    p = h @ w_in
    q = p[..., :D_A].reshape(B, L, H_A, DH)
    k = p[..., D_A:2 * D_A].reshape(B, L, H_A, DH)
    v = p[..., 2 * D_A:3 * D_A].reshape(B, L, H_A, DH)
    a = neighbourhood_attention(q, k, v, q_gain, k_gain, rpb)
    z = short_conv3(p[..., 3 * D_A:], sc_w, sc_b)
    kern = hyena_kernel(L, f_w1, f_b1, f_w2, f_b2, f_w3, f_b3, f_wout, f_freq)
    b = hyena(z, kern, h_bias)
    return jnp.concatenate([a, b], axis=-1) @ w_out


def mixer_c(h, w_in, v_gain, w_s, b_s, w_out):
    B, L, _ = h.shape
    zz = jax.nn.gelu(h @ w_in)
    u, v = jnp.split(zz, 2, axis=-1)
    v = rmsnorm(v, v_gain).reshape(B, L // CHUNK, CHUNK, G_C, DG_C)
    s = jnp.einsum('gpq,bnqgc->bnpgc', w_s, v) + b_s.T[None, None, :, :, None]
    return (u * s.reshape(B, L, D_C)) @ w_out


def mlp_sqrelu(h, w1, w2):
    return jnp.square(jax.nn.relu(h @ w1)) @ w2


def trunk(x, norm_mix, norm_mlp, w_in_ab, sc_w, sc_b, q_gain, k_gain, rpb, f_w1, f_b1, f_w2, f_b2,
          f_w3, f_b3, f_wout, f_freq, h_bias, w_out_ab, w_in_c, v_gain, w_s, b_s, w_out_c,
          w_mlp1, w_mlp2):
    for i in range(DEPTH):
        j = i // 2
        h = rmsnorm(x, norm_mix[i])
        if i % 2 == 0:
            x = x + mixer_ab(h, w_in_ab[j], sc_w[j], sc_b[j], q_gain[j], k_gain[j], rpb[j],
                             f_w1[j], f_b1[j], f_w2[j], f_b2[j], f_w3[j], f_b3[j], f_wout[j],
                             f_freq[j], h_bias[j], w_out_ab[j])
        else:
            x = x + mixer_c(h, w_in_c[j], v_gain[j], w_s[j], b_s[j], w_out_c[j])
        x = x + mlp_sqrelu(rmsnorm(x, norm_mlp[i]), w_mlp1[i], w_mlp2[i])
    return x


def setup_inputs(seed: int = 0) -> dict:
    key = jax.random.key(seed)
    ks = iter(jax.random.split(key, 40))

    def nrm(shape, scale):
        return jax.random.normal(next(ks), shape, jnp.float32) * scale

    def gain(shape):
        return 1.0 + nrm(shape, 0.05)

    return {
        "x_prompt": nrm((BATCH, SEQ, D_MODEL), 1.0),
        "x_sample": nrm((DEC_BATCH, DEC_SEQ, D_MODEL), 1.0),
        "norm_mix": gain((DEPTH, D_MODEL)),
        "norm_mlp": gain((DEPTH, D_MODEL)),
        "w_in_ab": nrm((N_EVEN, D_MODEL, D_IN_AB), D_MODEL ** -0.5),
        "sc_w": nrm((N_EVEN, 3, 3 * D_B), 3 ** -0.5),
        "sc_b": nrm((N_EVEN, 3 * D_B), 0.02),
        "q_gain": gain((N_EVEN, DH)),
        "k_gain": gain((N_EVEN, DH)),
        "rpb": nrm((N_EVEN, H_A, 2 * WIN_H - 1, 2 * WIN_W - 1), 0.1),
        "f_w1": nrm((N_EVEN, EMB, FO), EMB ** -0.5),
        "f_b1": nrm((N_EVEN, FO), 0.1),
        "f_w2": nrm((N_EVEN, FO, FO), FO ** -0.5),
        "f_b2": nrm((N_EVEN, FO), 0.1),
        "f_w3": nrm((N_EVEN, FO, FO), FO ** -0.5),
        "f_b3": nrm((N_EVEN, FO), 0.1),
        "f_wout": nrm((N_EVEN, FO, 2 * D_B), FO ** -0.5),
        "f_freq": 1.0 + nrm((N_EVEN, FO), 0.1),
        "h_bias": nrm((N_EVEN, D_B), 0.5),
        "w_out_ab": nrm((N_EVEN, D_MIX_AB, D_MODEL), D_MIX_AB ** -0.5),
        "w_in_c": nrm((N_ODD, D_MODEL, 2 * D_C), D_MODEL ** -0.5),
        "v_gain": gain((N_ODD, D_C)),
        "w_s": nrm((N_ODD, G_C, CHUNK, CHUNK), CHUNK ** -0.5),
        "b_s": 1.0 + nrm((N_ODD, G_C, CHUNK), 0.1),
        "w_out_c": nrm((N_ODD, D_C, D_MODEL), D_C ** -0.5),
        "w_mlp1": nrm((DEPTH, D_MODEL, D_FF), D_MODEL ** -0.5),
        "w_mlp2": nrm((DEPTH, D_FF, D_MODEL), D_FF ** -0.5),
    }


def reference(x_prompt, x_sample, norm_mix, norm_mlp, w_in_ab, sc_w, sc_b, q_gain, k_gain, rpb,
              f_w1, f_b1, f_w2, f_b2, f_w3, f_b3, f_wout, f_freq, h_bias, w_out_ab, w_in_c,
              v_gain, w_s, b_s, w_out_c, w_mlp1, w_mlp2):
    y_prompt = trunk(x_prompt, norm_mix, norm_mlp, w_in_ab, sc_w, sc_b, q_gain, k_gain, rpb,
                     f_w1, f_b1, f_w2, f_b2, f_w3, f_b3, f_wout, f_freq, h_bias, w_out_ab,
                     w_in_c, v_gain, w_s, b_s, w_out_c, w_mlp1, w_mlp2)
    y_sample = trunk(x_sample, norm_mix, norm_mlp, w_in_ab, sc_w, sc_b, q_gain, k_gain, rpb,
                     f_w1, f_b1, f_w2, f_b2, f_w3, f_b3, f_wout, f_freq, h_bias, w_out_ab,
                     w_in_c, v_gain, w_s, b_s, w_out_c, w_mlp1, w_mlp2)
    return (y_prompt, y_sample)
```

```python
import math
from contextlib import ExitStack

import numpy as np
import ml_dtypes

import concourse.bass as bass
import concourse.mybir as mybir
from concourse.bass_utils import run_bass_kernel_spmd

F32 = mybir.dt.float32
BF16 = mybir.dt.bfloat16
AF = mybir.ActivationFunctionType
ALU = mybir.AluOpType

D = 2048
L = 4096
KC = D // 128
T = 512
NT = L // T
DFF = 4 * D
DEPTH = 4
D_A = 1024
D_B = 1024
NH = 8
EPS = 1e-6
GELU_C = math.sqrt(2.0 / math.pi)


class Prog:
    COMPUTE = ("pe", "act", "dve", "pool")

    def __init__(self, nc):
        self.nc = nc
        self.ops = []
        self.lastw = {}
        self.readers = {}
        self.ps_next = 0
        self.last_by_src = {}
        self.fence_pending = {}
        self.pos = []

    def barrier(self):
        f = dict(self.last_by_src)
        for e in ("pe", "act", "dve", "pool", "sp"):
            d = dict(self.fence_pending.get(e, {}))
            d.update(f)
            self.fence_pending[e] = d

    def _src(self, idx):
        op = self.ops[idx]
        return ("dma", op[3]) if op[3] is not None else ("eng", op[0])

    def add(self, eng, fn, r=(), w=(), dma_grp=None, pos=None):
        idx = len(self.ops)
        self.pos.append(idx if pos is None else pos)
        deps = self.fence_pending.pop(eng, None) or {}

        def need(d):
            s = self._src(d)
            if deps.get(s, -1) < d:
                deps[s] = d

        for k in r:
            lw = self.lastw.get(k)
            if lw is not None:
                need(lw)
        for k in w:
            lw = self.lastw.get(k)
            if lw is not None:
                need(lw)
            rd = self.readers.get(k)
            if rd:
                for d in rd.values():
                    need(d)
        self.ops.append([eng, fn, deps, dma_grp])
        me = ("dma", dma_grp) if dma_grp is not None else ("eng", eng)
        self.last_by_src[me] = idx
        for k in w:
            self.lastw[k] = idx
            self.readers[k] = {}
        for k in r:
            self.readers.setdefault(k, {})[me] = idx
        return idx

    def mm(self, out, lhsT, rhs, start, stop, r, w):
        return self.add("pe", lambda e: e.matmul(out, lhsT, rhs, start=start, stop=stop), r, w)

    def tr(self, out, in_, ident, r, w):
        return self.add("pe", lambda e: e.transpose(out, in_, ident), r, w)

    def act(self, out, in_, func, r, w, bias=None, scale=None):
        kw = {}
        if bias is not None:
            kw["bias"] = bias
        if scale is not None:
            kw["scale"] = scale
        return self.add("act", lambda e: e.activation(out, in_, func, **kw), r, w)

    def v(self, eng, name, args, r, w, **kw):
        return self.add(eng, lambda e: getattr(e, name)(*args, **kw), r, w)

    def dma(self, q, out, in_, r, w, grp, pos=None, **kw):
        return self.add(q, lambda e: e.dma_start(out=out, in_=in_, **kw), r, w, dma_grp=grp, pos=pos)

    def emit(self, stack, same_engine_sync=True):
        nc = self.nc
        ops = self.ops
        signal = set()
        for idx, (eng, fn, deps, grp) in enumerate(ops):
            for s, d in deps.items():
                if s[0] == "eng":
                    if s[1] == eng and grp is None and (eng == "pe" or not same_engine_sync):
                        continue
                    signal.add(d)
        sem_eng = {e: stack.enter_context(nc.semaphore("sem_" + e)) for e in self.COMPUTE}
        grp_names = []
        for op in ops:
            if op[3] is not None and op[3] not in grp_names:
                grp_names.append(op[3])
        sem_grp = {g: stack.enter_context(nc.semaphore("dg_%d" % i)) for i, g in enumerate(grp_names)}
        cnt = {}
        run = {e: 0 for e in self.COMPUTE}
        rung = {g: 0 for g in grp_names}
        prevg = {}
        for idx, (eng, fn, deps, grp) in enumerate(ops):
            if grp is not None:
                prevg[idx] = rung[grp]
                rung[grp] += 16
                cnt[idx] = rung[grp]
            elif idx in signal:
                run[eng] += 1
                cnt[idx] = run[eng]
        per_eng = {e: [] for e in ("pe", "act", "dve", "pool", "sp")}
        for idx, op in enumerate(ops):
            per_eng[op[0]].append(idx)
        for e_ in per_eng:
            per_eng[e_].sort(key=lambda i: (self.pos[i], i))
        final_g = dict(rung)

        def emit_engine(ename, e):
            waited = {}
            for idx in per_eng[ename]:
                eng, fn, deps, grp = ops[idx]
                for s, d in deps.items():
                    if s[0] == "eng":
                        if s[1] == eng and grp is None and (eng == "pe" or not same_engine_sync):
                            continue
                        sem = sem_eng[s[1]]
                    else:
                        sem = sem_grp[s[1]]
                    c = cnt[d]
                    if waited.get(s, 0) >= c:
                        continue
                    e.wait_ge(sem, c)
                    waited[s] = c
                if grp is not None:
                    s = ("dma", grp)
                    if prevg[idx] > 0 and waited.get(s, 0) < prevg[idx]:
                        e.wait_ge(sem_grp[grp], prevg[idx])
                        waited[s] = prevg[idx]
                    fn(e).then_inc(sem_grp[grp], 16)
                else:
                    ins = fn(e)
                    if idx in signal:
                        ins.then_inc(sem_eng[eng], 1)
            if ename == "sp":
                for g, c in final_g.items():
                    if c > 0:
                        e.wait_ge(sem_grp[g], c)

        with nc.Block() as block:
            @block.tensor
            def _(e):
                emit_engine("pe", e)

            @block.scalar
            def _(e):
                emit_engine("act", e)

            @block.vector
            def _(e):
                emit_engine("dve", e)

            @block.gpsimd
            def _(e):
                emit_engine("pool", e)

            @block.sync
            def _(e):
                emit_engine("sp", e)


NSLOT = 21
NEG = -30000.0


class Builder:
    def __init__(self, cfg):
        self.cfg = cfg
        self.nseq = cfg.get("nseq", 2)
        self.tiles = cfg.get("tiles", list(range(NT)))
        self.layers = cfg.get("layers", list(range(DEPTH)))
        nc = bass.Bass("TRN2", target_bir_lowering=False)
        self.nc = nc
        self.P = Prog(nc)
        self.stack = ExitStack()
        self.pstack = None
        self.nm = 0
        self._decl_dram()

    def din(self, name, shape, dt=F32):
        return self.nc.dram_tensor(name, list(shape), dt, kind="ExternalInput").ap()

    def dscr(self, name, shape, dt=F32):
        if self.cfg.get("debug", False):
            return self.nc.dram_tensor(name, list(shape), dt, kind="ExternalOutput").ap()
        return self.nc.dram_tensor(name, list(shape), dt).ap()

    def _decl_dram(self):
        nc = self.nc
        ns = self.nseq
        self.x_in = self.din("x_in", [ns, L, D])
        self.y_out = nc.dram_tensor("y_out", [ns, L, D], F32, kind="ExternalOutput").ap()
        self.norm_mix = self.din("norm_mix", [DEPTH, 128, KC])
        self.norm_mlp = self.din("norm_mlp", [DEPTH, 128, KC])
        self.w_mlp1 = self.din("w_mlp1", [DEPTH, D, DFF])
        self.w_mlp2 = self.din("w_mlp2", [DEPTH, DFF, D])
        self.w_in_c = self.din("w_in_c", [2, D, 2 * D])
        self.v_gain = self.din("v_gain", [2, D])
        self.w_sT = self.din("w_sT", [2, 8, 128, 128])
        self.b_s = self.din("b_s", [2, 8 * 128])
        self.w_out_c = self.din("w_out_c", [2, D, D])
        self.ident = self.din("ident", [128, 128])
        self.w_in_ab = self.din("w_in_ab", [2, D, 6144])
        self.w_out_ab = self.din("w_out_ab", [2, D, D])
        self.qk_gain = self.din("qk_gain", [2, 128, 2])
        self.biasT = self.din("biasT", [2, NH, 128, NSLOT, 128], BF16)
        self.scw = self.din("scw", [2, 128, 4, 24])
        self.hbias = self.din("hbias", [2, 128, 8])
        self.f_w1 = self.din("f_w1", [2, 33, 64])
        self.f_w2 = self.din("f_w2", [2, 64, 64])
        self.f_w3 = self.din("f_w3", [2, 64, 64])
        self.f_wout = self.din("f_wout", [2, 64, 2048])
        self.f_vec = self.din("f_vec", [2, 64, 4])
        self.zfeat = self.din("zfeat", [33, L])
        self.deltas = self.din("deltas", [D_B])
        self.tcol = self.din("tcol", [128, 32])
        self.TF = self.din("TF", [2, 32, 128, 32, 128], BF16)
        self.TI = self.din("TI", [2, L, L], BF16)
        self.wb = {}
        for nm_, ap_ in (("w_mlp1", self.w_mlp1), ("w_mlp2", self.w_mlp2), ("w_in_c", self.w_in_c),
                         ("w_out_c", self.w_out_c), ("w_in_ab", self.w_in_ab), ("w_out_ab", self.w_out_ab)):
            self.wb[nm_] = self.nc.dram_tensor(nm_ + "_bf", list(ap_.shape), BF16).ap()
        self.xs = self.dscr("xs_scratch", [D, L])
        self.qT_s = self.dscr("qT_s", [D_A, L], BF16)
        self.kT_s = self.dscr("kT_s", [D_A, L], BF16)
        self.v_s = self.dscr("v_s", [L, D_A], BF16)
        self.zT_s = self.dscr("zT_s", [3 * D_B, L])
        self.uT_s = self.dscr("uT_s", [D_B, L])
        self.x0_s = self.dscr("x0_s", [D_B, L])
        self.u_s = self.dscr("u_s", [L, D_B], BF16)
        self.abT_s = self.dscr("abT_s", [D, L], BF16)
        self.Ksp = self.dscr("Ksp", [2, 2, L, D_B])

    def sb(self, name, shape, dt):
        return self.stack.enter_context(self.nc.sbuf_tensor(name, list(shape), dt))

    def sbp(self, name, shape, dt):
        self.nm += 1
        return self.pstack.enter_context(self.nc.sbuf_tensor("%s_%d" % (name, self.nm), list(shape), dt))

    def phase_begin(self):
        self.w_hist = []
        self.pstack = ExitStack()
        self.ph = self.nm

    def phase_end(self):
        self.P.barrier()
        self.pstack.close()
        self.pstack = None

    def bank(self, avoid=None):
        b = self.P.ps_next
        if avoid is not None and b == avoid:
            b = (b + 1) % 8
        self.P.ps_next = (b + 1) % 8
        return b

    def build(self):
        P = self.P
        nc = self.nc
        self.ps = [self.stack.enter_context(nc.psum_tensor("ps%d" % i, [128, 512], F32)) for i in range(8)]
        self.ones = self.sb("ones", [128, 128], F32)
        self.identf = self.sb("identf", [128, 128], F32)
        self.onesb = self.sb("onesb", [128, 128], BF16)
        self.identb = self.sb("identb", [128, 128], BF16)
        self.epsc = self.sb("epsc", [128, 4], F32)
        P.v("pool", "memset", (self.ones[:], 1.0), [], ["ones"])
        P.v("pool", "memset", (self.onesb[:], 1.0), [], ["onesb"])
        P.v("pool", "memset", (self.epsc[:, 0:1], D * EPS), [], ["epsc"])
        P.v("pool", "memset", (self.epsc[:, 1:2], EPS), [], ["epsc"])
        P.v("pool", "memset", (self.epsc[:, 2:3], 128 * EPS), [], ["epsc"])
        P.v("pool", "memset", (self.epsc[:, 3:4], 0.0), [], ["epsc"])
        self.epsD = self.epsc[:, 0:1]
        self.eps1 = self.epsc[:, 1:2]
        self.eps128 = self.epsc[:, 2:3]
        P.dma("sp", self.identf[:], self.ident, [], ["identf"], "c_ident")
        P.dma("pool", self.identb[:], self.ident, [], ["identb"], "c_identb")
        self.gvec = self.sb("gvec", [128, 2 * DEPTH, KC], F32)
        self.gsc = self.sb("gsc", [128, 2 * DEPTH, KC], F32)
        P.dma("sp", self.gvec[:, 0:DEPTH, :], self.norm_mix.rearrange("l p k -> p l k"), [], ["gvec"], "c_g1")
        P.dma("sp", self.gvec[:, DEPTH:2 * DEPTH, :], self.norm_mlp.rearrange("l p k -> p l k"), [], ["gvec"], "c_g2")
        P.v("dve", "tensor_scalar", (self.gsc[:], self.gvec[:], math.sqrt(D), None, ALU.mult), ["gvec"], ["gsc"])
        self.qkg = self.sb("qkg", [128, 2, 2], F32)
        P.dma("sp", self.qkg[:], self.qk_gain.rearrange("j p c -> p j c"), [], ["qkg"], "c_qk")
        P.v("dve", "tensor_scalar", (self.qkg[:, :, 1:2], self.qkg[:, :, 1:2], math.sqrt(128.0), None, ALU.mult),
            ["qkg"], ["qkg"])
        ci = 0
        need = {"w_mlp1": self.layers, "w_mlp2": self.layers,
                "w_in_c": sorted({l // 2 for l in self.layers if l % 2 == 1}),
                "w_out_c": sorted({l // 2 for l in self.layers if l % 2 == 1}),
                "w_in_ab": sorted({l // 2 for l in self.layers if l % 2 == 0}),
                "w_out_ab": sorted({l // 2 for l in self.layers if l % 2 == 0})}
        srcs = {"w_mlp1": self.w_mlp1, "w_mlp2": self.w_mlp2, "w_in_c": self.w_in_c, "w_out_c": self.w_out_c,
                "w_in_ab": self.w_in_ab, "w_out_ab": self.w_out_ab}
        for nm_, idxs in need.items():
            src = srcs[nm_]
            rows = src.shape[1]
            for li in idxs:
                for r0 in range(0, rows, 512):
                    P.dma("pool", self.wb[nm_][li, r0:r0 + 512, :].rearrange("(p a) c -> p a c", p=128),
                          src[li, r0:r0 + 512, :].rearrange("(p a) c -> p a c", p=128), [], [("wb", nm_, li, r0)],
                          "cv%d" % (ci % 4))
                    ci += 1
        runs_filter = any(l % 2 == 0 for l in self.layers) and "F" in self.cfg.get("even_phases", ("F",))
        if not runs_filter:
            P.barrier()

        even_layers = [l for l in self.layers if l % 2 == 0]
        for l in even_layers:
            if "F" in self.cfg.get("even_phases", ("F",)):
                self.filter_phase(l // 2)
        for s in range(self.nseq):
            i = 0
            first = True
            while i < len(self.layers):
                l = self.layers[i]
                if l % 2 == 0:
                    eph = self.cfg.get("even_phases", ("E1", "ATT", "HA", "HBC"))
                    if "E1" in eph:
                        self.e1_phase(s, l, from_input=first)
                    if "ATT" in eph:
                        self.attn_phase(l // 2)
                    if "HA" in eph:
                        self.hyena_a_phase(l // 2)
                    if "HBC" in eph:
                        self.hyena_bc_phase(l // 2)
                    if "TILE" not in self.cfg.get("even_phases", ("TILE",)):
                        i += 1
                        continue
                    nxt = self.layers[i + 1] if i + 1 < len(self.layers) and self.layers[i + 1] == l + 1 else None
                    last = (i + (2 if nxt is not None else 1)) >= len(self.layers)
                    self.tile_phase(s, even=l, odd=nxt, from_input=False, to_output=last)
                    i += 2 if nxt is not None else 1
                else:
                    last = (i + 1) >= len(self.layers)
                    self.tile_phase(s, even=None, odd=l, from_input=first, to_output=last)
                    i += 1
                first = False
        P.emit(self.stack, same_engine_sync=self.cfg.get("same_sync", True))
        self.stack.close()
        return nc

    def alloc_tile_bufs(self, odd_j=None):
        P = self.P
        self.xT = self.sbp("xT", [128, KC, T], F32)
        self.xn = self.sbp("xn", [128, KC, T], BF16)
        self.big = self.sbp("big", [128, 32, T], BF16)
        self.wsl = [self.sbp("wsl%d" % i, [128, 16 * 512], BF16) for i in range(3)]
        self.wsl_next = 0
        self.sqb = [self.sbp("sqb%d" % i, [128, T], BF16) for i in range(4)]
        self.rstd = self.sbp("rstd", [128, T], F32)
        self.tmp = [self.sbp("tmp%d" % i, [128, T], F32) for i in range(6)]
        self.tmp_next = 0
        self.tb = [self.sbp("tb%d" % i, [128, T], BF16) for i in range(4)]
        self.tb_next = 0
        self.iot = self.sbp("iot", [128, D], F32)
        if odd_j is not None:
            j = odd_j
            self.vg = self.sbp("vg", [128, D], F32)
            self.bsb = self.sbp("bsb", [128, 8 * 128], F32)
            self.wst = self.sbp("wst", [128, 8, 128], BF16)
            P.dma("sp", self.vg[:], self.v_gain[j].partition_broadcast(128), [], ["vg"], "c_vg")
            P.dma("sp", self.bsb[:], self.b_s[j].partition_broadcast(128), [], ["bsb"], "c_bs")
            P.dma("pool", self.wst[:], self.w_sT[j].rearrange("g q p -> q g p"), [], ["wst"], "c_ws")

    def wslot(self):
        i = self.wsl_next
        self.wsl_next = (i + 1) % 3
        return i

    def tmpslot(self):
        i = self.tmp_next
        self.tmp_next = (i + 1) % len(self.tmp)
        return i

    def tbslot(self):
        i = self.tb_next
        self.tb_next = (i + 1) % len(self.tb)
        return i

    def load_tile(self, s, t, from_input):
        P = self.P
        if from_input:
            for sub in range(4):
                r0 = t * T + sub * 128
                P.dma("sp", self.iot[:], self.x_in[s, r0:r0 + 128, :], [], ["iot"], "g_iot")
                for kq in range(4):
                    b = self.bank()
                    for kk in range(4):
                        k = kq * 4 + kk
                        P.tr(self.ps[b][:, kk * 128:(kk + 1) * 128], self.iot[:, k * 128:(k + 1) * 128],
                             self.identf[:], ["iot", "identf"], [("ps", b)])
                    for kk in range(4):
                        k = kq * 4 + kk
                        P.v("dve", "tensor_copy", (self.xT[:, k, sub * 128:(sub + 1) * 128],
                                                   self.ps[b][:, kk * 128:(kk + 1) * 128]),
                            [("ps", b)], [("xT", k)])
        else:
            P.dma("sp", self.xT[:], self.xs.rearrange("(k p) n -> p k n", p=128)[:, :, t * T:(t + 1) * T],
                  [("xs", t)], [("xT", k) for k in range(KC)], "g_xT")

    def store_tile(self, s, t, to_output):
        P = self.P
        if to_output:
            for sub in range(4):
                r0 = t * T + sub * 128
                for kq in range(4):
                    b = self.bank()
                    for kk in range(4):
                        k = kq * 4 + kk
                        P.tr(self.ps[b][:, kk * 128:(kk + 1) * 128], self.xT[:, k, sub * 128:(sub + 1) * 128],
                             self.identf[:], [("xT", k), "identf"], [("ps", b)])
                    P.act(self.iot[:, kq * 512:(kq + 1) * 512], self.ps[b][:], AF.Copy, [("ps", b)], ["iot"])
                P.dma("sp", self.y_out[s, r0:r0 + 128, :], self.iot[:], ["iot"], [("yout", s, t, sub)], "g_iot")
        else:
            P.dma("sp", self.xs.rearrange("(k p) n -> p k n", p=128)[:, :, t * T:(t + 1) * T], self.xT[:],
                  [("xT", k) for k in range(KC)], [("xs", t)], "g_xT")

    def rmsnorm_tile(self, gi):
        P = self.P
        b = self.bank()
        for k in range(KC):
            sl = k % 4
            if k % 2 == 0:
                P.act(self.sqb[sl][:], self.xT[:, k, :], AF.Square, [("xT", k)], [("sqb", sl)])
            else:
                P.v("pool", "tensor_tensor", (self.sqb[sl][:], self.xT[:, k, :], self.xT[:, k, :], ALU.mult),
                    [("xT", k)], [("sqb", sl)])
            P.mm(self.ps[b][:], self.onesb[:], self.sqb[sl][:], k == 0, k == KC - 1,
                 ["onesb", ("sqb", sl)], [("ps", b)])
        P.act(self.rstd[:], self.ps[b][:], AF.Sqrt, [("ps", b), "epsc"], ["rstd"], bias=self.epsD)
        P.v("dve", "reciprocal", (self.rstd[:], self.rstd[:]), ["rstd"], ["rstd"])
        for k in range(KC):
            P.v("dve", "scalar_tensor_tensor",
                (self.xn[:, k, :], self.xT[:, k, :], self.gsc[:, gi, k:k + 1], self.rstd[:], ALU.mult, ALU.mult),
                [("xT", k), "gsc", "rstd"], [("xn", k)])

    def load_w(self, wap, nk, c0, ncols, r0=0):
        P = self.P
        i = self.wslot()
        dst = self.wsl[i][:, 0:nk * ncols].rearrange("p (k n) -> p k n", n=ncols)
        src = wap[r0:r0 + nk * 128, :].rearrange("(k p) n -> p k n", p=128)[:, :, c0:c0 + ncols]
        hist = self.w_hist
        pos = None
        if len(hist) >= 2:
            pos = hist[-2] + 0.5
        idx = P.dma("sp", dst, src, [], [("wsl", i)], "g_w%d" % i, pos=pos)
        hist.append(idx)
        return i, dst

    def mlp(self, l):
        P = self.P
        self.rmsnorm_tile(DEPTH + l)
        w1 = self.wb["w_mlp1"][l]
        w2 = self.wb["w_mlp2"][l]
        for hh in range(2):
            for jb in range(8):
                wi, wv = self.load_w(w1, KC, (hh * 8 + jb) * 512, 512)
                for jj in range(4):
                    j = jb * 4 + jj
                    b = self.bank()
                    for k in range(KC):
                        P.mm(self.ps[b][:], wv[:, k, jj * 128:(jj + 1) * 128], self.xn[:, k, :], k == 0, k == KC - 1,
                             [("wsl", wi), ("xn", k)], [("ps", b)])
                    ts = self.tmpslot()
                    P.act(self.tmp[ts][:], self.ps[b][:], AF.Relu, [("ps", b)], [("tmp", ts)])
                    P.v("pool", "tensor_tensor", (self.big[:, j, :], self.tmp[ts][:], self.tmp[ts][:], ALU.mult),
                        [("tmp", ts)], [("big", j)])
            for mp in range(8):
                i, dst = self.load_w(w2, 32, mp * 256, 256, r0=hh * 4096)
                for m2 in range(2):
                    m = mp * 2 + m2
                    b = self.bank()
                    for j in range(32):
                        P.mm(self.ps[b][:], dst[:, j, m2 * 128:(m2 + 1) * 128], self.big[:, j, :], j == 0, j == 31,
                             [("wsl", i), ("big", j)], [("ps", b)])
                    P.v("dve", "tensor_tensor", (self.xT[:, m, :], self.xT[:, m, :], self.ps[b][:], ALU.add),
                        [("xT", m), ("ps", b)], [("xT", m)])

    def gelu(self, out, b, wkeys):
        P = self.P
        a = self.tmpslot()
        c = self.tmpslot()
        xs_, t_ = self.tmp[a], self.tmp[c]
        P.act(xs_[:], self.ps[b][:], AF.Copy, [("ps", b)], [("tmp", a)])
        P.v("pool", "tensor_tensor", (t_[:], xs_[:], xs_[:], ALU.mult), [("tmp", a)], [("tmp", c)])
        P.v("pool", "tensor_scalar", (t_[:], t_[:], 0.044715, 1.0, ALU.mult, ALU.add), [("tmp", c)], [("tmp", c)])
        P.v("dve", "tensor_tensor", (t_[:], t_[:], xs_[:], ALU.mult), [("tmp", a), ("tmp", c)], [("tmp", c)])
        P.act(t_[:], t_[:], AF.Sigmoid, [("tmp", c)], [("tmp", c)], scale=2.0 * GELU_C)
        P.v("dve", "tensor_tensor", (out, t_[:], xs_[:], ALU.mult), [("tmp", a), ("tmp", c)], wkeys)

    def mixer_c(self, l):
        P = self.P
        j = l // 2
        self.rmsnorm_tile(l)
        w_in = self.wb["w_in_c"][j]
        for cb in range(4):
            wi, wv = self.load_w(w_in, KC, cb * 512, 512)
            for cc in range(4):
                c = cb * 4 + cc
                b = self.bank()
                for k in range(KC):
                    P.mm(self.ps[b][:], wv[:, k, cc * 128:(cc + 1) * 128], self.xn[:, k, :], k == 0, k == KC - 1,
                         [("wsl", wi), ("xn", k)], [("ps", b)])
                self.gelu(self.big[:, c, :], b, [("big", c)])
        vt = [self.big[:, 16 + 4 * sub:16 + 4 * sub + 4, :].rearrange("p a b -> p (a b)") for sub in range(4)]
        vkeys = [[("big", 16 + 4 * sub + i) for i in range(4)] for sub in range(4)]
        for cb in range(4):
            wi, wv = self.load_w(w_in, KC, D + cb * 512, 512)
            for sub in range(4):
                b = self.bank()
                for k in range(KC):
                    P.mm(self.ps[b][:], self.xn[:, k, sub * 128:(sub + 1) * 128], wv[:, k, :], k == 0, k == KC - 1,
                         [("wsl", wi), ("xn", k)], [("ps", b)])
                self.gelu(vt[sub][:, cb * 512:(cb + 1) * 512], b, [vkeys[sub][cb]])
        for sub in range(4):
            a = self.tmpslot()
            c = self.tmpslot()
            ssq = self.tmp[a]
            junk = self.tmp[c]
            for cb in range(4):
                P.act(junk[:], vt[sub][:, cb * 512:(cb + 1) * 512], AF.Square, [vkeys[sub][cb]], [("tmp", c)])
                P.v("dve", "reduce_sum", (ssq[:, cb:cb + 1], junk[:], mybir.AxisListType.X),
                    [("tmp", c)], [("tmp", a)])
            P.v("dve", "reduce_sum", (ssq[:, 4:5], ssq[:, 0:4], mybir.AxisListType.X), [("tmp", a)], [("tmp", a)])
            P.act(ssq[:, 5:6], ssq[:, 4:5], AF.Sqrt, [("tmp", a), "epsc"], [("tmp", a)], bias=self.eps1, scale=1.0 / D)
            P.v("dve", "reciprocal", (ssq[:, 6:7], ssq[:, 5:6]), [("tmp", a)], [("tmp", a)])
            for cb in range(4):
                P.v("dve", "scalar_tensor_tensor",
                    (vt[sub][:, cb * 512:(cb + 1) * 512], vt[sub][:, cb * 512:(cb + 1) * 512], ssq[:, 6:7],
                     self.vg[:, cb * 512:(cb + 1) * 512], ALU.mult, ALU.mult),
                    [vkeys[sub][cb], ("tmp", a), "vg"], [vkeys[sub][cb]])
        for sub in range(4):
            for cq in range(4):
                b = self.bank()
                for cc in range(4):
                    c = cq * 4 + cc
                    g = c // 2
                    P.mm(self.ps[b][:, cc * 128:(cc + 1) * 128], vt[sub][:, c * 128:(c + 1) * 128],
                         self.wst[:, g, :], True, True, [vkeys[sub][cq], "wst"], [("ps", b)])
                for cc in range(4):
                    c = cq * 4 + cc
                    g = c // 2
                    ts = self.tmpslot()
                    P.v("dve", "tensor_tensor", (self.tmp[ts][:, 0:128], self.ps[b][:, cc * 128:(cc + 1) * 128],
                                                 self.bsb[:, g * 128:(g + 1) * 128], ALU.add),
                        [("ps", b), "bsb"], [("tmp", ts)])
                    P.v("pool", "tensor_tensor", (self.big[:, c, sub * 128:(sub + 1) * 128], self.tmp[ts][:, 0:128],
                                                  self.big[:, c, sub * 128:(sub + 1) * 128], ALU.mult),
                        [("tmp", ts), ("big", c)], [("big", c)])
        self.proj_residual(self.wb["w_out_c"][j])

    def proj_residual(self, w_o):
        P = self.P
        for mb in range(4):
            wi, wv = self.load_w(w_o, KC, mb * 512, 512)
            for m4 in range(4):
                m = mb * 4 + m4
                b = self.bank()
                for k in range(KC):
                    P.mm(self.ps[b][:], wv[:, k, m4 * 128:(m4 + 1) * 128], self.big[:, k, :], k == 0, k == KC - 1,
                         [("wsl", wi), ("big", k)], [("ps", b)])
                P.v("dve", "tensor_tensor", (self.xT[:, m, :], self.xT[:, m, :], self.ps[b][:], ALU.add),
                    [("xT", m), ("ps", b)], [("xT", m)])

    def tile_phase(self, s, even, odd, from_input, to_output):
        P = self.P
        self.phase_begin()
        self.alloc_tile_bufs(odd_j=(odd // 2 if odd is not None else None))
        for t in self.tiles:
            self.load_tile(s, t, from_input=from_input)
            if even is not None:
                P.dma("sp", self.big[:, 0:KC, :],
                      self.abT_s.rearrange("(k p) n -> p k n", p=128)[:, :, t * T:(t + 1) * T],
                      [("abT", t)], [("big", k) for k in range(KC)], "g_ab")
                self.proj_residual(self.wb["w_out_ab"][even // 2])
                self.mlp(even)
            if odd is not None:
                self.mixer_c(odd)
                self.mlp(odd)
            self.store_tile(s, t, to_output=to_output)
        self.phase_end()

    def e1_phase(self, s, l, from_input):
        P = self.P
        j = l // 2
        self.phase_begin()
        self.alloc_tile_bufs()
        w_in = self.wb["w_in_ab"][j]
        for t in self.tiles:
            self.load_tile(s, t, from_input=from_input)
            if from_input:
                self.store_tile(s, t, to_output=False)
            self.rmsnorm_tile(l)
            tsl = slice(t * T, (t + 1) * T)
            pending = []

            def tail(b, a, isq, h):
                b2 = self.bank()
                P.mm(self.ps[b2][:], self.onesb[:], self.sqb[a][:], True, True, ["onesb", ("sqb", a)], [("ps", b2)])
                c = self.tmpslot()
                P.act(self.tmp[c][:], self.ps[b2][:], AF.Sqrt, [("ps", b2), "epsc"], [("tmp", c)], bias=self.eps128)
                P.v("dve", "reciprocal", (self.tmp[c][:], self.tmp[c][:]), [("tmp", c)], [("tmp", c)])
                o = self.tbslot()
                P.v("dve", "scalar_tensor_tensor",
                    (self.tb[o][:], self.ps[b][:], self.qkg[:, j, (0 if isq else 1):(1 if isq else 2)],
                     self.tmp[c][:], ALU.mult, ALU.mult), [("ps", b), "qkg", ("tmp", c)], [("tb", o)])
                dst = (self.qT_s if isq else self.kT_s)[h * 128:(h + 1) * 128, tsl]
                P.dma("sp", dst, self.tb[o][:], [("tb", o)], [("qk", isq, h, t)], "g_tb%d" % o)

            nq = 0
            for blk in range(4):
                wi, wv = self.load_w(w_in, KC, blk * 512, 512)
                isq = blk < 2
                for cc in range(4):
                    h = (blk % 2) * 4 + cc
                    b = self.bank()
                    for k in range(KC):
                        P.mm(self.ps[b][:], wv[:, k, cc * 128:(cc + 1) * 128], self.xn[:, k, :], k == 0, k == KC - 1,
                             [("wsl", wi), ("xn", k)], [("ps", b)])
                    a = nq % 4
                    nq += 1
                    P.act(self.sqb[a][:], self.ps[b][:], AF.Square, [("ps", b)], [("sqb", a)])
                    if pending:
                        tail(*pending.pop())
                    pending.append((b, a, isq, h))
            tail(*pending.pop())
            for blk in range(2):
                wi, wv = self.load_w(w_in, KC, 2048 + blk * 512, 512)
                for sub in range(4):
                    b = self.bank()
                    for k in range(KC):
                        P.mm(self.ps[b][:], self.xn[:, k, sub * 128:(sub + 1) * 128], wv[:, k, :], k == 0, k == KC - 1,
                             [("wsl", wi), ("xn", k)], [("ps", b)])
                    o = self.tbslot()
                    P.act(self.tb[o][:], self.ps[b][:], AF.Copy, [("ps", b)], [("tb", o)])
                    r0 = t * T + sub * 128
                    P.dma("sp", self.v_s[r0:r0 + 128, blk * 512:(blk + 1) * 512], self.tb[o][:],
                          [("tb", o)], [("vs", t, sub, blk)], "g_tb%d" % o)
            for blk in range(6):
                wi, wv = self.load_w(w_in, KC, 3072 + blk * 512, 512)
                for cc in range(4):
                    c = blk * 4 + cc
                    b = self.bank()
                    for k in range(KC):
                        P.mm(self.ps[b][:], wv[:, k, cc * 128:(cc + 1) * 128], self.xn[:, k, :], k == 0, k == KC - 1,
                             [("wsl", wi), ("xn", k)], [("ps", b)])
                    a = self.tmpslot()
                    P.act(self.tmp[a][:], self.ps[b][:], AF.Copy, [("ps", b)], [("tmp", a)])
                    P.dma("sp", self.zT_s[c * 128:(c + 1) * 128, tsl], self.tmp[a][:],
                          [("tmp", a)], [("zs", c, t)], "g_tmp%d" % a)
        self.phase_end()

    def attn_phase(self, j):
        P = self.P
        self.phase_begin()
        qT = [self.sbp("qT", [128, L], BF16) for _ in range(2)]
        kT = [self.sbp("kT", [128, L], BF16) for _ in range(2)]
        vh = [self.sbp("vh", [128, 32, 128], BF16) for _ in range(2)]
        bh = [self.sbp("bh", [128, NSLOT, 128], BF16) for _ in range(2)]
        aT = [self.sbp("aT", [128, L], BF16) for _ in range(2)]
        eT = [self.sbp("eT", [128, 5, 128], BF16) for _ in range(3)]
        rd = [self.sbp("rd", [128, 128], F32) for _ in range(3)]
        for h in range(NH):
            sl = h % 2
            P.dma("sp", qT[sl][:], self.qT_s[h * 128:(h + 1) * 128, :], [("qk", True, h)], [("qT", sl)], "a_q%d" % sl)
            P.dma("sp", kT[sl][:], self.kT_s[h * 128:(h + 1) * 128, :], [("qk", False, h)], [("kT", sl)], "a_k%d" % sl)
            P.dma("sp", vh[sl][:], self.v_s.rearrange("(n p) c -> p n c", p=128)[:, :, h * 128:(h + 1) * 128],
                  [], [("vh", sl)], "a_v%d" % sl)
            P.dma("sp", bh[sl][:], self.biasT[j, h], [], [("bh", sl)], "a_b%d" % sl)
            for m in range(32):
                if 2 <= m <= 29:
                    nch, krow0, slot0 = 5, 2 * m - 4, 0
                elif m < 2:
                    nch, krow0, slot0 = 4, 0, 5 + 4 * m
                else:
                    nch, krow0, slot0 = 4, 56, 13 + 4 * (m - 30)
                qs = slice(m * 128, (m + 1) * 128)
                b1 = self.bank()
                b2 = self.bank() if nch == 5 else None
                ei = (h * 32 + m) % 3
                for c in range(nch):
                    bb, co = (b1, c) if c < 4 else (b2, 0)
                    k0 = (krow0 + 2 * c) * 64
                    outp = self.ps[bb][:, co * 128:(co + 1) * 128]
                    P.mm(outp, kT[sl][:, k0:k0 + 128], qT[sl][:, qs], True, False,
                         [("kT", sl), ("qT", sl)], [("ps", bb)])
                    P.mm(outp, self.identb[:], bh[sl][:, slot0 + c, :], False, True,
                         ["identb", ("bh", sl)], [("ps", bb)])
                P.act(eT[ei][:, 0:4, :].rearrange("p a b -> p (a b)"), self.ps[b1][:], AF.Exp,
                      [("ps", b1)], [("eT", ei)])
                if nch == 5:
                    P.act(eT[ei][:, 4, :], self.ps[b2][:, 0:128], AF.Exp, [("ps", b2)], [("eT", ei)])
                b3 = self.bank()
                for c in range(nch):
                    vi = krow0 // 2 + c
                    P.mm(self.ps[b3][:, 0:128], vh[sl][:, vi, :], eT[ei][:, c, :], c == 0, c == nch - 1,
                         [("vh", sl), ("eT", ei)], [("ps", b3)])
                for c in range(nch):
                    P.mm(self.ps[b3][:, 128:256], self.onesb[:], eT[ei][:, c, :], c == 0, c == nch - 1,
                         ["onesb", ("eT", ei)], [("ps", b3)])
                P.v("dve", "reciprocal", (rd[ei][:], self.ps[b3][:, 128:256]), [("ps", b3)], [("rd", ei)])
                P.v("dve", "tensor_tensor", (aT[sl][:, qs], self.ps[b3][:, 0:128], rd[ei][:], ALU.mult),
                    [("ps", b3), ("rd", ei)], [("aT", sl)])
            P.dma("sp", self.abT_s[h * 128:(h + 1) * 128, :], aT[sl][:], [("aT", sl)], [("abT", "a", h)], "a_o%d" % sl)
        self.phase_end()

    def hyena_a_phase(self, j):
        P = self.P
        self.phase_begin()
        zb = [self.sbp("zb", [128, L + 2], F32) for _ in range(3)]
        cv = [self.sbp("cv", [128, L], F32) for _ in range(3)]
        uT = self.sbp("uT", [128, L], F32)
        ust = self.sbp("ust", [128, 32, 128], BF16)
        scw = self.sbp("scw", [128, 4, 24], F32)
        P.dma("sp", scw[:], self.scw[j], [], ["scw"], "h_scw")
        for i in range(3):
            P.v("pool", "memset", (zb[i][:, 0:1], 0.0), [], [("zb", i)])
            P.v("pool", "memset", (zb[i][:, L + 1:L + 2], 0.0), [], [("zb", i)])
        for cc in range(8):
            for i in range(3):
                ci = i * 8 + cc
                eng = "dve"
                P.dma("sp", zb[i][:, 1:L + 1], self.zT_s[ci * 128:(ci + 1) * 128, :], [("zs", ci)], [("zb", i)], "h_z%d" % i)
                P.v(eng, "tensor_scalar", (cv[i][:], zb[i][:, 1:L + 1], scw[:, 1, ci:ci + 1], scw[:, 3, ci:ci + 1],
                                           ALU.mult, ALU.add), [("zb", i), "scw"], [("cv", i)])
                P.v(eng, "scalar_tensor_tensor", (cv[i][:], zb[i][:, 0:L], scw[:, 0, ci:ci + 1], cv[i][:],
                                                  ALU.mult, ALU.add), [("zb", i), "scw", ("cv", i)], [("cv", i)])
                P.v(eng, "scalar_tensor_tensor", (cv[i][:], zb[i][:, 2:L + 2], scw[:, 2, ci:ci + 1], cv[i][:],
                                                  ALU.mult, ALU.add), [("zb", i), "scw", ("cv", i)], [("cv", i)])
            P.v("pool", "tensor_tensor", (uT[:], cv[2][:], cv[1][:], ALU.mult), [("cv", 1), ("cv", 2)], ["uT"])
            P.dma("sp", self.uT_s[cc * 128:(cc + 1) * 128, :], uT[:], ["uT"], [("uTs", cc)], "h_u")
            P.dma("sp", self.x0_s[cc * 128:(cc + 1) * 128, :], cv[0][:], [("cv", 0)], [("x0s", cc)], "h_x0")
            for g in range(8):
                b = self.bank()
                for q in range(4):
                    n = g * 4 + q
                    P.tr(self.ps[b][:, q * 128:(q + 1) * 128], uT[:, n * 128:(n + 1) * 128], self.identf[:],
                         ["uT", "identf"], [("ps", b)])
                P.act(ust[:, g * 4:(g + 1) * 4, :].rearrange("p a b -> p (a b)"), self.ps[b][:], AF.Copy,
                      [("ps", b)], ["ust"])
            P.dma("sp", self.u_s.rearrange("(n p) c -> p n c", p=128)[:, :, cc * 128:(cc + 1) * 128], ust[:],
                  ["ust"], [("us", cc)], "h_us")
        self.phase_end()

    def dft_fwd(self, srcC, keysC, srcS, keysS, tf, consume):
        P = self.P
        for fi in range(32):
            banks = []
            for cs in range(2):
                sl = (fi * 2 + cs) % 4
                P.dma("sp", tf[sl][:], self.TF[cs, fi], [], [("tf", sl)], "d_tf%d" % sl)
                b = self.bank()
                src, keys = (srcC, keysC) if cs == 0 else (srcS, keysS)
                for n in range(32):
                    P.mm(self.ps[b][:], tf[sl][:, n, :], src[:, n, :], n == 0, n == 31,
                         [("tf", sl), keys(n)], [("ps", b)])
                banks.append(b)
            consume(fi, banks[0], banks[1])

    def hyena_bc_phase(self, j):
        P = self.P
        self.phase_begin()
        uti = self.sbp("uti", [128, 32, 512], BF16)
        Y = self.sbp("Y", [128, 64, 512], BF16)
        tf = [self.sbp("tf", [128, 32, 128], BF16) for _ in range(4)]
        kt = [self.sbp("kt", [128, 512], F32) for _ in range(4)]
        et = [self.sbp("et", [128, 512], F32) for _ in range(4)]
        tt = [self.sbp("tt", [128, 512], F32) for _ in range(4)]
        ob = [self.sbp("ob", [128, 512], BF16) for _ in range(2)]
        hb = self.sbp("hb", [128, 8], F32)
        P.dma("sp", hb[:], self.hbias[j], [], ["hb"], "d_hb")
        cnt = [0, 0, 0]
        for ch in range(2):
            csl = slice(ch * 512, (ch + 1) * 512)
            P.dma("sp", uti[:], self.u_s.rearrange("(n p) c -> p n c", p=128)[:, :, csl],
                  [("us", c) for c in range(8)], [("uti", 0), ("uti", 1)], "d_u")

            def consume(fi, bP, bQ, ch=ch, csl=csl):
                k0 = cnt[0] % 2
                cnt[0] += 1
                kp, kq = kt[2 * k0], kt[2 * k0 + 1]
                fsl = slice(fi * 128, (fi + 1) * 128)
                P.dma("sp", kp[:], self.Ksp[j, 0, fsl, csl], [("Ksp", j)], [("kt", 2 * k0)], "d_kp%d" % k0)
                P.dma("sp", kq[:], self.Ksp[j, 1, fsl, csl], [("Ksp", j)], [("kt", 2 * k0 + 1)], "d_kq%d" % k0)
                t0, t1, t2, t3 = tt
                base = (cnt[0] % 1) * 0
                P.v("dve", "tensor_tensor", (t0[:], self.ps[bP][:], kp[:], ALU.mult), [("ps", bP), ("kt", 2 * k0)], [("tt", 0)])
                P.v("dve", "tensor_tensor", (t1[:], self.ps[bQ][:], kq[:], ALU.mult), [("ps", bQ), ("kt", 2 * k0 + 1)], [("tt", 1)])
                P.v("dve", "tensor_tensor", (t2[:], self.ps[bP][:], kq[:], ALU.mult), [("ps", bP), ("kt", 2 * k0 + 1)], [("tt", 2)])
                P.v("dve", "tensor_tensor", (t3[:], self.ps[bQ][:], kp[:], ALU.mult), [("ps", bQ), ("kt", 2 * k0)], [("tt", 3)])
                P.v("pool", "tensor_tensor", (Y[:, fi, :], t0[:], t1[:], ALU.subtract), [("tt", 0), ("tt", 1)], [("Y", fi)])
                P.v("pool", "tensor_tensor", (Y[:, 32 + fi, :], t2[:], t3[:], ALU.add), [("tt", 2), ("tt", 3)], [("Y", 32 + fi)])

            self.dft_fwd(uti, lambda n: ("uti", n // 16), uti, lambda n: ("uti", n // 16), tf, consume)
            for tb in range(8):
                tsl = slice(tb * 512, (tb + 1) * 512)
                banks = [self.bank() for _ in range(4)]
                for kg in range(4):
                    cs, f0 = kg // 2, (kg % 2) * 2048
                    sl = cnt[1] % 2
                    cnt[1] += 1
                    ti = uti[:, sl * 16:(sl + 1) * 16, :]
                    P.dma("sp", ti, self.TI[cs, f0:f0 + 2048, :].rearrange("(k p) t -> p k t", p=128)[:, :, tsl],
                          [], [("uti", sl)], "d_ti%d" % sl)
                    for c in range(4):
                        for kk in range(16):
                            yk = cs * 32 + (kg % 2) * 16 + kk
                            P.mm(self.ps[banks[c]][:], Y[:, yk, c * 128:(c + 1) * 128], ti[:, kk, :],
                                 kg == 0 and kk == 0, kg == 3 and kk == 15,
                                 [("Y", yk), ("uti", sl)], [("ps", banks[c])])
                for c in range(4):
                    cc = ch * 4 + c
                    e0 = cnt[2] % 2
                    cnt[2] += 1
                    eu, ex = et[2 * e0], et[2 * e0 + 1]
                    P.dma("sp", eu[:], self.uT_s[cc * 128:(cc + 1) * 128, tsl], [("uTs", cc)], [("et", 2 * e0)], "d_eu%d" % e0)
                    P.dma("sp", ex[:], self.x0_s[cc * 128:(cc + 1) * 128, tsl], [("x0s", cc)], [("et", 2 * e0 + 1)], "d_ex%d" % e0)
                    P.v("dve", "scalar_tensor_tensor", (eu[:], eu[:], hb[:, cc:cc + 1], self.ps[banks[c]][:], ALU.mult, ALU.add),
                        [("et", 2 * e0), "hb", ("ps", banks[c])], [("et", 2 * e0)])
                    P.v("pool", "tensor_tensor", (ob[e0][:], eu[:], ex[:], ALU.mult),
                        [("et", 2 * e0), ("et", 2 * e0 + 1)], [("ob", e0)])
                    P.dma("sp", self.abT_s[1024 + cc * 128:1024 + (cc + 1) * 128, tsl], ob[e0][:],
                          [("ob", e0)], [("abT", "b", cc, tb)], "d_ob%d" % e0)
        self.phase_end()

    def filter_phase(self, j):
        P = self.P
        self.phase_begin()
        zf = self.sbp("zf", [33, L], F32)
        hA = self.sbp("hA", [64, L], F32)
        hB = self.sbp("hB", [64, L], F32)
        fw1 = self.sbp("fw1", [33, 64], F32)
        fw2 = self.sbp("fw2", [64, 64], F32)
        fw3 = self.sbp("fw3", [64, 64], F32)
        fwo = self.sbp("fwo", [64, 2048], F32)
        fv = self.sbp("fv", [64, 4], F32)
        fs = self.sbp("fs", [64, 4], F32)
        dl = self.sbp("dl", [128, D_B], F32)
        tc = self.sbp("tc", [128, 32], F32)
        hs = self.sbp("hs", [128, 32, 512], BF16)
        hd = self.sbp("hd", [128, 32, 512], BF16)
        dec = [self.sbp("dec", [128, 512], F32) for _ in range(2)]
        hf = [self.sbp("hf", [128, 512], F32) for _ in range(2)]
        hbw = [self.sbp("hbw", [128, 512], F32) for _ in range(2)]
        ab = [self.sbp("ab", [128, 512], F32) for _ in range(2)]
        ab2 = [self.sbp("ab2", [128, 512], F32) for _ in range(2)]
        rn = self.sbp("rn", [128, 512], F32)
        st = [self.sbp("st", [64, 512], F32) for _ in range(2)]
        s2 = [self.sbp("s2", [64, 512], F32) for _ in range(2)]
        tf = [self.sbp("tf", [128, 32, 128], BF16) for _ in range(4)]
        ko = [self.sbp("ko", [128, 512], F32) for _ in range(4)]
        P.dma("sp", zf[:], self.zfeat, [], ["zf"], "f_c0")
        P.dma("sp", fw1[:], self.f_w1[j], [], ["fw1"], "f_c1")
        P.dma("sp", fw2[:], self.f_w2[j], [], ["fw2"], "f_c2")
        P.dma("sp", fw3[:], self.f_w3[j], [], ["fw3"], "f_c3")
        P.dma("sp", fwo[:], self.f_wout[j], [], ["fwo"], "f_c4")
        P.dma("sp", fv[:], self.f_vec[j], [], ["fv"], "f_c5")
        P.dma("sp", dl[:], self.deltas.partition_broadcast(128), [], ["dl"], "f_c6")
        P.dma("sp", tc[:], self.tcol, [], ["tc"], "f_c7")
        P.v("dve", "tensor_scalar", (fs[:, 0:1], fv[:, 3:4], 1.0 / 3.0, None, ALU.mult), ["fv"], ["fs"])
        P.v("dve", "tensor_scalar", (fs[:, 1:4], fv[:, 0:3], fs[:, 0:1], None, ALU.mult), ["fv", "fs"], ["fs"])
        P.v("dve", "tensor_scalar", (tc[:], tc[:], -1.0, None, ALU.mult), ["tc"], ["tc"])

        def sin_layer(w, wkey, src, skey, dst, dkey, li, kdim):
            for nb in range(8):
                b = self.bank()
                P.mm(self.ps[b][0:64, :], w[0:kdim, :], src[0:kdim, nb * 512:(nb + 1) * 512], True, True,
                     [wkey, skey], [("ps", b)])
                a = nb % 2
                P.act(st[a][:], self.ps[b][0:64, :], AF.Sin, [("ps", b), "fs"], [("st", a)],
                      bias=fs[:, li:li + 1], scale=fs[:, 0:1])
                P.v("dve", "tensor_tensor", (s2[a][:], st[a][:], st[a][:], ALU.mult), [("st", a)], [("s2", a)])
                P.v("dve", "tensor_scalar", (s2[a][:], s2[a][:], -4.0, 3.0, ALU.mult, ALU.add), [("s2", a)], [("s2", a)])
                P.v("dve", "tensor_tensor", (dst[:, nb * 512:(nb + 1) * 512], s2[a][:], st[a][:], ALU.mult),
                    [("st", a), ("s2", a)], [dkey])

        sin_layer(fw1, "fw1", zf, "zf", hA, "hA", 1, 33)
        sin_layer(fw2, "fw2", hA, "hA", hB, "hB", 2, 64)
        sin_layer(fw3, "fw3", hB, "hB", hA, "hA", 3, 64)
        cnt = [0]
        for ch in range(2):
            csl = slice(ch * 512, (ch + 1) * 512)
            bn = self.bank()
            for n in range(32):
                a = n % 2
                bf_, bb_ = self.bank(avoid=bn), self.bank(avoid=bn)
                P.mm(self.ps[bf_][:], hA[:, n * 128:(n + 1) * 128], fwo[:, ch * 512:(ch + 1) * 512], True, True,
                     ["hA", "fwo"], [("ps", bf_)])
                P.mm(self.ps[bb_][:], hA[:, n * 128:(n + 1) * 128], fwo[:, D_B + ch * 512:D_B + (ch + 1) * 512], True, True,
                     ["hA", "fwo"], [("ps", bb_)])
                P.act(dec[a][:], dl[:, csl], AF.Exp, ["dl", "tc"], [("dec", a)], scale=tc[:, n:n + 1])
                P.v("dve", "tensor_tensor", (hf[a][:], self.ps[bf_][:], dec[a][:], ALU.mult),
                    [("ps", bf_), ("dec", a)], [("hf", a)])
                P.v("dve", "tensor_tensor", (hbw[a][:], self.ps[bb_][:], dec[a][:], ALU.mult),
                    [("ps", bb_), ("dec", a)], [("hbw", a)])
                if n == 0:
                    P.v("dve", "memset", (hbw[a][0:1, :], 0.0), [("hbw", a)], [("hbw", a)])
                P.act(ab[a][:], hf[a][:], AF.Abs, [("hf", a)], [("ab", a)])
                P.act(ab2[a][:], hbw[a][:], AF.Abs, [("hbw", a)], [("ab2", a)])
                P.v("dve", "tensor_tensor", (ab[a][:], ab[a][:], ab2[a][:], ALU.add), [("ab", a), ("ab2", a)], [("ab", a)])
                P.mm(self.ps[bn][:], self.ones[:], ab[a][:], n == 0, n == 31, ["ones", ("ab", a)], [("ps", bn)])
                P.v("dve", "tensor_tensor", (hs[:, n, :], hf[a][:], hbw[a][:], ALU.add), [("hf", a), ("hbw", a)], [("hs", n // 16)])
                P.v("dve", "tensor_tensor", (hd[:, n, :], hf[a][:], hbw[a][:], ALU.subtract), [("hf", a), ("hbw", a)], [("hd", n // 16)])
            P.v("dve", "reciprocal", (rn[:], self.ps[bn][:]), [("ps", bn)], ["rn"])

            def consume(fi, bP, bQ, ch=ch, csl=csl):
                k0 = cnt[0] % 2
                cnt[0] += 1
                fsl = slice(fi * 128, (fi + 1) * 128)
                for cs, bb in ((0, bP), (1, bQ)):
                    o = 2 * k0 + cs
                    P.v("dve", "tensor_tensor", (ko[o][:], self.ps[bb][:], rn[:], ALU.mult), [("ps", bb), "rn"], [("ko", o)])
                    P.dma("sp", self.Ksp[j, cs, fsl, csl], ko[o][:], [("ko", o)], [("Ksp", j, cs, fi, ch)], "f_ko%d" % o)

            self.dft_fwd(hs, lambda n: ("hs", n // 16), hd, lambda n: ("hd", n // 16), tf, consume)
        self.phase_end()


_CONST_CACHE = {}


def algo_consts():
    if _CONST_CACHE:
        return _CONST_CACHE
    f32 = np.float32
    c = {}
    c["ident"] = np.eye(128, dtype=f32)
    t = np.linspace(0.0, 1.0, L, dtype=f32)[:, None]
    w = (2.0 * np.pi * np.arange(L, dtype=f32)[:, None] / L).astype(f32)
    f = np.linspace(1e-4, 15, 16, dtype=f32)[None, :]
    fw = (f * w).astype(f32)
    z = np.concatenate([t, np.cos(fw), -np.sin(fw)], axis=-1).astype(f32)
    c["zfeat"] = np.ascontiguousarray(z.T)
    c["deltas"] = np.abs(np.linspace(math.log(1e-2) / 0.3, math.log(1e-2) / 1.5, D_B, dtype=f32)).astype(f32)
    c["tcol"] = np.ascontiguousarray(t[:, 0].reshape(32, 128).T)
    n = np.arange(L, dtype=np.float64)[:, None]
    fq = np.arange(L, dtype=np.float64)[None, :]
    ang = np.pi * (2 * fq + 1) * n / (2 * L)
    C = np.cos(ang)
    S = np.sin(ang)
    TF = np.stack([C, S])
    TF = TF.reshape(2, 32, 128, 32, 128).transpose(0, 3, 2, 1, 4)
    c["TF"] = np.ascontiguousarray(TF).astype(ml_dtypes.bfloat16)
    TI = np.stack([C.T, S.T]) / L
    c["TI"] = np.ascontiguousarray(TI).astype(ml_dtypes.bfloat16)
    _CONST_CACHE.update(c)
    return c


def bias_tiles(rpb):
    rpb = np.asarray(rpb, np.float32)
    specs = [(2, [0, 2, 4, 6, 8])] + [(m, [0, 2, 4, 6]) for m in (0, 1)] + [(m, [56, 58, 60, 62]) for m in (30, 31)]
    kc = np.arange(64)[:, None]
    qc = np.arange(64)[None, :]
    cs = np.clip(qc - 8, 0, 48)
    col_ok = (kc >= cs) & (kc < cs + 16)
    dc = np.clip(kc - qc, -15, 15) + 15
    out = np.full((2, NH, 128, NSLOT, 128), NEG, np.float32)
    slot = 0
    for m, krows in specs:
        for kr0 in krows:
            for pk in range(2):
                kr = kr0 + pk
                for pq in range(2):
                    r = 2 * m + pq
                    rs = min(max(r - 4, 0), 56)
                    if not (rs <= kr < rs + 8):
                        continue
                    dr = kr - r + 7
                    vals = rpb[:, :, dr, :][:, :, dc]
                    vals = np.where(col_ok[None, None], vals, NEG)
                    out[:, :, pk * 64:(pk + 1) * 64, slot, pq * 64:(pq + 1) * 64] = vals
            slot += 1
    return out.astype(ml_dtypes.bfloat16)


def host_consts(inp):
    f32 = np.float32
    g = lambda k: np.asarray(inp[k], f32)
    c = dict(algo_consts())
    c["norm_mix"] = np.ascontiguousarray(g("norm_mix").reshape(DEPTH, KC, 128).transpose(0, 2, 1))
    c["norm_mlp"] = np.ascontiguousarray(g("norm_mlp").reshape(DEPTH, KC, 128).transpose(0, 2, 1))
    c["w_mlp1"] = g("w_mlp1")
    c["w_mlp2"] = g("w_mlp2")
    c["w_in_c"] = g("w_in_c")
    c["v_gain"] = g("v_gain")
    c["w_sT"] = np.ascontiguousarray(g("w_s").transpose(0, 1, 3, 2))
    c["b_s"] = np.ascontiguousarray(g("b_s").reshape(2, 8 * 128))
    c["w_out_c"] = g("w_out_c")
    c["w_in_ab"] = g("w_in_ab")
    c["w_out_ab"] = g("w_out_ab")
    c["qk_gain"] = np.ascontiguousarray(np.stack([g("q_gain"), g("k_gain")], axis=-1))
    c["biasT"] = bias_tiles(inp["rpb"])
    scw = np.concatenate([g("sc_w"), g("sc_b")[:, None, :]], axis=1)
    c["scw"] = np.ascontiguousarray(scw.reshape(2, 4, 24, 128).transpose(0, 3, 1, 2))
    c["hbias"] = np.ascontiguousarray(g("h_bias").reshape(2, 8, 128).transpose(0, 2, 1))
    c["f_w1"] = g("f_w1")
    c["f_w2"] = g("f_w2")
    c["f_w3"] = g("f_w3")
    c["f_wout"] = g("f_wout")
    c["f_vec"] = np.ascontiguousarray(np.stack([g("f_b1"), g("f_b2"), g("f_b3"), g("f_freq")], axis=-1))
    return c


def run_cfg(cfg, x_per_core, inp, trace=False):
    b = Builder(cfg)
    nc = b.build()
    consts = host_consts(inp)
    in_maps = []
    for xc in x_per_core:
        m = dict(consts)
        m["x_in"] = np.ascontiguousarray(xc, dtype=np.float32)
        in_maps.append(m)
    res = run_bass_kernel_spmd(nc, in_maps, core_ids=list(range(len(x_per_core))), trace=trace)
    return res


def kernel(**inputs):
    xp = np.asarray(inputs["x_prompt"], np.float32)
    xsm = np.asarray(inputs["x_sample"], np.float32)
    x_per_core = []
    for c in range(8):
        second = xsm[c] if c < 2 else xp[c]
        x_per_core.append(np.stack([xp[c], second]))
    res = run_cfg({"nseq": 2}, x_per_core, inputs)
    y_prompt = np.stack([res.results[c]["y_out"][0] for c in range(8)]).astype(np.float32)
    y_sample = np.stack([res.results[c]["y_out"][1] for c in range(2)]).astype(np.float32)
    return (y_prompt, y_sample)
```

```python
import math
from contextlib import ExitStack

import numpy as np
import ml_dtypes

import concourse.bass as bass
import concourse.mybir as mybir
from concourse.bass_utils import run_bass_kernel_spmd

F32 = mybir.dt.float32
BF16 = mybir.dt.bfloat16
AF = mybir.ActivationFunctionType
ALU = mybir.AluOpType

D = 2048
L = 4096
KC = D // 128
T = 512
NT = L // T
DFF = 4 * D
DEPTH = 4
D_A = 1024
D_B = 1024
NH = 8
EPS = 1e-6
GELU_C = math.sqrt(2.0 / math.pi)


class Prog:
    COMPUTE = ("pe", "act", "dve", "pool")

    def __init__(self, nc):
        self.nc = nc
        self.ops = []
        self.lastw = {}
        self.readers = {}
        self.ps_next = 0
        self.last_by_src = {}
        self.fence_pending = {}
        self.pos = []

    def barrier(self):
        f = dict(self.last_by_src)
        for e in ("pe", "act", "dve", "pool", "sp"):
            d = dict(self.fence_pending.get(e, {}))
            d.update(f)
            self.fence_pending[e] = d

    def _src(self, idx):
        op = self.ops[idx]
        return ("dma", op[3]) if op[3] is not None else ("eng", op[0])

    def add(self, eng, fn, r=(), w=(), dma_grp=None, pos=None):
        idx = len(self.ops)
        self.pos.append(idx if pos is None else pos)
        deps = self.fence_pending.pop(eng, None) or {}

        def need(d):
            s = self._src(d)
            if deps.get(s, -1) < d:
                deps[s] = d

        for k in r:
            lw = self.lastw.get(k)
            if lw is not None:
                need(lw)
        for k in w:
            lw = self.lastw.get(k)
            if lw is not None:
                need(lw)
            rd = self.readers.get(k)
            if rd:
                for d in rd.values():
                    need(d)
        self.ops.append([eng, fn, deps, dma_grp])
        me = ("dma", dma_grp) if dma_grp is not None else ("eng", eng)
        self.last_by_src[me] = idx
        for k in w:
            self.lastw[k] = idx
            self.readers[k] = {}
        for k in r:
            self.readers.setdefault(k, {})[me] = idx
        return idx

    def mm(self, out, lhsT, rhs, start, stop, r, w):
        return self.add("pe", lambda e: e.matmul(out, lhsT, rhs, start=start, stop=stop), r, w)

    def tr(self, out, in_, ident, r, w):
        return self.add("pe", lambda e: e.transpose(out, in_, ident), r, w)

    def act(self, out, in_, func, r, w, bias=None, scale=None):
        kw = {}
        if bias is not None:
            kw["bias"] = bias
        if scale is not None:
            kw["scale"] = scale
        return self.add("act", lambda e: e.activation(out, in_, func, **kw), r, w)

    def v(self, eng, name, args, r, w, **kw):
        return self.add(eng, lambda e: getattr(e, name)(*args, **kw), r, w)

    def dma(self, q, out, in_, r, w, grp, pos=None, **kw):
        return self.add(q, lambda e: e.dma_start(out=out, in_=in_, **kw), r, w, dma_grp=grp, pos=pos)

    def emit(self, stack, same_engine_sync=True):
        nc = self.nc
        ops = self.ops
        signal = set()
        for idx, (eng, fn, deps, grp) in enumerate(ops):
            for s, d in deps.items():
                if s[0] == "eng":
                    if s[1] == eng and grp is None and (eng == "pe" or not same_engine_sync):
                        continue
                    signal.add(d)
        sem_eng = {e: stack.enter_context(nc.semaphore("sem_" + e)) for e in self.COMPUTE}
        grp_names = []
        for op in ops:
            if op[3] is not None and op[3] not in grp_names:
                grp_names.append(op[3])
        sem_grp = {g: stack.enter_context(nc.semaphore("dg_%d" % i)) for i, g in enumerate(grp_names)}
        cnt = {}
        run = {e: 0 for e in self.COMPUTE}
        rung = {g: 0 for g in grp_names}
        prevg = {}
        for idx, (eng, fn, deps, grp) in enumerate(ops):
            if grp is not None:
                prevg[idx] = rung[grp]
                rung[grp] += 16
                cnt[idx] = rung[grp]
            elif idx in signal:
                run[eng] += 1
                cnt[idx] = run[eng]
        per_eng = {e: [] for e in ("pe", "act", "dve", "pool", "sp")}
        for idx, op in enumerate(ops):
            per_eng[op[0]].append(idx)
        for e_ in per_eng:
            per_eng[e_].sort(key=lambda i: (self.pos[i], i))
        final_g = dict(rung)

        def emit_engine(ename, e):
            waited = {}
            for idx in per_eng[ename]:
                eng, fn, deps, grp = ops[idx]
                for s, d in deps.items():
                    if s[0] == "eng":
                        if s[1] == eng and grp is None and (eng == "pe" or not same_engine_sync):
                            continue
                        sem = sem_eng[s[1]]
                    else:
                        sem = sem_grp[s[1]]
                    c = cnt[d]
                    if waited.get(s, 0) >= c:
                        continue
                    e.wait_ge(sem, c)
                    waited[s] = c
                if grp is not None:
                    s = ("dma", grp)
                    if prevg[idx] > 0 and waited.get(s, 0) < prevg[idx]:
                        e.wait_ge(sem_grp[grp], prevg[idx])
                        waited[s] = prevg[idx]
                    fn(e).then_inc(sem_grp[grp], 16)
                else:
                    ins = fn(e)
                    if idx in signal:
                        ins.then_inc(sem_eng[eng], 1)
            if ename == "sp":
                for g, c in final_g.items():
                    if c > 0:
                        e.wait_ge(sem_grp[g], c)

        with nc.Block() as block:
            @block.tensor
            def _(e):
                emit_engine("pe", e)

            @block.scalar
            def _(e):
                emit_engine("act", e)

            @block.vector
            def _(e):
                emit_engine("dve", e)

            @block.gpsimd
            def _(e):
                emit_engine("pool", e)

            @block.sync
            def _(e):
                emit_engine("sp", e)


NSLOT = 21
NEG = -30000.0


class Builder:
    def __init__(self, cfg):
        self.cfg = cfg
        self.nseq = cfg.get("nseq", 2)
        self.tiles = cfg.get("tiles", list(range(NT)))
        self.layers = cfg.get("layers", list(range(DEPTH)))
        nc = bass.Bass("TRN2", target_bir_lowering=False)
        self.nc = nc
        self.P = Prog(nc)
        self.stack = ExitStack()
        self.pstack = None
        self.nm = 0
        self._decl_dram()

    def din(self, name, shape, dt=F32):
        return self.nc.dram_tensor(name, list(shape), dt, kind="ExternalInput").ap()

    def dscr(self, name, shape, dt=F32):
        if self.cfg.get("debug", False):
            return self.nc.dram_tensor(name, list(shape), dt, kind="ExternalOutput").ap()
        return self.nc.dram_tensor(name, list(shape), dt).ap()

    def _decl_dram(self):
        nc = self.nc
        ns = self.nseq
        self.x_in = self.din("x_in", [ns, L, D])
        self.y_out = nc.dram_tensor("y_out", [ns, L, D], F32, kind="ExternalOutput").ap()
        self.norm_mix = self.din("norm_mix", [DEPTH, 128, KC])
        self.norm_mlp = self.din("norm_mlp", [DEPTH, 128, KC])
        self.w_mlp1 = self.din("w_mlp1", [DEPTH, D, DFF])
        self.w_mlp2 = self.din("w_mlp2", [DEPTH, DFF, D])
        self.w_in_c = self.din("w_in_c", [2, D, 2 * D])
        self.v_gain = self.din("v_gain", [2, D])
        self.w_sT = self.din("w_sT", [2, 8, 128, 128])
        self.b_s = self.din("b_s", [2, 8 * 128])
        self.w_out_c = self.din("w_out_c", [2, D, D])
        self.ident = self.din("ident", [128, 128])
        self.w_in_ab = self.din("w_in_ab", [2, D, 6144])
        self.w_out_ab = self.din("w_out_ab", [2, D, D])
        self.qk_gain = self.din("qk_gain", [2, 128, 2])
        self.biasT = self.din("biasT", [2, NH, 128, NSLOT, 128], BF16)
        self.scw = self.din("scw", [2, 128, 4, 24])
        self.hbias = self.din("hbias", [2, 128, 8])
        self.f_w1 = self.din("f_w1", [2, 33, 64])
        self.f_w2 = self.din("f_w2", [2, 64, 64])
        self.f_w3 = self.din("f_w3", [2, 64, 64])
        self.f_wout = self.din("f_wout", [2, 64, 2048])
        self.f_vec = self.din("f_vec", [2, 64, 4])
        self.zfeat = self.din("zfeat", [33, L])
        self.deltas = self.din("deltas", [D_B])
        self.tcol = self.din("tcol", [128, 32])
        self.TF = self.din("TF", [2, 32, 128, 32, 128], BF16)
        self.TI = self.din("TI", [2, L, L], BF16)
        self.wb = {}
        for nm_, ap_ in (("w_mlp1", self.w_mlp1), ("w_mlp2", self.w_mlp2), ("w_in_c", self.w_in_c),
                         ("w_out_c", self.w_out_c), ("w_in_ab", self.w_in_ab), ("w_out_ab", self.w_out_ab)):
            self.wb[nm_] = self.nc.dram_tensor(nm_ + "_bf", list(ap_.shape), BF16).ap()
        self.xs = self.dscr("xs_scratch", [D, L])
        self.qT_s = self.dscr("qT_s", [D_A, L], BF16)
        self.kT_s = self.dscr("kT_s", [D_A, L], BF16)
        self.v_s = self.dscr("v_s", [L, D_A], BF16)
        self.zT_s = self.dscr("zT_s", [3 * D_B, L])
        self.uT_s = self.dscr("uT_s", [D_B, L])
        self.x0_s = self.dscr("x0_s", [D_B, L])
        self.u_s = self.dscr("u_s", [L, D_B], BF16)
        self.abT_s = self.dscr("abT_s", [D, L], BF16)
        self.Ksp = self.dscr("Ksp", [2, 2, L, D_B])

    def sb(self, name, shape, dt):
        return self.stack.enter_context(self.nc.sbuf_tensor(name, list(shape), dt))

    def sbp(self, name, shape, dt):
        self.nm += 1
        return self.pstack.enter_context(self.nc.sbuf_tensor("%s_%d" % (name, self.nm), list(shape), dt))

    def phase_begin(self):
        self.w_hist = []
        self.pstack = ExitStack()
        self.ph = self.nm

    def phase_end(self):
        self.P.barrier()
        self.pstack.close()
        self.pstack = None

    def bank(self, avoid=None):
        b = self.P.ps_next
        if avoid is not None and b == avoid:
            b = (b + 1) % 8
        self.P.ps_next = (b + 1) % 8
        return b

    def build(self):
        P = self.P
        nc = self.nc
        self.ps = [self.stack.enter_context(nc.psum_tensor("ps%d" % i, [128, 512], F32)) for i in range(8)]
        self.ones = self.sb("ones", [128, 128], F32)
        self.identf = self.sb("identf", [128, 128], F32)
        self.onesb = self.sb("onesb", [128, 128], BF16)
        self.identb = self.sb("identb", [128, 128], BF16)
        self.epsc = self.sb("epsc", [128, 4], F32)
        P.v("pool", "memset", (self.ones[:], 1.0), [], ["ones"])
        P.v("pool", "memset", (self.onesb[:], 1.0), [], ["onesb"])
        P.v("pool", "memset", (self.epsc[:, 0:1], D * EPS), [], ["epsc"])
        P.v("pool", "memset", (self.epsc[:, 1:2], EPS), [], ["epsc"])
        P.v("pool", "memset", (self.epsc[:, 2:3], 128 * EPS), [], ["epsc"])
        P.v("pool", "memset", (self.epsc[:, 3:4], 0.0), [], ["epsc"])
        self.epsD = self.epsc[:, 0:1]
        self.eps1 = self.epsc[:, 1:2]
        self.eps128 = self.epsc[:, 2:3]
        P.dma("sp", self.identf[:], self.ident, [], ["identf"], "c_ident")
        P.dma("pool", self.identb[:], self.ident, [], ["identb"], "c_identb")
        self.gvec = self.sb("gvec", [128, 2 * DEPTH, KC], F32)
        self.gsc = self.sb("gsc", [128, 2 * DEPTH, KC], F32)
        P.dma("sp", self.gvec[:, 0:DEPTH, :], self.norm_mix.rearrange("l p k -> p l k"), [], ["gvec"], "c_g1")
        P.dma("sp", self.gvec[:, DEPTH:2 * DEPTH, :], self.norm_mlp.rearrange("l p k -> p l k"), [], ["gvec"], "c_g2")
        P.v("dve", "tensor_scalar", (self.gsc[:], self.gvec[:], math.sqrt(D), None, ALU.mult), ["gvec"], ["gsc"])
        self.qkg = self.sb("qkg", [128, 2, 2], F32)
        P.dma("sp", self.qkg[:], self.qk_gain.rearrange("j p c -> p j c"), [], ["qkg"], "c_qk")
        P.v("dve", "tensor_scalar", (self.qkg[:, :, 1:2], self.qkg[:, :, 1:2], math.sqrt(128.0), None, ALU.mult),
            ["qkg"], ["qkg"])
        ci = 0
        need = {"w_mlp1": self.layers, "w_mlp2": self.layers,
                "w_in_c": sorted({l // 2 for l in self.layers if l % 2 == 1}),
                "w_out_c": sorted({l // 2 for l in self.layers if l % 2 == 1}),
                "w_in_ab": sorted({l // 2 for l in self.layers if l % 2 == 0}),
                "w_out_ab": sorted({l // 2 for l in self.layers if l % 2 == 0})}
        srcs = {"w_mlp1": self.w_mlp1, "w_mlp2": self.w_mlp2, "w_in_c": self.w_in_c, "w_out_c": self.w_out_c,
                "w_in_ab": self.w_in_ab, "w_out_ab": self.w_out_ab}
        for nm_, idxs in need.items():
            src = srcs[nm_]
            rows = src.shape[1]
            for li in idxs:
                for r0 in range(0, rows, 512):
                    P.dma("pool", self.wb[nm_][li, r0:r0 + 512, :].rearrange("(p a) c -> p a c", p=128),
                          src[li, r0:r0 + 512, :].rearrange("(p a) c -> p a c", p=128), [], [("wb", nm_, li, r0)],
                          "cv%d" % (ci % 4))
                    ci += 1
        runs_filter = any(l % 2 == 0 for l in self.layers) and "F" in self.cfg.get("even_phases", ("F",))
        if not runs_filter:
            P.barrier()

        even_layers = [l for l in self.layers if l % 2 == 0]
        for l in even_layers:
            if "F" in self.cfg.get("even_phases", ("F",)):
                self.filter_phase(l // 2)
        for s in range(self.nseq):
            i = 0
            first = True
            while i < len(self.layers):
                l = self.layers[i]
                if l % 2 == 0:
                    eph = self.cfg.get("even_phases", ("E1", "ATT", "HA", "HBC"))
                    if "E1" in eph:
                        self.e1_phase(s, l, from_input=first)
                    if "ATT" in eph:
                        self.attn_phase(l // 2)
                    if "HA" in eph:
                        self.hyena_a_phase(l // 2)
                    if "HBC" in eph:
                        self.hyena_bc_phase(l // 2)
                    if "TILE" not in self.cfg.get("even_phases", ("TILE",)):
                        i += 1
                        continue
                    nxt = self.layers[i + 1] if i + 1 < len(self.layers) and self.layers[i + 1] == l + 1 else None
                    last = (i + (2 if nxt is not None else 1)) >= len(self.layers)
                    self.tile_phase(s, even=l, odd=nxt, from_input=False, to_output=last)
                    i += 2 if nxt is not None else 1
                else:
                    last = (i + 1) >= len(self.layers)
                    self.tile_phase(s, even=None, odd=l, from_input=first, to_output=last)
                    i += 1
                first = False
        P.emit(self.stack, same_engine_sync=self.cfg.get("same_sync", True))
        self.stack.close()
        return nc

    def alloc_tile_bufs(self, odd_j=None, double_x=False):
        P = self.P
        self.xT = self.sbp("xT", [128, KC, T], F32)
        self.xk = "xT"
        self.xTbufs = [(self.xT, "xT")]
        if double_x:
            self.xTbufs.append((self.sbp("xTb", [128, KC, T], F32), "xTb"))
        self.xn = self.sbp("xn", [128, KC, T], BF16)
        self.big = self.sbp("big", [128, 32, T], BF16)
        self.wsl = [self.sbp("wsl%d" % i, [128, 16 * 512], BF16) for i in range(3)]
        self.wsl_next = 0
        self.sqb = [self.sbp("sqb%d" % i, [128, T], BF16) for i in range(4)]
        self.rstd = self.sbp("rstd", [128, T], F32)
        self.tmp = [self.sbp("tmp%d" % i, [128, T], F32) for i in range(6)]
        self.tmp_next = 0
        self.tb = [self.sbp("tb%d" % i, [128, T], BF16) for i in range(4)]
        self.tb_next = 0
        self.iot = self.sbp("iot", [128, D], F32)
        if odd_j is not None:
            j = odd_j
            self.vg = self.sbp("vg", [128, D], F32)
            self.bsb = self.sbp("bsb", [128, 8 * 128], F32)
            self.wst = self.sbp("wst", [128, 8, 128], BF16)
            P.dma("sp", self.vg[:], self.v_gain[j].partition_broadcast(128), [], ["vg"], "c_vg")
            P.dma("sp", self.bsb[:], self.b_s[j].partition_broadcast(128), [], ["bsb"], "c_bs")
            P.dma("pool", self.wst[:], self.w_sT[j].rearrange("g q p -> q g p"), [], ["wst"], "c_ws")

    def wslot(self):
        i = self.wsl_next
        self.wsl_next = (i + 1) % 3
        return i

    def tmpslot(self):
        i = self.tmp_next
        self.tmp_next = (i + 1) % len(self.tmp)
        return i

    def tbslot(self):
        i = self.tb_next
        self.tb_next = (i + 1) % len(self.tb)
        return i

    def load_tile(self, s, t, from_input):
        P = self.P
        if from_input:
            for sub in range(4):
                r0 = t * T + sub * 128
                P.dma("sp", self.iot[:], self.x_in[s, r0:r0 + 128, :], [], ["iot"], "g_iot")
                for kq in range(4):
                    b = self.bank()
                    for kk in range(4):
                        k = kq * 4 + kk
                        P.tr(self.ps[b][:, kk * 128:(kk + 1) * 128], self.iot[:, k * 128:(k + 1) * 128],
                             self.identf[:], ["iot", "identf"], [("ps", b)])
                    for kk in range(4):
                        k = kq * 4 + kk
                        P.v("dve", "tensor_copy", (self.xT[:, k, sub * 128:(sub + 1) * 128],
                                                   self.ps[b][:, kk * 128:(kk + 1) * 128]),
                            [("ps", b)], [(self.xk, k)])
        else:
            P.dma("sp", self.xT[:], self.xs.rearrange("(k p) n -> p k n", p=128)[:, :, t * T:(t + 1) * T],
                  [("xs", t)], [(self.xk, k) for k in range(KC)], "g_xT")

    def store_tile(self, s, t, to_output):
        P = self.P
        if to_output:
            for sub in range(4):
                r0 = t * T + sub * 128
                for kq in range(4):
                    b = self.bank()
                    for kk in range(4):
                        k = kq * 4 + kk
                        P.tr(self.ps[b][:, kk * 128:(kk + 1) * 128], self.xT[:, k, sub * 128:(sub + 1) * 128],
                             self.identf[:], [(self.xk, k), "identf"], [("ps", b)])
                    P.act(self.iot[:, kq * 512:(kq + 1) * 512], self.ps[b][:], AF.Copy, [("ps", b)], ["iot"])
                P.dma("sp", self.y_out[s, r0:r0 + 128, :], self.iot[:], ["iot"], [("yout", s, t, sub)], "g_iot")
        else:
            P.dma("sp", self.xs.rearrange("(k p) n -> p k n", p=128)[:, :, t * T:(t + 1) * T], self.xT[:],
                  [(self.xk, k) for k in range(KC)], [("xs", t)], "g_xT")

    def rmsnorm_tile(self, gi):
        P = self.P
        b = self.bank()
        for k in range(KC):
            sl = k % 4
            if k % 2 == 0:
                P.act(self.sqb[sl][:], self.xT[:, k, :], AF.Square, [(self.xk, k)], [("sqb", sl)])
            else:
                P.v("pool", "tensor_tensor", (self.sqb[sl][:], self.xT[:, k, :], self.xT[:, k, :], ALU.mult),
                    [(self.xk, k)], [("sqb", sl)])
            P.mm(self.ps[b][:], self.onesb[:], self.sqb[sl][:], k == 0, k == KC - 1,
                 ["onesb", ("sqb", sl)], [("ps", b)])
        P.act(self.rstd[:], self.ps[b][:], AF.Sqrt, [("ps", b), "epsc"], ["rstd"], bias=self.epsD)
        P.v("dve", "reciprocal", (self.rstd[:], self.rstd[:]), ["rstd"], ["rstd"])
        for k in range(KC):
            P.v("dve", "scalar_tensor_tensor",
                (self.xn[:, k, :], self.xT[:, k, :], self.gsc[:, gi, k:k + 1], self.rstd[:], ALU.mult, ALU.mult),
                [(self.xk, k), "gsc", "rstd"], [("xn", k)])

    def load_w(self, wap, nk, c0, ncols, r0=0):
        P = self.P
        i = self.wslot()
        dst = self.wsl[i][:, 0:nk * ncols].rearrange("p (k n) -> p k n", n=ncols)
        src = wap[r0:r0 + nk * 128, :].rearrange("(k p) n -> p k n", p=128)[:, :, c0:c0 + ncols]
        hist = self.w_hist
        pos = None
        if len(hist) >= 2:
            pos = hist[-2] + 0.5
        idx = P.dma("sp", dst, src, [], [("wsl", i)], "g_w%d" % i, pos=pos)
        hist.append(idx)
        return i, dst

    def mlp(self, l):
        P = self.P
        self.rmsnorm_tile(DEPTH + l)
        w1 = self.wb["w_mlp1"][l]
        w2 = self.wb["w_mlp2"][l]
        for hh in range(2):
            for jb in range(8):
                wi, wv = self.load_w(w1, KC, (hh * 8 + jb) * 512, 512)
                for jj in range(4):
                    j = jb * 4 + jj
                    b = self.bank()
                    for k in range(KC):
                        P.mm(self.ps[b][:], wv[:, k, jj * 128:(jj + 1) * 128], self.xn[:, k, :], k == 0, k == KC - 1,
                             [("wsl", wi), ("xn", k)], [("ps", b)])
                    ts = self.tmpslot()
                    P.act(self.tmp[ts][:], self.ps[b][:], AF.Relu, [("ps", b)], [("tmp", ts)])
                    P.v("pool", "tensor_tensor", (self.big[:, j, :], self.tmp[ts][:], self.tmp[ts][:], ALU.mult),
                        [("tmp", ts)], [("big", j)])
            for mp in range(8):
                i, dst = self.load_w(w2, 32, mp * 256, 256, r0=hh * 4096)
                for m2 in range(2):
                    m = mp * 2 + m2
                    b = self.bank()
                    for j in range(32):
                        P.mm(self.ps[b][:], dst[:, j, m2 * 128:(m2 + 1) * 128], self.big[:, j, :], j == 0, j == 31,
                             [("wsl", i), ("big", j)], [("ps", b)])
                    P.v("dve", "tensor_tensor", (self.xT[:, m, :], self.xT[:, m, :], self.ps[b][:], ALU.add),
                        [(self.xk, m), ("ps", b)], [(self.xk, m)])

    def gelu(self, out, b, wkeys):
        P = self.P
        a = self.tmpslot()
        c = self.tmpslot()
        xs_, t_ = self.tmp[a], self.tmp[c]
        P.act(xs_[:], self.ps[b][:], AF.Copy, [("ps", b)], [("tmp", a)])
        P.v("pool", "tensor_tensor", (t_[:], xs_[:], xs_[:], ALU.mult), [("tmp", a)], [("tmp", c)])
        P.v("pool", "tensor_scalar", (t_[:], t_[:], 0.044715, 1.0, ALU.mult, ALU.add), [("tmp", c)], [("tmp", c)])
        P.v("dve", "tensor_tensor", (t_[:], t_[:], xs_[:], ALU.mult), [("tmp", a), ("tmp", c)], [("tmp", c)])
        P.act(t_[:], t_[:], AF.Sigmoid, [("tmp", c)], [("tmp", c)], scale=2.0 * GELU_C)
        P.v("dve", "tensor_tensor", (out, t_[:], xs_[:], ALU.mult), [("tmp", a), ("tmp", c)], wkeys)

    def mixer_c(self, l):
        P = self.P
        j = l // 2
        self.rmsnorm_tile(l)
        w_in = self.wb["w_in_c"][j]
        for cb in range(4):
            wi, wv = self.load_w(w_in, KC, cb * 512, 512)
            for cc in range(4):
                c = cb * 4 + cc
                b = self.bank()
                for k in range(KC):
                    P.mm(self.ps[b][:], wv[:, k, cc * 128:(cc + 1) * 128], self.xn[:, k, :], k == 0, k == KC - 1,
                         [("wsl", wi), ("xn", k)], [("ps", b)])
                self.gelu(self.big[:, c, :], b, [("big", c)])
        vt = [self.big[:, 16 + 4 * sub:16 + 4 * sub + 4, :].rearrange("p a b -> p (a b)") for sub in range(4)]
        vkeys = [[("big", 16 + 4 * sub + i) for i in range(4)] for sub in range(4)]
        for cb in range(4):
            wi, wv = self.load_w(w_in, KC, D + cb * 512, 512)
            for sub in range(4):
                b = self.bank()
                for k in range(KC):
                    P.mm(self.ps[b][:], self.xn[:, k, sub * 128:(sub + 1) * 128], wv[:, k, :], k == 0, k == KC - 1,
                         [("wsl", wi), ("xn", k)], [("ps", b)])
                self.gelu(vt[sub][:, cb * 512:(cb + 1) * 512], b, [vkeys[sub][cb]])
        for sub in range(4):
            a = self.tmpslot()
            c = self.tmpslot()
            ssq = self.tmp[a]
            junk = self.tmp[c]
            for cb in range(4):
                P.act(junk[:], vt[sub][:, cb * 512:(cb + 1) * 512], AF.Square, [vkeys[sub][cb]], [("tmp", c)])
                P.v("dve", "reduce_sum", (ssq[:, cb:cb + 1], junk[:], mybir.AxisListType.X),
                    [("tmp", c)], [("tmp", a)])
            P.v("dve", "reduce_sum", (ssq[:, 4:5], ssq[:, 0:4], mybir.AxisListType.X), [("tmp", a)], [("tmp", a)])
            P.act(ssq[:, 5:6], ssq[:, 4:5], AF.Sqrt, [("tmp", a), "epsc"], [("tmp", a)], bias=self.eps1, scale=1.0 / D)
            P.v("dve", "reciprocal", (ssq[:, 6:7], ssq[:, 5:6]), [("tmp", a)], [("tmp", a)])
            for cb in range(4):
                P.v("dve", "scalar_tensor_tensor",
                    (vt[sub][:, cb * 512:(cb + 1) * 512], vt[sub][:, cb * 512:(cb + 1) * 512], ssq[:, 6:7],
                     self.vg[:, cb * 512:(cb + 1) * 512], ALU.mult, ALU.mult),
                    [vkeys[sub][cb], ("tmp", a), "vg"], [vkeys[sub][cb]])
        for sub in range(4):
            for cq in range(4):
                b = self.bank()
                for cc in range(4):
                    c = cq * 4 + cc
                    g = c // 2
                    P.mm(self.ps[b][:, cc * 128:(cc + 1) * 128], vt[sub][:, c * 128:(c + 1) * 128],
                         self.wst[:, g, :], True, True, [vkeys[sub][cq], "wst"], [("ps", b)])
                for cc in range(4):
                    c = cq * 4 + cc
                    g = c // 2
                    ts = self.tmpslot()
                    P.v("dve", "tensor_tensor", (self.tmp[ts][:, 0:128], self.ps[b][:, cc * 128:(cc + 1) * 128],
                                                 self.bsb[:, g * 128:(g + 1) * 128], ALU.add),
                        [("ps", b), "bsb"], [("tmp", ts)])
                    P.v("pool", "tensor_tensor", (self.big[:, c, sub * 128:(sub + 1) * 128], self.tmp[ts][:, 0:128],
                                                  self.big[:, c, sub * 128:(sub + 1) * 128], ALU.mult),
                        [("tmp", ts), ("big", c)], [("big", c)])
        self.proj_residual(self.wb["w_out_c"][j])

    def proj_residual(self, w_o):
        P = self.P
        for mb in range(4):
            wi, wv = self.load_w(w_o, KC, mb * 512, 512)
            for m4 in range(4):
                m = mb * 4 + m4
                b = self.bank()
                for k in range(KC):
                    P.mm(self.ps[b][:], wv[:, k, m4 * 128:(m4 + 1) * 128], self.big[:, k, :], k == 0, k == KC - 1,
                         [("wsl", wi), ("big", k)], [("ps", b)])
                P.v("dve", "tensor_tensor", (self.xT[:, m, :], self.xT[:, m, :], self.ps[b][:], ALU.add),
                    [(self.xk, m), ("ps", b)], [(self.xk, m)])

    def tile_phase(self, s, even, odd, from_input, to_output):
        P = self.P
        self.phase_begin()
        self.alloc_tile_bufs(odd_j=(odd // 2 if odd is not None else None))
        for t in self.tiles:
            self.load_tile(s, t, from_input=from_input)
            if even is not None:
                P.dma("sp", self.big[:, 0:KC, :],
                      self.abT_s.rearrange("(k p) n -> p k n", p=128)[:, :, t * T:(t + 1) * T],
                      [("abT", t)], [("big", k) for k in range(KC)], "g_ab")
                self.proj_residual(self.wb["w_out_ab"][even // 2])
                self.mlp(even)
            if odd is not None:
                self.mixer_c(odd)
                self.mlp(odd)
            self.store_tile(s, t, to_output=to_output)
        self.phase_end()

    def e1_phase(self, s, l, from_input):
        P = self.P
        j = l // 2
        self.phase_begin()
        self.alloc_tile_bufs(double_x=True)
        w_in = self.wb["w_in_ab"][j]

        def fetch(ti):
            self.xT, self.xk = self.xTbufs[ti % 2]
            self.load_tile(s, self.tiles[ti], from_input=from_input)
            if from_input:
                self.store_tile(s, self.tiles[ti], to_output=False)

        fetch(0)
        for ti, t in enumerate(self.tiles):
            self.xT, self.xk = self.xTbufs[ti % 2]
            self.rmsnorm_tile(l)
            if ti + 1 < len(self.tiles):
                fetch(ti + 1)
                self.xT, self.xk = self.xTbufs[ti % 2]
            tsl = slice(t * T, (t + 1) * T)
            pending = []

            def tail(b, a, isq, h):
                b2 = self.bank()
                P.mm(self.ps[b2][:], self.onesb[:], self.sqb[a][:], True, True, ["onesb", ("sqb", a)], [("ps", b2)])
                c = self.tmpslot()
                P.act(self.tmp[c][:], self.ps[b2][:], AF.Sqrt, [("ps", b2), "epsc"], [("tmp", c)], bias=self.eps128)
                P.v("dve", "reciprocal", (self.tmp[c][:], self.tmp[c][:]), [("tmp", c)], [("tmp", c)])
                o = self.tbslot()
                P.v("dve", "scalar_tensor_tensor",
                    (self.tb[o][:], self.ps[b][:], self.qkg[:, j, (0 if isq else 1):(1 if isq else 2)],
                     self.tmp[c][:], ALU.mult, ALU.mult), [("ps", b), "qkg", ("tmp", c)], [("tb", o)])
                dst = (self.qT_s if isq else self.kT_s)[h * 128:(h + 1) * 128, tsl]
                P.dma("sp", dst, self.tb[o][:], [("tb", o)], [("qk", isq, h, t)], "g_tb%d" % o)

            nq = 0
            for blk in range(4):
                wi, wv = self.load_w(w_in, KC, blk * 512, 512)
                isq = blk < 2
                for cc in range(4):
                    h = (blk % 2) * 4 + cc
                    b = self.bank()
                    for k in range(KC):
                        P.mm(self.ps[b][:], wv[:, k, cc * 128:(cc + 1) * 128], self.xn[:, k, :], k == 0, k == KC - 1,
                             [("wsl", wi), ("xn", k)], [("ps", b)])
                    a = nq % 4
                    nq += 1
                    P.act(self.sqb[a][:], self.ps[b][:], AF.Square, [("ps", b)], [("sqb", a)])
                    if pending:
                        tail(*pending.pop())
                    pending.append((b, a, isq, h))
            tail(*pending.pop())
            for blk in range(2):
                wi, wv = self.load_w(w_in, KC, 2048 + blk * 512, 512)
                for sub in range(4):
                    b = self.bank()
                    for k in range(KC):
                        P.mm(self.ps[b][:], self.xn[:, k, sub * 128:(sub + 1) * 128], wv[:, k, :], k == 0, k == KC - 1,
                             [("wsl", wi), ("xn", k)], [("ps", b)])
                    o = self.tbslot()
                    P.act(self.tb[o][:], self.ps[b][:], AF.Copy, [("ps", b)], [("tb", o)])
                    r0 = t * T + sub * 128
                    P.dma("sp", self.v_s[r0:r0 + 128, blk * 512:(blk + 1) * 512], self.tb[o][:],
                          [("tb", o)], [("vs", t, sub, blk)], "g_tb%d" % o)
            for blk in range(6):
                wi, wv = self.load_w(w_in, KC, 3072 + blk * 512, 512)
                for cc in range(4):
                    c = blk * 4 + cc
                    b = self.bank()
                    for k in range(KC):
                        P.mm(self.ps[b][:], wv[:, k, cc * 128:(cc + 1) * 128], self.xn[:, k, :], k == 0, k == KC - 1,
                             [("wsl", wi), ("xn", k)], [("ps", b)])
                    a = self.tmpslot()
                    P.act(self.tmp[a][:], self.ps[b][:], AF.Copy, [("ps", b)], [("tmp", a)])
                    P.dma("sp", self.zT_s[c * 128:(c + 1) * 128, tsl], self.tmp[a][:],
                          [("tmp", a)], [("zs", c, t)], "g_tmp%d" % a)
        self.phase_end()

    def attn_phase(self, j):
        P = self.P
        self.phase_begin()
        qT = [self.sbp("qT", [128, L], BF16) for _ in range(2)]
        kT = [self.sbp("kT", [128, L], BF16) for _ in range(2)]
        vh = [self.sbp("vh", [128, 32, 128], BF16) for _ in range(2)]
        bh = [self.sbp("bh", [128, NSLOT, 128], BF16) for _ in range(2)]
        aT = [self.sbp("aT", [128, L], BF16) for _ in range(2)]
        eT = [self.sbp("eT", [128, 5, 128], BF16) for _ in range(3)]
        rd = [self.sbp("rd", [128, 128], F32) for _ in range(3)]
        for h in range(NH):
            sl = h % 2
            P.dma("sp", qT[sl][:], self.qT_s[h * 128:(h + 1) * 128, :], [("qk", True, h)], [("qT", sl)], "a_q%d" % sl)
            P.dma("sp", kT[sl][:], self.kT_s[h * 128:(h + 1) * 128, :], [("qk", False, h)], [("kT", sl)], "a_k%d" % sl)
            P.dma("sp", vh[sl][:], self.v_s.rearrange("(n p) c -> p n c", p=128)[:, :, h * 128:(h + 1) * 128],
                  [], [("vh", sl)], "a_v%d" % sl)
            P.dma("sp", bh[sl][:], self.biasT[j, h], [], [("bh", sl)], "a_b%d" % sl)
            def pv(nch, krow0, ei, qs, sl=sl):
                b3 = self.bank()
                for c in range(nch):
                    vi = krow0 // 2 + c
                    P.mm(self.ps[b3][:, 0:128], vh[sl][:, vi, :], eT[ei][:, c, :], c == 0, c == nch - 1,
                         [("vh", sl), ("eT", ei)], [("ps", b3)])
                for c in range(nch):
                    P.mm(self.ps[b3][:, 128:256], self.onesb[:], eT[ei][:, c, :], c == 0, c == nch - 1,
                         ["onesb", ("eT", ei)], [("ps", b3)])
                P.v("dve", "reciprocal", (rd[ei][:], self.ps[b3][:, 128:256]), [("ps", b3)], [("rd", ei)])
                P.v("dve", "tensor_tensor", (aT[sl][:, qs], self.ps[b3][:, 0:128], rd[ei][:], ALU.mult),
                    [("ps", b3), ("rd", ei)], [("aT", sl)])

            pend = None
            for m in range(32):
                if 2 <= m <= 29:
                    nch, krow0, slot0 = 5, 2 * m - 4, 0
                elif m < 2:
                    nch, krow0, slot0 = 4, 0, 5 + 4 * m
                else:
                    nch, krow0, slot0 = 4, 56, 13 + 4 * (m - 30)
                qs = slice(m * 128, (m + 1) * 128)
                b1 = self.bank()
                b2 = self.bank() if nch == 5 else None
                ei = (h * 32 + m) % 3
                for c in range(nch):
                    bb, co = (b1, c) if c < 4 else (b2, 0)
                    k0 = (krow0 + 2 * c) * 64
                    outp = self.ps[bb][:, co * 128:(co + 1) * 128]
                    P.mm(outp, kT[sl][:, k0:k0 + 128], qT[sl][:, qs], True, False,
                         [("kT", sl), ("qT", sl)], [("ps", bb)])
                    P.mm(outp, self.identb[:], bh[sl][:, slot0 + c, :], False, True,
                         ["identb", ("bh", sl)], [("ps", bb)])
                P.act(eT[ei][:, 0:4, :].rearrange("p a b -> p (a b)"), self.ps[b1][:], AF.Exp,
                      [("ps", b1)], [("eT", ei)])
                if nch == 5:
                    P.act(eT[ei][:, 4, :], self.ps[b2][:, 0:128], AF.Exp, [("ps", b2)], [("eT", ei)])
                if pend is not None:
                    pv(*pend)
                pend = (nch, krow0, ei, qs)
            pv(*pend)
            pend = None
            P.dma("sp", self.abT_s[h * 128:(h + 1) * 128, :], aT[sl][:], [("aT", sl)], [("abT", "a", h)], "a_o%d" % sl)
        self.phase_end()

    def hyena_a_phase(self, j):
        P = self.P
        self.phase_begin()
        zb = [self.sbp("zb", [128, L + 2], F32) for _ in range(3)]
        cv = [self.sbp("cv", [128, L], F32) for _ in range(3)]
        uT = self.sbp("uT", [128, L], F32)
        ust = self.sbp("ust", [128, 32, 128], BF16)
        scw = self.sbp("scw", [128, 4, 24], F32)
        P.dma("sp", scw[:], self.scw[j], [], ["scw"], "h_scw")
        for i in range(3):
            P.v("pool", "memset", (zb[i][:, 0:1], 0.0), [], [("zb", i)])
            P.v("pool", "memset", (zb[i][:, L + 1:L + 2], 0.0), [], [("zb", i)])
        for cc in range(8):
            for i in range(3):
                ci = i * 8 + cc
                eng = "dve"
                P.dma("sp", zb[i][:, 1:L + 1], self.zT_s[ci * 128:(ci + 1) * 128, :], [("zs", ci)], [("zb", i)], "h_z%d" % i)
                P.v(eng, "tensor_scalar", (cv[i][:], zb[i][:, 1:L + 1], scw[:, 1, ci:ci + 1], scw[:, 3, ci:ci + 1],
                                           ALU.mult, ALU.add), [("zb", i), "scw"], [("cv", i)])
                P.v(eng, "scalar_tensor_tensor", (cv[i][:], zb[i][:, 0:L], scw[:, 0, ci:ci + 1], cv[i][:],
                                                  ALU.mult, ALU.add), [("zb", i), "scw", ("cv", i)], [("cv", i)])
                P.v(eng, "scalar_tensor_tensor", (cv[i][:], zb[i][:, 2:L + 2], scw[:, 2, ci:ci + 1], cv[i][:],
                                                  ALU.mult, ALU.add), [("zb", i), "scw", ("cv", i)], [("cv", i)])
            P.v("pool", "tensor_tensor", (uT[:], cv[2][:], cv[1][:], ALU.mult), [("cv", 1), ("cv", 2)], ["uT"])
            P.dma("sp", self.uT_s[cc * 128:(cc + 1) * 128, :], uT[:], ["uT"], [("uTs", cc)], "h_u")
            P.dma("sp", self.x0_s[cc * 128:(cc + 1) * 128, :], cv[0][:], [("cv", 0)], [("x0s", cc)], "h_x0")
            for g in range(8):
                b = self.bank()
                for q in range(4):
                    n = g * 4 + q
                    P.tr(self.ps[b][:, q * 128:(q + 1) * 128], uT[:, n * 128:(n + 1) * 128], self.identf[:],
                         ["uT", "identf"], [("ps", b)])
                P.act(ust[:, g * 4:(g + 1) * 4, :].rearrange("p a b -> p (a b)"), self.ps[b][:], AF.Copy,
                      [("ps", b)], ["ust"])
            P.dma("sp", self.u_s.rearrange("(n p) c -> p n c", p=128)[:, :, cc * 128:(cc + 1) * 128], ust[:],
                  ["ust"], [("us", cc)], "h_us")
        self.phase_end()

    def dft_fwd(self, srcC, keysC, srcS, keysS, tf, consume):
        P = self.P
        for fi in range(32):
            banks = []
            for cs in range(2):
                sl = (fi * 2 + cs) % 4
                P.dma("sp", tf[sl][:], self.TF[cs, fi], [], [("tf", sl)], "d_tf%d" % sl)
                b = self.bank()
                src, keys = (srcC, keysC) if cs == 0 else (srcS, keysS)
                for n in range(32):
                    P.mm(self.ps[b][:], tf[sl][:, n, :], src[:, n, :], n == 0, n == 31,
                         [("tf", sl), keys(n)], [("ps", b)])
                banks.append(b)
            consume(fi, banks[0], banks[1])

    def hyena_bc_phase(self, j):
        P = self.P
        self.phase_begin()
        uti = self.sbp("uti", [128, 32, 512], BF16)
        Y = self.sbp("Y", [128, 64, 512], BF16)
        tf = [self.sbp("tf", [128, 32, 128], BF16) for _ in range(4)]
        kt = [self.sbp("kt", [128, 512], F32) for _ in range(4)]
        et = [self.sbp("et", [128, 512], F32) for _ in range(4)]
        tt = [self.sbp("tt", [128, 512], F32) for _ in range(4)]
        ob = [self.sbp("ob", [128, 512], BF16) for _ in range(2)]
        hb = self.sbp("hb", [128, 8], F32)
        P.dma("sp", hb[:], self.hbias[j], [], ["hb"], "d_hb")
        cnt = [0, 0, 0]
        for ch in range(2):
            csl = slice(ch * 512, (ch + 1) * 512)
            P.dma("sp", uti[:], self.u_s.rearrange("(n p) c -> p n c", p=128)[:, :, csl],
                  [("us", c) for c in range(8)], [("uti", 0), ("uti", 1)], "d_u")

            def consume(fi, bP, bQ, ch=ch, csl=csl):
                k0 = cnt[0] % 2
                cnt[0] += 1
                kp, kq = kt[2 * k0], kt[2 * k0 + 1]
                fsl = slice(fi * 128, (fi + 1) * 128)
                P.dma("sp", kp[:], self.Ksp[j, 0, fsl, csl], [("Ksp", j)], [("kt", 2 * k0)], "d_kp%d" % k0)
                P.dma("sp", kq[:], self.Ksp[j, 1, fsl, csl], [("Ksp", j)], [("kt", 2 * k0 + 1)], "d_kq%d" % k0)
                t0, t1, t2, t3 = tt
                base = (cnt[0] % 1) * 0
                P.v("dve", "tensor_tensor", (t0[:], self.ps[bP][:], kp[:], ALU.mult), [("ps", bP), ("kt", 2 * k0)], [("tt", 0)])
                P.v("dve", "tensor_tensor", (t1[:], self.ps[bQ][:], kq[:], ALU.mult), [("ps", bQ), ("kt", 2 * k0 + 1)], [("tt", 1)])
                P.v("dve", "tensor_tensor", (t2[:], self.ps[bP][:], kq[:], ALU.mult), [("ps", bP), ("kt", 2 * k0 + 1)], [("tt", 2)])
                P.v("dve", "tensor_tensor", (t3[:], self.ps[bQ][:], kp[:], ALU.mult), [("ps", bQ), ("kt", 2 * k0)], [("tt", 3)])
                P.v("pool", "tensor_tensor", (Y[:, fi, :], t0[:], t1[:], ALU.subtract), [("tt", 0), ("tt", 1)], [("Y", fi)])
                P.v("pool", "tensor_tensor", (Y[:, 32 + fi, :], t2[:], t3[:], ALU.add), [("tt", 2), ("tt", 3)], [("Y", 32 + fi)])

            self.dft_fwd(uti, lambda n: ("uti", n // 16), uti, lambda n: ("uti", n // 16), tf, consume)
            for tb in range(8):
                tsl = slice(tb * 512, (tb + 1) * 512)
                banks = [self.bank() for _ in range(4)]
                for kg in range(4):
                    cs, f0 = kg // 2, (kg % 2) * 2048
                    sl = cnt[1] % 2
                    cnt[1] += 1
                    ti = uti[:, sl * 16:(sl + 1) * 16, :]
                    P.dma("sp", ti, self.TI[cs, f0:f0 + 2048, :].rearrange("(k p) t -> p k t", p=128)[:, :, tsl],
                          [], [("uti", sl)], "d_ti%d" % sl)
                    for c in range(4):
                        for kk in range(16):
                            yk = cs * 32 + (kg % 2) * 16 + kk
                            P.mm(self.ps[banks[c]][:], Y[:, yk, c * 128:(c + 1) * 128], ti[:, kk, :],
                                 kg == 0 and kk == 0, kg == 3 and kk == 15,
                                 [("Y", yk), ("uti", sl)], [("ps", banks[c])])
                for c in range(4):
                    cc = ch * 4 + c
                    e0 = cnt[2] % 2
                    cnt[2] += 1
                    eu, ex = et[2 * e0], et[2 * e0 + 1]
                    P.dma("sp", eu[:], self.uT_s[cc * 128:(cc + 1) * 128, tsl], [("uTs", cc)], [("et", 2 * e0)], "d_eu%d" % e0)
                    P.dma("sp", ex[:], self.x0_s[cc * 128:(cc + 1) * 128, tsl], [("x0s", cc)], [("et", 2 * e0 + 1)], "d_ex%d" % e0)
                    P.v("dve", "scalar_tensor_tensor", (eu[:], eu[:], hb[:, cc:cc + 1], self.ps[banks[c]][:], ALU.mult, ALU.add),
                        [("et", 2 * e0), "hb", ("ps", banks[c])], [("et", 2 * e0)])
                    P.v("pool", "tensor_tensor", (ob[e0][:], eu[:], ex[:], ALU.mult),
                        [("et", 2 * e0), ("et", 2 * e0 + 1)], [("ob", e0)])
                    P.dma("sp", self.abT_s[1024 + cc * 128:1024 + (cc + 1) * 128, tsl], ob[e0][:],
                          [("ob", e0)], [("abT", "b", cc, tb)], "d_ob%d" % e0)
        self.phase_end()

    def filter_phase(self, j):
        P = self.P
        self.phase_begin()
        zf = self.sbp("zf", [33, L], F32)
        hA = self.sbp("hA", [64, L], F32)
        hB = self.sbp("hB", [64, L], F32)
        fw1 = self.sbp("fw1", [33, 64], F32)
        fw2 = self.sbp("fw2", [64, 64], F32)
        fw3 = self.sbp("fw3", [64, 64], F32)
        fwo = self.sbp("fwo", [64, 2048], F32)
        fv = self.sbp("fv", [64, 4], F32)
        fs = self.sbp("fs", [64, 4], F32)
        dl = self.sbp("dl", [128, D_B], F32)
        tc = self.sbp("tc", [128, 32], F32)
        hs = self.sbp("hs", [128, 32, 512], BF16)
        hd = self.sbp("hd", [128, 32, 512], BF16)
        dec = [self.sbp("dec", [128, 512], F32) for _ in range(2)]
        hf = [self.sbp("hf", [128, 512], F32) for _ in range(2)]
        hbw = [self.sbp("hbw", [128, 512], F32) for _ in range(2)]
        ab = [self.sbp("ab", [128, 512], F32) for _ in range(2)]
        ab2 = [self.sbp("ab2", [128, 512], F32) for _ in range(2)]
        rn = self.sbp("rn", [128, 512], F32)
        st = [self.sbp("st", [64, 512], F32) for _ in range(2)]
        s2 = [self.sbp("s2", [64, 512], F32) for _ in range(2)]
        tf = [self.sbp("tf", [128, 32, 128], BF16) for _ in range(4)]
        ko = [self.sbp("ko", [128, 512], F32) for _ in range(4)]
        P.dma("sp", zf[:], self.zfeat, [], ["zf"], "f_c0")
        P.dma("sp", fw1[:], self.f_w1[j], [], ["fw1"], "f_c1")
        P.dma("sp", fw2[:], self.f_w2[j], [], ["fw2"], "f_c2")
        P.dma("sp", fw3[:], self.f_w3[j], [], ["fw3"], "f_c3")
        P.dma("sp", fwo[:], self.f_wout[j], [], ["fwo"], "f_c4")
        P.dma("sp", fv[:], self.f_vec[j], [], ["fv"], "f_c5")
        P.dma("sp", dl[:], self.deltas.partition_broadcast(128), [], ["dl"], "f_c6")
        P.dma("sp", tc[:], self.tcol, [], ["tc"], "f_c7")
        P.v("dve", "tensor_scalar", (fs[:, 0:1], fv[:, 3:4], 1.0 / 3.0, None, ALU.mult), ["fv"], ["fs"])
        P.v("dve", "tensor_scalar", (fs[:, 1:4], fv[:, 0:3], fs[:, 0:1], None, ALU.mult), ["fv", "fs"], ["fs"])
        P.v("dve", "tensor_scalar", (tc[:], tc[:], -1.0, None, ALU.mult), ["tc"], ["tc"])

        def sin_layer(w, wkey, src, skey, dst, dkey, li, kdim):
            for nb in range(8):
                b = self.bank()
                P.mm(self.ps[b][0:64, :], w[0:kdim, :], src[0:kdim, nb * 512:(nb + 1) * 512], True, True,
                     [wkey, skey], [("ps", b)])
                a = nb % 2
                P.act(st[a][:], self.ps[b][0:64, :], AF.Sin, [("ps", b), "fs"], [("st", a)],
                      bias=fs[:, li:li + 1], scale=fs[:, 0:1])
                P.v("dve", "tensor_tensor", (s2[a][:], st[a][:], st[a][:], ALU.mult), [("st", a)], [("s2", a)])
                P.v("dve", "tensor_scalar", (s2[a][:], s2[a][:], -4.0, 3.0, ALU.mult, ALU.add), [("s2", a)], [("s2", a)])
                P.v("dve", "tensor_tensor", (dst[:, nb * 512:(nb + 1) * 512], s2[a][:], st[a][:], ALU.mult),
                    [("st", a), ("s2", a)], [dkey])

        sin_layer(fw1, "fw1", zf, "zf", hA, "hA", 1, 33)
        sin_layer(fw2, "fw2", hA, "hA", hB, "hB", 2, 64)
        sin_layer(fw3, "fw3", hB, "hB", hA, "hA", 3, 64)
        cnt = [0]
        for ch in range(2):
            csl = slice(ch * 512, (ch + 1) * 512)
            bn = self.bank()
            for n in range(32):
                a = n % 2
                bf_, bb_ = self.bank(avoid=bn), self.bank(avoid=bn)
                P.mm(self.ps[bf_][:], hA[:, n * 128:(n + 1) * 128], fwo[:, ch * 512:(ch + 1) * 512], True, True,
                     ["hA", "fwo"], [("ps", bf_)])
                P.mm(self.ps[bb_][:], hA[:, n * 128:(n + 1) * 128], fwo[:, D_B + ch * 512:D_B + (ch + 1) * 512], True, True,
                     ["hA", "fwo"], [("ps", bb_)])
                P.act(dec[a][:], dl[:, csl], AF.Exp, ["dl", "tc"], [("dec", a)], scale=tc[:, n:n + 1])
                P.v("dve", "tensor_tensor", (hf[a][:], self.ps[bf_][:], dec[a][:], ALU.mult),
                    [("ps", bf_), ("dec", a)], [("hf", a)])
                P.v("dve", "tensor_tensor", (hbw[a][:], self.ps[bb_][:], dec[a][:], ALU.mult),
                    [("ps", bb_), ("dec", a)], [("hbw", a)])
                if n == 0:
                    P.v("dve", "memset", (hbw[a][0:1, :], 0.0), [("hbw", a)], [("hbw", a)])
                P.act(ab[a][:], hf[a][:], AF.Abs, [("hf", a)], [("ab", a)])
                P.act(ab2[a][:], hbw[a][:], AF.Abs, [("hbw", a)], [("ab2", a)])
                P.v("dve", "tensor_tensor", (ab[a][:], ab[a][:], ab2[a][:], ALU.add), [("ab", a), ("ab2", a)], [("ab", a)])
                P.mm(self.ps[bn][:], self.ones[:], ab[a][:], n == 0, n == 31, ["ones", ("ab", a)], [("ps", bn)])
                P.v("dve", "tensor_tensor", (hs[:, n, :], hf[a][:], hbw[a][:], ALU.add), [("hf", a), ("hbw", a)], [("hs", n // 16)])
                P.v("dve", "tensor_tensor", (hd[:, n, :], hf[a][:], hbw[a][:], ALU.subtract), [("hf", a), ("hbw", a)], [("hd", n // 16)])
            P.v("dve", "reciprocal", (rn[:], self.ps[bn][:]), [("ps", bn)], ["rn"])

            def consume(fi, bP, bQ, ch=ch, csl=csl):
                k0 = cnt[0] % 2
                cnt[0] += 1
                fsl = slice(fi * 128, (fi + 1) * 128)
                for cs, bb in ((0, bP), (1, bQ)):
                    o = 2 * k0 + cs
                    P.v("dve", "tensor_tensor", (ko[o][:], self.ps[bb][:], rn[:], ALU.mult), [("ps", bb), "rn"], [("ko", o)])
                    P.dma("sp", self.Ksp[j, cs, fsl, csl], ko[o][:], [("ko", o)], [("Ksp", j, cs, fi, ch)], "f_ko%d" % o)

            self.dft_fwd(hs, lambda n: ("hs", n // 16), hd, lambda n: ("hd", n // 16), tf, consume)
        self.phase_end()


_CONST_CACHE = {}


def algo_consts():
    if _CONST_CACHE:
        return _CONST_CACHE
    f32 = np.float32
    c = {}
    c["ident"] = np.eye(128, dtype=f32)
    t = np.linspace(0.0, 1.0, L, dtype=f32)[:, None]
    w = (2.0 * np.pi * np.arange(L, dtype=f32)[:, None] / L).astype(f32)
    f = np.linspace(1e-4, 15, 16, dtype=f32)[None, :]
    fw = (f * w).astype(f32)
    z = np.concatenate([t, np.cos(fw), -np.sin(fw)], axis=-1).astype(f32)
    c["zfeat"] = np.ascontiguousarray(z.T)
    c["deltas"] = np.abs(np.linspace(math.log(1e-2) / 0.3, math.log(1e-2) / 1.5, D_B, dtype=f32)).astype(f32)
    c["tcol"] = np.ascontiguousarray(t[:, 0].reshape(32, 128).T)
    n = np.arange(L, dtype=np.float64)[:, None]
    fq = np.arange(L, dtype=np.float64)[None, :]
    ang = np.pi * (2 * fq + 1) * n / (2 * L)
    C = np.cos(ang)
    S = np.sin(ang)
    TF = np.stack([C, S])
    TF = TF.reshape(2, 32, 128, 32, 128).transpose(0, 3, 2, 1, 4)
    c["TF"] = np.ascontiguousarray(TF).astype(ml_dtypes.bfloat16)
    TI = np.stack([C.T, S.T]) / L
    c["TI"] = np.ascontiguousarray(TI).astype(ml_dtypes.bfloat16)
    _CONST_CACHE.update(c)
    return c


def bias_tiles(rpb):
    rpb = np.asarray(rpb, np.float32)
    specs = [(2, [0, 2, 4, 6, 8])] + [(m, [0, 2, 4, 6]) for m in (0, 1)] + [(m, [56, 58, 60, 62]) for m in (30, 31)]
    kc = np.arange(64)[:, None]
    qc = np.arange(64)[None, :]
    cs = np.clip(qc - 8, 0, 48)
    col_ok = (kc >= cs) & (kc < cs + 16)
    dc = np.clip(kc - qc, -15, 15) + 15
    out = np.full((2, NH, 128, NSLOT, 128), NEG, np.float32)
    slot = 0
    for m, krows in specs:
        for kr0 in krows:
            for pk in range(2):
                kr = kr0 + pk
                for pq in range(2):
                    r = 2 * m + pq
                    rs = min(max(r - 4, 0), 56)
                    if not (rs <= kr < rs + 8):
                        continue
                    dr = kr - r + 7
                    vals = rpb[:, :, dr, :][:, :, dc]
                    vals = np.where(col_ok[None, None], vals, NEG)
                    out[:, :, pk * 64:(pk + 1) * 64, slot, pq * 64:(pq + 1) * 64] = vals
            slot += 1
    return out.astype(ml_dtypes.bfloat16)


def host_consts(inp):
    f32 = np.float32
    g = lambda k: np.asarray(inp[k], f32)
    c = dict(algo_consts())
    c["norm_mix"] = np.ascontiguousarray(g("norm_mix").reshape(DEPTH, KC, 128).transpose(0, 2, 1))
    c["norm_mlp"] = np.ascontiguousarray(g("norm_mlp").reshape(DEPTH, KC, 128).transpose(0, 2, 1))
    c["w_mlp1"] = g("w_mlp1")
    c["w_mlp2"] = g("w_mlp2")
    c["w_in_c"] = g("w_in_c")
    c["v_gain"] = g("v_gain")
    c["w_sT"] = np.ascontiguousarray(g("w_s").transpose(0, 1, 3, 2))
    c["b_s"] = np.ascontiguousarray(g("b_s").reshape(2, 8 * 128))
    c["w_out_c"] = g("w_out_c")
    c["w_in_ab"] = g("w_in_ab")
    c["w_out_ab"] = g("w_out_ab")
    c["qk_gain"] = np.ascontiguousarray(np.stack([g("q_gain"), g("k_gain")], axis=-1))
    c["biasT"] = bias_tiles(inp["rpb"])
    scw = np.concatenate([g("sc_w"), g("sc_b")[:, None, :]], axis=1)
    c["scw"] = np.ascontiguousarray(scw.reshape(2, 4, 24, 128).transpose(0, 3, 1, 2))
    c["hbias"] = np.ascontiguousarray(g("h_bias").reshape(2, 8, 128).transpose(0, 2, 1))
    c["f_w1"] = g("f_w1")
    c["f_w2"] = g("f_w2")
    c["f_w3"] = g("f_w3")
    c["f_wout"] = g("f_wout")
    c["f_vec"] = np.ascontiguousarray(np.stack([g("f_b1"), g("f_b2"), g("f_b3"), g("f_freq")], axis=-1))
    return c


def run_cfg(cfg, x_per_core, inp, trace=False):
    b = Builder(cfg)
    nc = b.build()
    consts = host_consts(inp)
    in_maps = []
    for xc in x_per_core:
        m = dict(consts)
        m["x_in"] = np.ascontiguousarray(xc, dtype=np.float32)
        in_maps.append(m)
    res = run_bass_kernel_spmd(nc, in_maps, core_ids=list(range(len(x_per_core))), trace=trace)
    return res


def kernel(**inputs):
    xp = np.asarray(inputs["x_prompt"], np.float32)
    xsm = np.asarray(inputs["x_sample"], np.float32)
    owners = {0: 0, 4: 1}
    zero = np.zeros_like(xp[0])
    x_per_core = []
    for c in range(8):
        second = xsm[owners[c]] if c in owners else zero
        x_per_core.append(np.stack([xp[c], second]))
    res = run_cfg({"nseq": 2}, x_per_core, inputs)
    y_prompt = np.stack([res.results[c]["y_out"][0] for c in range(8)]).astype(np.float32)
    y_sample = np.stack([res.results[c]["y_out"][1] for c in (0, 4)]).astype(np.float32)
    return (y_prompt, y_sample)
```

```python
import math
from contextlib import ExitStack

import numpy as np
import ml_dtypes

import concourse.bass as bass
import concourse.mybir as mybir
from concourse.bass_utils import run_bass_kernel_spmd

F32 = mybir.dt.float32
BF16 = mybir.dt.bfloat16
AF = mybir.ActivationFunctionType
ALU = mybir.AluOpType

D = 2048
L = 4096
KC = D // 128
T = 512
NT = L // T
DFF = 4 * D
DEPTH = 4
D_A = 1024
D_B = 1024
NH = 8
EPS = 1e-6
GELU_C = math.sqrt(2.0 / math.pi)


class Prog:
    COMPUTE = ("pe", "act", "dve", "pool")

    def __init__(self, nc):
        self.nc = nc
        self.ops = []
        self.lastw = {}
        self.readers = {}
        self.ps_next = 0
        self.last_by_src = {}
        self.fence_pending = {}
        self.pos = []
        self.nofence = set()

    def barrier(self):
        f = {k: v for k, v in self.last_by_src.items() if k not in self.nofence}
        for e in ("pe", "act", "dve", "pool", "sp"):
            d = dict(self.fence_pending.get(e, {}))
            d.update(f)
            self.fence_pending[e] = d

    def _src(self, idx):
        op = self.ops[idx]
        return ("dma", op[3]) if op[3] is not None else ("eng", op[0])

    def add(self, eng, fn, r=(), w=(), dma_grp=None, pos=None):
        idx = len(self.ops)
        self.pos.append(idx if pos is None else pos)
        deps = self.fence_pending.pop(eng, None) or {}

        def need(d):
            s = self._src(d)
            if deps.get(s, -1) < d:
                deps[s] = d

        for k in r:
            lw = self.lastw.get(k)
            if lw is not None:
                need(lw)
        for k in w:
            lw = self.lastw.get(k)
            if lw is not None:
                need(lw)
            rd = self.readers.get(k)
            if rd:
                for d in rd.values():
                    need(d)
        self.ops.append([eng, fn, deps, dma_grp])
        me = ("dma", dma_grp) if dma_grp is not None else ("eng", eng)
        self.last_by_src[me] = idx
        for k in w:
            self.lastw[k] = idx
            self.readers[k] = {}
        for k in r:
            self.readers.setdefault(k, {})[me] = idx
        return idx

    def mm(self, out, lhsT, rhs, start, stop, r, w):
        return self.add("pe", lambda e: e.matmul(out, lhsT, rhs, start=start, stop=stop), r, w)

    def tr(self, out, in_, ident, r, w):
        return self.add("pe", lambda e: e.transpose(out, in_, ident), r, w)

    def act(self, out, in_, func, r, w, bias=None, scale=None):
        kw = {}
        if bias is not None:
            kw["bias"] = bias
        if scale is not None:
            kw["scale"] = scale
        return self.add("act", lambda e: e.activation(out, in_, func, **kw), r, w)

    def v(self, eng, name, args, r, w, **kw):
        return self.add(eng, lambda e: getattr(e, name)(*args, **kw), r, w)

    def dma(self, q, out, in_, r, w, grp, pos=None, **kw):
        return self.add(q, lambda e: e.dma_start(out=out, in_=in_, **kw), r, w, dma_grp=grp, pos=pos)

    def emit(self, stack, same_engine_sync=True):
        nc = self.nc
        ops = self.ops
        signal = set()
        for idx, (eng, fn, deps, grp) in enumerate(ops):
            for s, d in deps.items():
                if s[0] == "eng":
                    if s[1] == eng and grp is None and (eng == "pe" or not same_engine_sync):
                        continue
                    signal.add(d)
        sem_eng = {e: stack.enter_context(nc.semaphore("sem_" + e)) for e in self.COMPUTE}
        grp_names = []
        for op in ops:
            if op[3] is not None and op[3] not in grp_names:
                grp_names.append(op[3])
        sem_grp = {g: stack.enter_context(nc.semaphore("dg_%d" % i)) for i, g in enumerate(grp_names)}
        cnt = {}
        run = {e: 0 for e in self.COMPUTE}
        rung = {g: 0 for g in grp_names}
        prevg = {}
        for idx, (eng, fn, deps, grp) in enumerate(ops):
            if grp is not None:
                prevg[idx] = rung[grp]
                rung[grp] += 16
                cnt[idx] = rung[grp]
            elif idx in signal:
                run[eng] += 1
                cnt[idx] = run[eng]
        per_eng = {e: [] for e in ("pe", "act", "dve", "pool", "sp")}
        for idx, op in enumerate(ops):
            per_eng[op[0]].append(idx)
        for e_ in per_eng:
            per_eng[e_].sort(key=lambda i: (self.pos[i], i))
        final_g = dict(rung)

        def emit_engine(ename, e):
            waited = {}
            for idx in per_eng[ename]:
                eng, fn, deps, grp = ops[idx]
                for s, d in deps.items():
                    if s[0] == "eng":
                        if s[1] == eng and grp is None and (eng == "pe" or not same_engine_sync):
                            continue
                        sem = sem_eng[s[1]]
                    else:
                        sem = sem_grp[s[1]]
                    c = cnt[d]
                    if waited.get(s, 0) >= c:
                        continue
                    e.wait_ge(sem, c)
                    waited[s] = c
                if grp is not None:
                    s = ("dma", grp)
                    if prevg[idx] > 0 and waited.get(s, 0) < prevg[idx]:
                        e.wait_ge(sem_grp[grp], prevg[idx])
                        waited[s] = prevg[idx]
                    fn(e).then_inc(sem_grp[grp], 16)
                else:
                    ins = fn(e)
                    if idx in signal:
                        ins.then_inc(sem_eng[eng], 1)
            if ename == "sp":
                for g, c in final_g.items():
                    if c > 0:
                        e.wait_ge(sem_grp[g], c)

        with nc.Block() as block:
            @block.tensor
            def _(e):
                emit_engine("pe", e)

            @block.scalar
            def _(e):
                emit_engine("act", e)

            @block.vector
            def _(e):
                emit_engine("dve", e)

            @block.gpsimd
            def _(e):
                emit_engine("pool", e)

            @block.sync
            def _(e):
                emit_engine("sp", e)


NSLOT = 21
NEG = -30000.0


class Builder:
    def __init__(self, cfg):
        self.cfg = cfg
        self.nseq = cfg.get("nseq", 2)
        self.tiles = cfg.get("tiles", list(range(NT)))
        self.layers = cfg.get("layers", list(range(DEPTH)))
        nc = bass.Bass("TRN2", target_bir_lowering=False)
        self.nc = nc
        self.P = Prog(nc)
        self.stack = ExitStack()
        self.pstack = None
        self.nm = 0
        self._decl_dram()

    def din(self, name, shape, dt=F32):
        return self.nc.dram_tensor(name, list(shape), dt, kind="ExternalInput").ap()

    def dscr(self, name, shape, dt=F32):
        if self.cfg.get("debug", False):
            return self.nc.dram_tensor(name, list(shape), dt, kind="ExternalOutput").ap()
        return self.nc.dram_tensor(name, list(shape), dt).ap()

    def _decl_dram(self):
        nc = self.nc
        ns = self.nseq
        self.x_in = self.din("x_in", [ns, L, D])
        self.y_out = nc.dram_tensor("y_out", [ns, L, D], F32, kind="ExternalOutput").ap()
        self.norm_mix = self.din("norm_mix", [DEPTH, 128, KC])
        self.norm_mlp = self.din("norm_mlp", [DEPTH, 128, KC])
        self.w_mlp1 = self.din("w_mlp1", [DEPTH, D, DFF])
        self.w_mlp2 = self.din("w_mlp2", [DEPTH, DFF, D])
        self.w_in_c = self.din("w_in_c", [2, D, 2 * D])
        self.v_gain = self.din("v_gain", [2, D])
        self.w_sT = self.din("w_sT", [2, 8, 128, 128])
        self.b_s = self.din("b_s", [2, 8 * 128])
        self.w_out_c = self.din("w_out_c", [2, D, D])
        self.ident = self.din("ident", [128, 128])
        self.w_in_ab = self.din("w_in_ab", [2, D, 6144])
        self.w_out_ab = self.din("w_out_ab", [2, D, D])
        self.qk_gain = self.din("qk_gain", [2, 128, 2])
        self.biasT = self.din("biasT", [2, NH, 128, NSLOT, 128], BF16)
        self.scw = self.din("scw", [2, 128, 4, 24])
        self.hbias = self.din("hbias", [2, 128, 8])
        self.f_w1 = self.din("f_w1", [2, 33, 64])
        self.f_w2 = self.din("f_w2", [2, 64, 64])
        self.f_w3 = self.din("f_w3", [2, 64, 64])
        self.f_wout = self.din("f_wout", [2, 64, 2048])
        self.f_vec = self.din("f_vec", [2, 64, 4])
        self.zfeat = self.din("zfeat", [33, L])
        self.deltas = self.din("deltas", [D_B])
        self.tcol = self.din("tcol", [128, 32])
        self.TF = self.din("TF", [2, 32, 128, 32, 128], BF16)
        self.TI = self.din("TI", [2, L, L], BF16)
        self.wb = {}
        for nm_, ap_ in (("w_mlp1", self.w_mlp1), ("w_mlp2", self.w_mlp2), ("w_in_c", self.w_in_c),
                         ("w_out_c", self.w_out_c), ("w_in_ab", self.w_in_ab), ("w_out_ab", self.w_out_ab)):
            self.wb[nm_] = self.nc.dram_tensor(nm_ + "_bf", list(ap_.shape), BF16).ap()
        self.xs = self.dscr("xs_scratch", [D, L])
        self.qT_s = self.dscr("qT_s", [D_A, L], BF16)
        self.kT_s = self.dscr("kT_s", [D_A, L], BF16)
        self.v_s = self.dscr("v_s", [L, D_A], BF16)
        self.zT_s = self.dscr("zT_s", [3 * D_B, L])
        self.uT_s = self.dscr("uT_s", [D_B, L])
        self.x0_s = self.dscr("x0_s", [D_B, L])
        self.u_s = self.dscr("u_s", [L, D_B], BF16)
        self.abT_s = self.dscr("abT_s", [D, L], BF16)
        self.Ksp = self.dscr("Ksp", [2, 2, L, D_B])

    def sb(self, name, shape, dt):
        return self.stack.enter_context(self.nc.sbuf_tensor(name, list(shape), dt))

    def sbp(self, name, shape, dt):
        self.nm += 1
        return self.pstack.enter_context(self.nc.sbuf_tensor("%s_%d" % (name, self.nm), list(shape), dt))

    def phase_begin(self):
        self.w_hist = []
        self.pstack = ExitStack()
        self.ph = self.nm

    def phase_end(self):
        self.P.barrier()
        self.pstack.close()
        self.pstack = None

    def bank(self, avoid=None):
        b = self.P.ps_next
        if avoid is not None and b == avoid:
            b = (b + 1) % 8
        self.P.ps_next = (b + 1) % 8
        return b

    def build(self):
        P = self.P
        nc = self.nc
        self.ps = [self.stack.enter_context(nc.psum_tensor("ps%d" % i, [128, 512], F32)) for i in range(8)]
        self.ones = self.sb("ones", [128, 128], F32)
        self.identf = self.sb("identf", [128, 128], F32)
        self.onesb = self.sb("onesb", [128, 128], BF16)
        self.identb = self.sb("identb", [128, 128], BF16)
        self.epsc = self.sb("epsc", [128, 4], F32)
        P.v("pool", "memset", (self.ones[:], 1.0), [], ["ones"])
        P.v("pool", "memset", (self.onesb[:], 1.0), [], ["onesb"])
        P.v("pool", "memset", (self.epsc[:, 0:1], D * EPS), [], ["epsc"])
        P.v("pool", "memset", (self.epsc[:, 1:2], EPS), [], ["epsc"])
        P.v("pool", "memset", (self.epsc[:, 2:3], 128 * EPS), [], ["epsc"])
        P.v("pool", "memset", (self.epsc[:, 3:4], 0.0), [], ["epsc"])
        self.epsD = self.epsc[:, 0:1]
        self.eps1 = self.epsc[:, 1:2]
        self.eps128 = self.epsc[:, 2:3]
        P.dma("sp", self.identf[:], self.ident, [], ["identf"], "c_ident")
        P.dma("pool", self.identb[:], self.ident, [], ["identb"], "c_identb")
        self.gvec = self.sb("gvec", [128, 2 * DEPTH, KC], F32)
        self.gsc = self.sb("gsc", [128, 2 * DEPTH, KC], F32)
        P.dma("sp", self.gvec[:, 0:DEPTH, :], self.norm_mix.rearrange("l p k -> p l k"), [], ["gvec"], "c_g1")
        P.dma("sp", self.gvec[:, DEPTH:2 * DEPTH, :], self.norm_mlp.rearrange("l p k -> p l k"), [], ["gvec"], "c_g2")
        P.v("dve", "tensor_scalar", (self.gsc[:], self.gvec[:], math.sqrt(D), None, ALU.mult), ["gvec"], ["gsc"])
        self.qkg = self.sb("qkg", [128, 2, 2], F32)
        P.dma("sp", self.qkg[:], self.qk_gain.rearrange("j p c -> p j c"), [], ["qkg"], "c_qk")
        P.v("dve", "tensor_scalar", (self.qkg[:, :, 1:2], self.qkg[:, :, 1:2], math.sqrt(128.0), None, ALU.mult),
            ["qkg"], ["qkg"])
        srcs = {"w_mlp1": self.w_mlp1, "w_mlp2": self.w_mlp2, "w_in_c": self.w_in_c, "w_out_c": self.w_out_c,
                "w_in_ab": self.w_in_ab, "w_out_ab": self.w_out_ab}
        order = []
        for l in self.layers:
            if l % 2 == 0:
                order += [("w_in_ab", l // 2), ("w_out_ab", l // 2)]
            else:
                order += [("w_in_c", l // 2), ("w_out_c", l // 2)]
            order += [("w_mlp1", l), ("w_mlp2", l)]
        self.wbkeys = {}
        ci = 0
        for nm_, li in order:
            src = srcs[nm_]
            rows = src.shape[1]
            keys = []
            for r0 in range(0, rows, 512):
                key = ("wbk", nm_, li, r0)
                keys.append(key)
                P.dma("pool", self.wb[nm_][li, r0:r0 + 512, :].rearrange("(p a) c -> p a c", p=128),
                      src[li, r0:r0 + 512, :].rearrange("(p a) c -> p a c", p=128), [], [key], "cv%d" % (ci % 4))
                ci += 1
            self.wbkeys[(nm_, li)] = keys
        for g_ in range(4):
            P.nofence.add(("dma", "cv%d" % g_))
        self.conv_pending = True

        even_layers = [l for l in self.layers if l % 2 == 0]
        for l in even_layers:
            if "F" in self.cfg.get("even_phases", ("F",)):
                self.filter_phase(l // 2)
        for s in range(self.nseq):
            i = 0
            first = True
            while i < len(self.layers):
                l = self.layers[i]
                if l % 2 == 0:
                    eph = self.cfg.get("even_phases", ("E1", "ATT", "HA", "HBC"))
                    if "E1" in eph:
                        self.e1_phase(s, l, from_input=first)
                    if "ATT" in eph:
                        self.attn_phase(l // 2)
                    if "HA" in eph:
                        self.hyena_a_phase(l // 2)
                    if "HBC" in eph:
                        self.hyena_bc_phase(l // 2)
                    if "TILE" not in self.cfg.get("even_phases", ("TILE",)):
                        i += 1
                        continue
                    nxt = self.layers[i + 1] if i + 1 < len(self.layers) and self.layers[i + 1] == l + 1 else None
                    last = (i + (2 if nxt is not None else 1)) >= len(self.layers)
                    self.tile_phase(s, even=l, odd=nxt, from_input=False, to_output=last)
                    i += 2 if nxt is not None else 1
                else:
                    last = (i + 1) >= len(self.layers)
                    self.tile_phase(s, even=None, odd=l, from_input=first, to_output=last)
                    i += 1
                first = False
        P.emit(self.stack, same_engine_sync=self.cfg.get("same_sync", True))
        self.stack.close()
        return nc

    def alloc_tile_bufs(self, odd_j=None, double_x=False):
        P = self.P
        self.xT = self.sbp("xT", [128, KC, T], F32)
        self.xk = "xT"
        self.xTbufs = [(self.xT, "xT")]
        if double_x:
            self.xTbufs.append((self.sbp("xTb", [128, KC, T], F32), "xTb"))
        self.xn = self.sbp("xn", [128, KC, T], BF16)
        self.big = self.sbp("big", [128, 32, T], BF16)
        self.wsl = [self.sbp("wsl%d" % i, [128, 16 * 512], BF16) for i in range(3)]
        self.wsl_next = 0
        self.sqb = [self.sbp("sqb%d" % i, [128, T], BF16) for i in range(4)]
        self.rstd = self.sbp("rstd", [128, T], F32)
        self.tmp = [self.sbp("tmp%d" % i, [128, T], F32) for i in range(6)]
        self.tmp_next = 0
        self.tb = [self.sbp("tb%d" % i, [128, T], BF16) for i in range(4)]
        self.tb_next = 0
        self.iot = self.sbp("iot", [128, D], F32)
        if odd_j is not None:
            j = odd_j
            self.vg = self.sbp("vg", [128, D], F32)
            self.bsb = self.sbp("bsb", [128, 8 * 128], F32)
            self.wst = self.sbp("wst", [128, 8, 128], BF16)
            P.dma("sp", self.vg[:], self.v_gain[j].partition_broadcast(128), [], ["vg"], "c_vg")
            P.dma("sp", self.bsb[:], self.b_s[j].partition_broadcast(128), [], ["bsb"], "c_bs")
            P.dma("pool", self.wst[:], self.w_sT[j].rearrange("g q p -> q g p"), [], ["wst"], "c_ws")

    def wslot(self):
        i = self.wsl_next
        self.wsl_next = (i + 1) % 3
        return i

    def tmpslot(self):
        i = self.tmp_next
        self.tmp_next = (i + 1) % len(self.tmp)
        return i

    def tbslot(self):
        i = self.tb_next
        self.tb_next = (i + 1) % len(self.tb)
        return i

    def load_tile(self, s, t, from_input):
        P = self.P
        if from_input:
            for sub in range(4):
                r0 = t * T + sub * 128
                P.dma("sp", self.iot[:], self.x_in[s, r0:r0 + 128, :], [], ["iot"], "g_iot")
                for kq in range(4):
                    b = self.bank()
                    for kk in range(4):
                        k = kq * 4 + kk
                        P.tr(self.ps[b][:, kk * 128:(kk + 1) * 128], self.iot[:, k * 128:(k + 1) * 128],
                             self.identf[:], ["iot", "identf"], [("ps", b)])
                    for kk in range(4):
                        k = kq * 4 + kk
                        P.v("dve", "tensor_copy", (self.xT[:, k, sub * 128:(sub + 1) * 128],
                                                   self.ps[b][:, kk * 128:(kk + 1) * 128]),
                            [("ps", b)], [(self.xk, k)])
        else:
            P.dma("sp", self.xT[:], self.xs.rearrange("(k p) n -> p k n", p=128)[:, :, t * T:(t + 1) * T],
                  [("xs", t)], [(self.xk, k) for k in range(KC)], "g_xT")

    def store_tile(self, s, t, to_output):
        P = self.P
        if to_output:
            for sub in range(4):
                r0 = t * T + sub * 128
                for kq in range(4):
                    b = self.bank()
                    for kk in range(4):
                        k = kq * 4 + kk
                        P.tr(self.ps[b][:, kk * 128:(kk + 1) * 128], self.xT[:, k, sub * 128:(sub + 1) * 128],
                             self.identf[:], [(self.xk, k), "identf"], [("ps", b)])
                    P.act(self.iot[:, kq * 512:(kq + 1) * 512], self.ps[b][:], AF.Copy, [("ps", b)], ["iot"])
                P.dma("sp", self.y_out[s, r0:r0 + 128, :], self.iot[:], ["iot"], [("yout", s, t, sub)], "g_iot")
        else:
            P.dma("sp", self.xs.rearrange("(k p) n -> p k n", p=128)[:, :, t * T:(t + 1) * T], self.xT[:],
                  [(self.xk, k) for k in range(KC)], [("xs", t)], "g_xT")

    def rmsnorm_tile(self, gi):
        P = self.P
        b = self.bank()
        for k in range(KC):
            sl = k % 4
            if k % 2 == 0 or self.conv_pending:
                P.act(self.sqb[sl][:], self.xT[:, k, :], AF.Square, [(self.xk, k)], [("sqb", sl)])
            else:
                P.v("pool", "tensor_tensor", (self.sqb[sl][:], self.xT[:, k, :], self.xT[:, k, :], ALU.mult),
                    [(self.xk, k)], [("sqb", sl)])
            P.mm(self.ps[b][:], self.onesb[:], self.sqb[sl][:], k == 0, k == KC - 1,
                 ["onesb", ("sqb", sl)], [("ps", b)])
        P.act(self.rstd[:], self.ps[b][:], AF.Sqrt, [("ps", b), "epsc"], ["rstd"], bias=self.epsD)
        P.v("dve", "reciprocal", (self.rstd[:], self.rstd[:]), ["rstd"], ["rstd"])
        for k in range(KC):
            P.v("dve", "scalar_tensor_tensor",
                (self.xn[:, k, :], self.xT[:, k, :], self.gsc[:, gi, k:k + 1], self.rstd[:], ALU.mult, ALU.mult),
                [(self.xk, k), "gsc", "rstd"], [("xn", k)])

    def load_w(self, wspec, nk, c0, ncols, r0=0):
        P = self.P
        wap = self.wb[wspec[0]][wspec[1]]
        rkeys = self.wbkeys[wspec]
        i = self.wslot()
        dst = self.wsl[i][:, 0:nk * ncols].rearrange("p (k n) -> p k n", n=ncols)
        src = wap[r0:r0 + nk * 128, :].rearrange("(k p) n -> p k n", p=128)[:, :, c0:c0 + ncols]
        hist = self.w_hist
        pos = None
        if len(hist) >= 2:
            pos = hist[-2] + 0.5
        idx = P.dma("sp", dst, src, rkeys, [("wsl", i)], "g_w%d" % i, pos=pos)
        hist.append(idx)
        return i, dst

    def mlp(self, l):
        P = self.P
        self.rmsnorm_tile(DEPTH + l)
        w1 = ("w_mlp1", l)
        w2 = ("w_mlp2", l)
        for hh in range(2):
            for jb in range(8):
                wi, wv = self.load_w(w1, KC, (hh * 8 + jb) * 512, 512)
                for jj in range(4):
                    j = jb * 4 + jj
                    b = self.bank()
                    for k in range(KC):
                        P.mm(self.ps[b][:], wv[:, k, jj * 128:(jj + 1) * 128], self.xn[:, k, :], k == 0, k == KC - 1,
                             [("wsl", wi), ("xn", k)], [("ps", b)])
                    ts = self.tmpslot()
                    P.act(self.tmp[ts][:], self.ps[b][:], AF.Relu, [("ps", b)], [("tmp", ts)])
                    P.v("pool", "tensor_tensor", (self.big[:, j, :], self.tmp[ts][:], self.tmp[ts][:], ALU.mult),
                        [("tmp", ts)], [("big", j)])
            for mp in range(8):
                i, dst = self.load_w(w2, 32, mp * 256, 256, r0=hh * 4096)
                for m2 in range(2):
                    m = mp * 2 + m2
                    b = self.bank()
                    for j in range(32):
                        P.mm(self.ps[b][:], dst[:, j, m2 * 128:(m2 + 1) * 128], self.big[:, j, :], j == 0, j == 31,
                             [("wsl", i), ("big", j)], [("ps", b)])
                    P.v("dve", "tensor_tensor", (self.xT[:, m, :], self.xT[:, m, :], self.ps[b][:], ALU.add),
                        [(self.xk, m), ("ps", b)], [(self.xk, m)])

    def gelu(self, out, b, wkeys):
        P = self.P
        a = self.tmpslot()
        c = self.tmpslot()
        xs_, t_ = self.tmp[a], self.tmp[c]
        P.act(xs_[:], self.ps[b][:], AF.Copy, [("ps", b)], [("tmp", a)])
        P.v("pool", "tensor_tensor", (t_[:], xs_[:], xs_[:], ALU.mult), [("tmp", a)], [("tmp", c)])
        P.v("pool", "tensor_scalar", (t_[:], t_[:], 0.044715, 1.0, ALU.mult, ALU.add), [("tmp", c)], [("tmp", c)])
        P.v("dve", "tensor_tensor", (t_[:], t_[:], xs_[:], ALU.mult), [("tmp", a), ("tmp", c)], [("tmp", c)])
        P.act(t_[:], t_[:], AF.Sigmoid, [("tmp", c)], [("tmp", c)], scale=2.0 * GELU_C)
        P.v("dve", "tensor_tensor", (out, t_[:], xs_[:], ALU.mult), [("tmp", a), ("tmp", c)], wkeys)

    def mixer_c(self, l):
        P = self.P
        j = l // 2
        self.rmsnorm_tile(l)
        w_in = ("w_in_c", j)
        for cb in range(4):
            wi, wv = self.load_w(w_in, KC, cb * 512, 512)
            for cc in range(4):
                c = cb * 4 + cc
                b = self.bank()
                for k in range(KC):
                    P.mm(self.ps[b][:], wv[:, k, cc * 128:(cc + 1) * 128], self.xn[:, k, :], k == 0, k == KC - 1,
                         [("wsl", wi), ("xn", k)], [("ps", b)])
                self.gelu(self.big[:, c, :], b, [("big", c)])
        vt = [self.big[:, 16 + 4 * sub:16 + 4 * sub + 4, :].rearrange("p a b -> p (a b)") for sub in range(4)]
        vkeys = [[("big", 16 + 4 * sub + i) for i in range(4)] for sub in range(4)]
        for cb in range(4):
            wi, wv = self.load_w(w_in, KC, D + cb * 512, 512)
            for sub in range(4):
                b = self.bank()
                for k in range(KC):
                    P.mm(self.ps[b][:], self.xn[:, k, sub * 128:(sub + 1) * 128], wv[:, k, :], k == 0, k == KC - 1,
                         [("wsl", wi), ("xn", k)], [("ps", b)])
                self.gelu(vt[sub][:, cb * 512:(cb + 1) * 512], b, [vkeys[sub][cb]])
        for sub in range(4):
            a = self.tmpslot()
            c = self.tmpslot()
            ssq = self.tmp[a]
            junk = self.tmp[c]
            for cb in range(4):
                P.act(junk[:], vt[sub][:, cb * 512:(cb + 1) * 512], AF.Square, [vkeys[sub][cb]], [("tmp", c)])
                P.v("dve", "reduce_sum", (ssq[:, cb:cb + 1], junk[:], mybir.AxisListType.X),
                    [("tmp", c)], [("tmp", a)])
            P.v("dve", "reduce_sum", (ssq[:, 4:5], ssq[:, 0:4], mybir.AxisListType.X), [("tmp", a)], [("tmp", a)])
            P.act(ssq[:, 5:6], ssq[:, 4:5], AF.Sqrt, [("tmp", a), "epsc"], [("tmp", a)], bias=self.eps1, scale=1.0 / D)
            P.v("dve", "reciprocal", (ssq[:, 6:7], ssq[:, 5:6]), [("tmp", a)], [("tmp", a)])
            for cb in range(4):
                P.v("dve", "scalar_tensor_tensor",
                    (vt[sub][:, cb * 512:(cb + 1) * 512], vt[sub][:, cb * 512:(cb + 1) * 512], ssq[:, 6:7],
                     self.vg[:, cb * 512:(cb + 1) * 512], ALU.mult, ALU.mult),
                    [vkeys[sub][cb], ("tmp", a), "vg"], [vkeys[sub][cb]])
        for sub in range(4):
            for cq in range(4):
                b = self.bank()
                for cc in range(4):
                    c = cq * 4 + cc
                    g = c // 2
                    P.mm(self.ps[b][:, cc * 128:(cc + 1) * 128], vt[sub][:, c * 128:(c + 1) * 128],
                         self.wst[:, g, :], True, True, [vkeys[sub][cq], "wst"], [("ps", b)])
                for cc in range(4):
                    c = cq * 4 + cc
                    g = c // 2
                    ts = self.tmpslot()
                    P.v("dve", "tensor_tensor", (self.tmp[ts][:, 0:128], self.ps[b][:, cc * 128:(cc + 1) * 128],
                                                 self.bsb[:, g * 128:(g + 1) * 128], ALU.add),
                        [("ps", b), "bsb"], [("tmp", ts)])
                    P.v("pool", "tensor_tensor", (self.big[:, c, sub * 128:(sub + 1) * 128], self.tmp[ts][:, 0:128],
                                                  self.big[:, c, sub * 128:(sub + 1) * 128], ALU.mult),
                        [("tmp", ts), ("big", c)], [("big", c)])
        self.proj_residual(("w_out_c", j))

    def proj_residual(self, w_o):
        P = self.P
        for mb in range(4):
            wi, wv = self.load_w(w_o, KC, mb * 512, 512)
            for m4 in range(4):
                m = mb * 4 + m4
                b = self.bank()
                for k in range(KC):
                    P.mm(self.ps[b][:], wv[:, k, m4 * 128:(m4 + 1) * 128], self.big[:, k, :], k == 0, k == KC - 1,
                         [("wsl", wi), ("big", k)], [("ps", b)])
                P.v("dve", "tensor_tensor", (self.xT[:, m, :], self.xT[:, m, :], self.ps[b][:], ALU.add),
                    [(self.xk, m), ("ps", b)], [(self.xk, m)])

    def tile_phase(self, s, even, odd, from_input, to_output):
        P = self.P
        if even is None and self.conv_pending:
            P.nofence.clear()
            P.barrier()
            self.conv_pending = False
        self.phase_begin()
        self.alloc_tile_bufs(odd_j=(odd // 2 if odd is not None else None))
        for t in self.tiles:
            self.load_tile(s, t, from_input=from_input)
            if even is not None:
                P.dma("sp", self.big[:, 0:KC, :],
                      self.abT_s.rearrange("(k p) n -> p k n", p=128)[:, :, t * T:(t + 1) * T],
                      [("abT", t)], [("big", k) for k in range(KC)], "g_ab")
                self.proj_residual(("w_out_ab", even // 2))
                self.mlp(even)
            if odd is not None:
                self.mixer_c(odd)
                self.mlp(odd)
            self.store_tile(s, t, to_output=to_output)
        self.phase_end()

    def e1_phase(self, s, l, from_input):
        P = self.P
        j = l // 2
        self.phase_begin()
        self.alloc_tile_bufs(double_x=True)
        w_in = ("w_in_ab", j)

        def fetch(ti):
            self.xT, self.xk = self.xTbufs[ti % 2]
            self.load_tile(s, self.tiles[ti], from_input=from_input)
            if from_input:
                self.store_tile(s, self.tiles[ti], to_output=False)

        fetch(0)
        for ti, t in enumerate(self.tiles):
            self.xT, self.xk = self.xTbufs[ti % 2]
            self.rmsnorm_tile(l)
            if ti + 1 < len(self.tiles):
                fetch(ti + 1)
                self.xT, self.xk = self.xTbufs[ti % 2]
            tsl = slice(t * T, (t + 1) * T)
            pending = []

            def tail(b, a, isq, h):
                b2 = self.bank()
                P.mm(self.ps[b2][:], self.onesb[:], self.sqb[a][:], True, True, ["onesb", ("sqb", a)], [("ps", b2)])
                c = self.tmpslot()
                P.act(self.tmp[c][:], self.ps[b2][:], AF.Sqrt, [("ps", b2), "epsc"], [("tmp", c)], bias=self.eps128)
                P.v("dve", "reciprocal", (self.tmp[c][:], self.tmp[c][:]), [("tmp", c)], [("tmp", c)])
                o = self.tbslot()
                P.v("dve", "scalar_tensor_tensor",
                    (self.tb[o][:], self.ps[b][:], self.qkg[:, j, (0 if isq else 1):(1 if isq else 2)],
                     self.tmp[c][:], ALU.mult, ALU.mult), [("ps", b), "qkg", ("tmp", c)], [("tb", o)])
                dst = (self.qT_s if isq else self.kT_s)[h * 128:(h + 1) * 128, tsl]
                P.dma("sp", dst, self.tb[o][:], [("tb", o)], [("qk", isq, h, t)], "g_tb%d" % o)

            nq = 0
            for blk in range(4):
                wi, wv = self.load_w(w_in, KC, blk * 512, 512)
                isq = blk < 2
                for cc in range(4):
                    h = (blk % 2) * 4 + cc
                    b = self.bank()
                    for k in range(KC):
                        P.mm(self.ps[b][:], wv[:, k, cc * 128:(cc + 1) * 128], self.xn[:, k, :], k == 0, k == KC - 1,
                             [("wsl", wi), ("xn", k)], [("ps", b)])
                    a = nq % 4
                    nq += 1
                    P.act(self.sqb[a][:], self.ps[b][:], AF.Square, [("ps", b)], [("sqb", a)])
                    if pending:
                        tail(*pending.pop())
                    pending.append((b, a, isq, h))
            tail(*pending.pop())
            for blk in range(2):
                wi, wv = self.load_w(w_in, KC, 2048 + blk * 512, 512)
                for sub in range(4):
                    b = self.bank()
                    for k in range(KC):
                        P.mm(self.ps[b][:], self.xn[:, k, sub * 128:(sub + 1) * 128], wv[:, k, :], k == 0, k == KC - 1,
                             [("wsl", wi), ("xn", k)], [("ps", b)])
                    o = self.tbslot()
                    P.act(self.tb[o][:], self.ps[b][:], AF.Copy, [("ps", b)], [("tb", o)])
                    r0 = t * T + sub * 128
                    P.dma("sp", self.v_s[r0:r0 + 128, blk * 512:(blk + 1) * 512], self.tb[o][:],
                          [("tb", o)], [("vs", t, sub, blk)], "g_tb%d" % o)
            for blk in range(6):
                wi, wv = self.load_w(w_in, KC, 3072 + blk * 512, 512)
                for cc in range(4):
                    c = blk * 4 + cc
                    b = self.bank()
                    for k in range(KC):
                        P.mm(self.ps[b][:], wv[:, k, cc * 128:(cc + 1) * 128], self.xn[:, k, :], k == 0, k == KC - 1,
                             [("wsl", wi), ("xn", k)], [("ps", b)])
                    a = self.tmpslot()
                    P.act(self.tmp[a][:], self.ps[b][:], AF.Copy, [("ps", b)], [("tmp", a)])
                    P.dma("sp", self.zT_s[c * 128:(c + 1) * 128, tsl], self.tmp[a][:],
                          [("tmp", a)], [("zs", c, t)], "g_tmp%d" % a)
        self.conv_pending = False
        self.phase_end()

    def attn_phase(self, j):
        P = self.P
        self.phase_begin()
        qT = [self.sbp("qT", [128, L], BF16) for _ in range(2)]
        kT = [self.sbp("kT", [128, L], BF16) for _ in range(2)]
        vh = [self.sbp("vh", [128, 32, 128], BF16) for _ in range(2)]
        bh = [self.sbp("bh", [128, NSLOT, 128], BF16) for _ in range(2)]
        aT = [self.sbp("aT", [128, L], BF16) for _ in range(2)]
        eT = [self.sbp("eT", [128, 5, 128], BF16) for _ in range(3)]
        rd = [self.sbp("rd", [128, 128], F32) for _ in range(3)]
        for h in range(NH):
            sl = h % 2
            P.dma("sp", qT[sl][:], self.qT_s[h * 128:(h + 1) * 128, :], [("qk", True, h)], [("qT", sl)], "a_q%d" % sl)
            P.dma("sp", kT[sl][:], self.kT_s[h * 128:(h + 1) * 128, :], [("qk", False, h)], [("kT", sl)], "a_k%d" % sl)
            P.dma("sp", vh[sl][:], self.v_s.rearrange("(n p) c -> p n c", p=128)[:, :, h * 128:(h + 1) * 128],
                  [], [("vh", sl)], "a_v%d" % sl)
            P.dma("sp", bh[sl][:], self.biasT[j, h], [], [("bh", sl)], "a_b%d" % sl)
            def pv(nch, krow0, ei, qs, sl=sl):
                b3 = self.bank()
                for c in range(nch):
                    vi = krow0 // 2 + c
                    P.mm(self.ps[b3][:, 0:128], vh[sl][:, vi, :], eT[ei][:, c, :], c == 0, c == nch - 1,
                         [("vh", sl), ("eT", ei)], [("ps", b3)])
                for c in range(nch):
                    P.mm(self.ps[b3][:, 128:256], self.onesb[:], eT[ei][:, c, :], c == 0, c == nch - 1,
                         ["onesb", ("eT", ei)], [("ps", b3)])
                P.v("dve", "reciprocal", (rd[ei][:], self.ps[b3][:, 128:256]), [("ps", b3)], [("rd", ei)])
                P.v("dve", "tensor_tensor", (aT[sl][:, qs], self.ps[b3][:, 0:128], rd[ei][:], ALU.mult),
                    [("ps", b3), ("rd", ei)], [("aT", sl)])

            pend = None
            for m in range(32):
                if 2 <= m <= 29:
                    nch, krow0, slot0 = 5, 2 * m - 4, 0
                elif m < 2:
                    nch, krow0, slot0 = 4, 0, 5 + 4 * m
                else:
                    nch, krow0, slot0 = 4, 56, 13 + 4 * (m - 30)
                qs = slice(m * 128, (m + 1) * 128)
                b1 = self.bank()
                b2 = self.bank() if nch == 5 else None
                ei = (h * 32 + m) % 3
                for c in range(nch):
                    bb, co = (b1, c) if c < 4 else (b2, 0)
                    k0 = (krow0 + 2 * c) * 64
                    outp = self.ps[bb][:, co * 128:(co + 1) * 128]
                    P.mm(outp, kT[sl][:, k0:k0 + 128], qT[sl][:, qs], True, False,
                         [("kT", sl), ("qT", sl)], [("ps", bb)])
                    P.mm(outp, self.identb[:], bh[sl][:, slot0 + c, :], False, True,
                         ["identb", ("bh", sl)], [("ps", bb)])
                P.act(eT[ei][:, 0:4, :].rearrange("p a b -> p (a b)"), self.ps[b1][:], AF.Exp,
                      [("ps", b1)], [("eT", ei)])
                if nch == 5:
                    P.act(eT[ei][:, 4, :], self.ps[b2][:, 0:128], AF.Exp, [("ps", b2)], [("eT", ei)])
                if pend is not None:
                    pv(*pend)
                pend = (nch, krow0, ei, qs)
            pv(*pend)
            pend = None
            P.dma("sp", self.abT_s[h * 128:(h + 1) * 128, :], aT[sl][:], [("aT", sl)], [("abT", "a", h)], "a_o%d" % sl)
        self.phase_end()

    def hyena_a_phase(self, j):
        P = self.P
        self.phase_begin()
        zb = [self.sbp("zb", [128, L + 2], F32) for _ in range(3)]
        cv = [self.sbp("cv", [128, L], F32) for _ in range(3)]
        uT = self.sbp("uT", [128, L], F32)
        ust = self.sbp("ust", [128, 32, 128], BF16)
        scw = self.sbp("scw", [128, 4, 24], F32)
        P.dma("sp", scw[:], self.scw[j], [], ["scw"], "h_scw")
        for i in range(3):
            P.v("pool", "memset", (zb[i][:, 0:1], 0.0), [], [("zb", i)])
            P.v("pool", "memset", (zb[i][:, L + 1:L + 2], 0.0), [], [("zb", i)])
        for cc in range(8):
            for i in range(3):
                ci = i * 8 + cc
                eng = "dve"
                P.dma("sp", zb[i][:, 1:L + 1], self.zT_s[ci * 128:(ci + 1) * 128, :], [("zs", ci)], [("zb", i)], "h_z%d" % i)
                P.act(cv[i][:], zb[i][:, 1:L + 1], AF.Identity, [("zb", i), "scw"], [("cv", i)],
                      bias=scw[:, 3, ci:ci + 1], scale=scw[:, 1, ci:ci + 1])
                P.v(eng, "scalar_tensor_tensor", (cv[i][:], zb[i][:, 0:L], scw[:, 0, ci:ci + 1], cv[i][:],
                                                  ALU.mult, ALU.add), [("zb", i), "scw", ("cv", i)], [("cv", i)])
                P.v(eng, "scalar_tensor_tensor", (cv[i][:], zb[i][:, 2:L + 2], scw[:, 2, ci:ci + 1], cv[i][:],
                                                  ALU.mult, ALU.add), [("zb", i), "scw", ("cv", i)], [("cv", i)])
            P.v("pool", "tensor_tensor", (uT[:], cv[2][:], cv[1][:], ALU.mult), [("cv", 1), ("cv", 2)], ["uT"])
            P.dma("sp", self.uT_s[cc * 128:(cc + 1) * 128, :], uT[:], ["uT"], [("uTs", cc)], "h_u")
            P.dma("sp", self.x0_s[cc * 128:(cc + 1) * 128, :], cv[0][:], [("cv", 0)], [("x0s", cc)], "h_x0")
            for g in range(8):
                b = self.bank()
                for q in range(4):
                    n = g * 4 + q
                    P.tr(self.ps[b][:, q * 128:(q + 1) * 128], uT[:, n * 128:(n + 1) * 128], self.identf[:],
                         ["uT", "identf"], [("ps", b)])
                P.act(ust[:, g * 4:(g + 1) * 4, :].rearrange("p a b -> p (a b)"), self.ps[b][:], AF.Copy,
                      [("ps", b)], ["ust"])
            P.dma("sp", self.u_s.rearrange("(n p) c -> p n c", p=128)[:, :, cc * 128:(cc + 1) * 128], ust[:],
                  ["ust"], [("us", cc)], "h_us")
        self.phase_end()

    def dft_fwd(self, srcC, keysC, srcS, keysS, tf, consume):
        P = self.P
        for fi in range(32):
            banks = []
            for cs in range(2):
                sl = (fi * 2 + cs) % 4
                P.dma("sp", tf[sl][:], self.TF[cs, fi], [], [("tf", sl)], "d_tf%d" % sl)
                b = self.bank()
                src, keys = (srcC, keysC) if cs == 0 else (srcS, keysS)
                for n in range(32):
                    P.mm(self.ps[b][:], tf[sl][:, n, :], src[:, n, :], n == 0, n == 31,
                         [("tf", sl), keys(n)], [("ps", b)])
                banks.append(b)
            consume(fi, banks[0], banks[1])

    def hyena_bc_phase(self, j):
        P = self.P
        self.phase_begin()
        uti = self.sbp("uti", [128, 32, 512], BF16)
        Y = self.sbp("Y", [128, 64, 512], BF16)
        tf = [self.sbp("tf", [128, 32, 128], BF16) for _ in range(4)]
        kt = [self.sbp("kt", [128, 512], F32) for _ in range(4)]
        et = [self.sbp("et", [128, 512], F32) for _ in range(4)]
        tt = [self.sbp("tt", [128, 512], F32) for _ in range(4)]
        ob = [self.sbp("ob", [128, 512], BF16) for _ in range(2)]
        hb = self.sbp("hb", [128, 8], F32)
        P.dma("sp", hb[:], self.hbias[j], [], ["hb"], "d_hb")
        cnt = [0, 0, 0]
        for ch in range(2):
            csl = slice(ch * 512, (ch + 1) * 512)
            P.dma("sp", uti[:], self.u_s.rearrange("(n p) c -> p n c", p=128)[:, :, csl],
                  [("us", c) for c in range(8)], [("uti", 0), ("uti", 1)], "d_u")

            def consume(fi, bP, bQ, ch=ch, csl=csl):
                k0 = cnt[0] % 2
                cnt[0] += 1
                kp, kq = kt[2 * k0], kt[2 * k0 + 1]
                fsl = slice(fi * 128, (fi + 1) * 128)
                P.dma("sp", kp[:], self.Ksp[j, 0, fsl, csl], [("Ksp", j)], [("kt", 2 * k0)], "d_kp%d" % k0)
                P.dma("sp", kq[:], self.Ksp[j, 1, fsl, csl], [("Ksp", j)], [("kt", 2 * k0 + 1)], "d_kq%d" % k0)
                t0, t1, t2, t3 = tt
                base = (cnt[0] % 1) * 0
                P.v("dve", "tensor_tensor", (t0[:], self.ps[bP][:], kp[:], ALU.mult), [("ps", bP), ("kt", 2 * k0)], [("tt", 0)])
                P.v("dve", "tensor_tensor", (t1[:], self.ps[bQ][:], kq[:], ALU.mult), [("ps", bQ), ("kt", 2 * k0 + 1)], [("tt", 1)])
                P.v("dve", "tensor_tensor", (t2[:], self.ps[bP][:], kq[:], ALU.mult), [("ps", bP), ("kt", 2 * k0 + 1)], [("tt", 2)])
                P.v("dve", "tensor_tensor", (t3[:], self.ps[bQ][:], kp[:], ALU.mult), [("ps", bQ), ("kt", 2 * k0)], [("tt", 3)])
                P.v("pool", "tensor_tensor", (Y[:, fi, :], t0[:], t1[:], ALU.subtract), [("tt", 0), ("tt", 1)], [("Y", fi)])
                P.v("pool", "tensor_tensor", (Y[:, 32 + fi, :], t2[:], t3[:], ALU.add), [("tt", 2), ("tt", 3)], [("Y", 32 + fi)])

            self.dft_fwd(uti, lambda n: ("uti", n // 16), uti, lambda n: ("uti", n // 16), tf, consume)
            for tb in range(8):
                tsl = slice(tb * 512, (tb + 1) * 512)
                banks = [self.bank() for _ in range(4)]
                for kg in range(4):
                    cs, f0 = kg // 2, (kg % 2) * 2048
                    sl = cnt[1] % 2
                    cnt[1] += 1
                    ti = uti[:, sl * 16:(sl + 1) * 16, :]
                    P.dma("sp", ti, self.TI[cs, f0:f0 + 2048, :].rearrange("(k p) t -> p k t", p=128)[:, :, tsl],
                          [], [("uti", sl)], "d_ti%d" % sl)
                    for c in range(4):
                        for kk in range(16):
                            yk = cs * 32 + (kg % 2) * 16 + kk
                            P.mm(self.ps[banks[c]][:], Y[:, yk, c * 128:(c + 1) * 128], ti[:, kk, :],
                                 kg == 0 and kk == 0, kg == 3 and kk == 15,
                                 [("Y", yk), ("uti", sl)], [("ps", banks[c])])
                for c in range(4):
                    cc = ch * 4 + c
                    e0 = cnt[2] % 2
                    cnt[2] += 1
                    eu, ex = et[2 * e0], et[2 * e0 + 1]
                    P.dma("sp", eu[:], self.uT_s[cc * 128:(cc + 1) * 128, tsl], [("uTs", cc)], [("et", 2 * e0)], "d_eu%d" % e0)
                    P.dma("sp", ex[:], self.x0_s[cc * 128:(cc + 1) * 128, tsl], [("x0s", cc)], [("et", 2 * e0 + 1)], "d_ex%d" % e0)
                    P.v("dve", "scalar_tensor_tensor", (eu[:], eu[:], hb[:, cc:cc + 1], self.ps[banks[c]][:], ALU.mult, ALU.add),
                        [("et", 2 * e0), "hb", ("ps", banks[c])], [("et", 2 * e0)])
                    P.v("pool", "tensor_tensor", (ob[e0][:], eu[:], ex[:], ALU.mult),
                        [("et", 2 * e0), ("et", 2 * e0 + 1)], [("ob", e0)])
                    P.dma("sp", self.abT_s[1024 + cc * 128:1024 + (cc + 1) * 128, tsl], ob[e0][:],
                          [("ob", e0)], [("abT", "b", cc, tb)], "d_ob%d" % e0)
        self.phase_end()

    def filter_phase(self, j):
        P = self.P
        self.phase_begin()
        zf = self.sbp("zf", [33, L], F32)
        hA = self.sbp("hA", [64, L], F32)
        hB = self.sbp("hB", [64, L], F32)
        fw1 = self.sbp("fw1", [33, 64], F32)
        fw2 = self.sbp("fw2", [64, 64], F32)
        fw3 = self.sbp("fw3", [64, 64], F32)
        fwo = self.sbp("fwo", [64, 2048], F32)
        fv = self.sbp("fv", [64, 4], F32)
        fs = self.sbp("fs", [64, 4], F32)
        dl = self.sbp("dl", [128, D_B], F32)
        tc = self.sbp("tc", [128, 32], F32)
        hs = self.sbp("hs", [128, 32, 512], BF16)
        hd = self.sbp("hd", [128, 32, 512], BF16)
        dec = [self.sbp("dec", [128, 512], F32) for _ in range(2)]
        hf = [self.sbp("hf", [128, 512], F32) for _ in range(2)]
        hbw = [self.sbp("hbw", [128, 512], F32) for _ in range(2)]
        ab = [self.sbp("ab", [128, 512], F32) for _ in range(2)]
        ab2 = [self.sbp("ab2", [128, 512], F32) for _ in range(2)]
        rn = self.sbp("rn", [128, 512], F32)
        st = [self.sbp("st", [64, 512], F32) for _ in range(2)]
        s2 = [self.sbp("s2", [64, 512], F32) for _ in range(2)]
        tf = [self.sbp("tf", [128, 32, 128], BF16) for _ in range(4)]
        ko = [self.sbp("ko", [128, 512], F32) for _ in range(4)]
        P.dma("sp", zf[:], self.zfeat, [], ["zf"], "f_c0")
        P.dma("sp", fw1[:], self.f_w1[j], [], ["fw1"], "f_c1")
        P.dma("sp", fw2[:], self.f_w2[j], [], ["fw2"], "f_c2")
        P.dma("sp", fw3[:], self.f_w3[j], [], ["fw3"], "f_c3")
        P.dma("sp", fwo[:], self.f_wout[j], [], ["fwo"], "f_c4")
        P.dma("sp", fv[:], self.f_vec[j], [], ["fv"], "f_c5")
        P.dma("sp", dl[:], self.deltas.partition_broadcast(128), [], ["dl"], "f_c6")
        P.dma("sp", tc[:], self.tcol, [], ["tc"], "f_c7")
        P.v("dve", "tensor_scalar", (fs[:, 0:1], fv[:, 3:4], 1.0 / 3.0, None, ALU.mult), ["fv"], ["fs"])
        P.v("dve", "tensor_scalar", (fs[:, 1:4], fv[:, 0:3], fs[:, 0:1], None, ALU.mult), ["fv", "fs"], ["fs"])
        P.v("dve", "tensor_scalar", (tc[:], tc[:], -1.0, None, ALU.mult), ["tc"], ["tc"])

        def sin_layer(w, wkey, src, skey, dst, dkey, li, kdim):
            for nb in range(8):
                b = self.bank()
                P.mm(self.ps[b][0:64, :], w[0:kdim, :], src[0:kdim, nb * 512:(nb + 1) * 512], True, True,
                     [wkey, skey], [("ps", b)])
                a = nb % 2
                P.act(st[a][:], self.ps[b][0:64, :], AF.Sin, [("ps", b), "fs"], [("st", a)],
                      bias=fs[:, li:li + 1], scale=fs[:, 0:1])
                P.v("dve", "tensor_tensor", (s2[a][:], st[a][:], st[a][:], ALU.mult), [("st", a)], [("s2", a)])
                P.v("dve", "tensor_scalar", (s2[a][:], s2[a][:], -4.0, 3.0, ALU.mult, ALU.add), [("s2", a)], [("s2", a)])
                P.v("dve", "tensor_tensor", (dst[:, nb * 512:(nb + 1) * 512], s2[a][:], st[a][:], ALU.mult),
                    [("st", a), ("s2", a)], [dkey])

        sin_layer(fw1, "fw1", zf, "zf", hA, "hA", 1, 33)
        sin_layer(fw2, "fw2", hA, "hA", hB, "hB", 2, 64)
        sin_layer(fw3, "fw3", hB, "hB", hA, "hA", 3, 64)
        cnt = [0]
        for ch in range(2):
            csl = slice(ch * 512, (ch + 1) * 512)
            bn = self.bank()
            for n in range(32):
                a = n % 2
                bf_, bb_ = self.bank(avoid=bn), self.bank(avoid=bn)
                P.mm(self.ps[bf_][:], hA[:, n * 128:(n + 1) * 128], fwo[:, ch * 512:(ch + 1) * 512], True, True,
                     ["hA", "fwo"], [("ps", bf_)])
                P.mm(self.ps[bb_][:], hA[:, n * 128:(n + 1) * 128], fwo[:, D_B + ch * 512:D_B + (ch + 1) * 512], True, True,
                     ["hA", "fwo"], [("ps", bb_)])
                P.act(dec[a][:], dl[:, csl], AF.Exp, ["dl", "tc"], [("dec", a)], scale=tc[:, n:n + 1])
                P.v("dve", "tensor_tensor", (hf[a][:], self.ps[bf_][:], dec[a][:], ALU.mult),
                    [("ps", bf_), ("dec", a)], [("hf", a)])
                P.v("dve", "tensor_tensor", (hbw[a][:], self.ps[bb_][:], dec[a][:], ALU.mult),
                    [("ps", bb_), ("dec", a)], [("hbw", a)])
                if n == 0:
                    P.v("dve", "memset", (hbw[a][0:1, :], 0.0), [("hbw", a)], [("hbw", a)])
                P.act(ab[a][:], hf[a][:], AF.Abs, [("hf", a)], [("ab", a)])
                P.act(ab2[a][:], hbw[a][:], AF.Abs, [("hbw", a)], [("ab2", a)])
                P.v("dve", "tensor_tensor", (ab[a][:], ab[a][:], ab2[a][:], ALU.add), [("ab", a), ("ab2", a)], [("ab", a)])
                P.mm(self.ps[bn][:], self.ones[:], ab[a][:], n == 0, n == 31, ["ones", ("ab", a)], [("ps", bn)])
                P.v("dve", "tensor_tensor", (hs[:, n, :], hf[a][:], hbw[a][:], ALU.add), [("hf", a), ("hbw", a)], [("hs", n // 16)])
                P.v("dve", "tensor_tensor", (hd[:, n, :], hf[a][:], hbw[a][:], ALU.subtract), [("hf", a), ("hbw", a)], [("hd", n // 16)])
            P.v("dve", "reciprocal", (rn[:], self.ps[bn][:]), [("ps", bn)], ["rn"])

            def consume(fi, bP, bQ, ch=ch, csl=csl):
                k0 = cnt[0] % 2
                cnt[0] += 1
                fsl = slice(fi * 128, (fi + 1) * 128)
                for cs, bb in ((0, bP), (1, bQ)):
                    o = 2 * k0 + cs
                    P.v("dve", "tensor_tensor", (ko[o][:], self.ps[bb][:], rn[:], ALU.mult), [("ps", bb), "rn"], [("ko", o)])
                    P.dma("sp", self.Ksp[j, cs, fsl, csl], ko[o][:], [("ko", o)], [("Ksp", j, cs, fi, ch)], "f_ko%d" % o)

            self.dft_fwd(hs, lambda n: ("hs", n // 16), hd, lambda n: ("hd", n // 16), tf, consume)
        self.phase_end()


_CONST_CACHE = {}


def algo_consts():
    if _CONST_CACHE:
        return _CONST_CACHE
    f32 = np.float32
    c = {}
    c["ident"] = np.eye(128, dtype=f32)
    t = np.linspace(0.0, 1.0, L, dtype=f32)[:, None]
    w = (2.0 * np.pi * np.arange(L, dtype=f32)[:, None] / L).astype(f32)
    f = np.linspace(1e-4, 15, 16, dtype=f32)[None, :]
    fw = (f * w).astype(f32)
    z = np.concatenate([t, np.cos(fw), -np.sin(fw)], axis=-1).astype(f32)
    c["zfeat"] = np.ascontiguousarray(z.T)
    c["deltas"] = np.abs(np.linspace(math.log(1e-2) / 0.3, math.log(1e-2) / 1.5, D_B, dtype=f32)).astype(f32)
    c["tcol"] = np.ascontiguousarray(t[:, 0].reshape(32, 128).T)
    n = np.arange(L, dtype=np.float64)[:, None]
    fq = np.arange(L, dtype=np.float64)[None, :]
    ang = np.pi * (2 * fq + 1) * n / (2 * L)
    C = np.cos(ang)
    S = np.sin(ang)
    TF = np.stack([C, S])
    TF = TF.reshape(2, 32, 128, 32, 128).transpose(0, 3, 2, 1, 4)
    c["TF"] = np.ascontiguousarray(TF).astype(ml_dtypes.bfloat16)
    TI = np.stack([C.T, S.T]) / L
    c["TI"] = np.ascontiguousarray(TI).astype(ml_dtypes.bfloat16)
    _CONST_CACHE.update(c)
    return c


def bias_tiles(rpb):
    rpb = np.asarray(rpb, np.float32)
    specs = [(2, [0, 2, 4, 6, 8])] + [(m, [0, 2, 4, 6]) for m in (0, 1)] + [(m, [56, 58, 60, 62]) for m in (30, 31)]
    kc = np.arange(64)[:, None]
    qc = np.arange(64)[None, :]
    cs = np.clip(qc - 8, 0, 48)
    col_ok = (kc >= cs) & (kc < cs + 16)
    dc = np.clip(kc - qc, -15, 15) + 15
    out = np.full((2, NH, 128, NSLOT, 128), NEG, np.float32)
    slot = 0
    for m, krows in specs:
        for kr0 in krows:
            for pk in range(2):
                kr = kr0 + pk
                for pq in range(2):
                    r = 2 * m + pq
                    rs = min(max(r - 4, 0), 56)
                    if not (rs <= kr < rs + 8):
                        continue
                    dr = kr - r + 7
                    vals = rpb[:, :, dr, :][:, :, dc]
                    vals = np.where(col_ok[None, None], vals, NEG)
                    out[:, :, pk * 64:(pk + 1) * 64, slot, pq * 64:(pq + 1) * 64] = vals
            slot += 1
    return out.astype(ml_dtypes.bfloat16)


def host_consts(inp):
    f32 = np.float32
    g = lambda k: np.asarray(inp[k], f32)
    c = dict(algo_consts())
    c["norm_mix"] = np.ascontiguousarray(g("norm_mix").reshape(DEPTH, KC, 128).transpose(0, 2, 1))
    c["norm_mlp"] = np.ascontiguousarray(g("norm_mlp").reshape(DEPTH, KC, 128).transpose(0, 2, 1))
    c["w_mlp1"] = g("w_mlp1")
    c["w_mlp2"] = g("w_mlp2")
    c["w_in_c"] = g("w_in_c")
    c["v_gain"] = g("v_gain")
    c["w_sT"] = np.ascontiguousarray(g("w_s").transpose(0, 1, 3, 2))
    c["b_s"] = np.ascontiguousarray(g("b_s").reshape(2, 8 * 128))
    c["w_out_c"] = g("w_out_c")
    c["w_in_ab"] = g("w_in_ab")
    c["w_out_ab"] = g("w_out_ab")
    c["qk_gain"] = np.ascontiguousarray(np.stack([g("q_gain"), g("k_gain")], axis=-1))
    c["biasT"] = bias_tiles(inp["rpb"])
    scw = np.concatenate([g("sc_w"), g("sc_b")[:, None, :]], axis=1)
    c["scw"] = np.ascontiguousarray(scw.reshape(2, 4, 24, 128).transpose(0, 3, 1, 2))
    c["hbias"] = np.ascontiguousarray(g("h_bias").reshape(2, 8, 128).transpose(0, 2, 1))
    c["f_w1"] = g("f_w1")
    c["f_w2"] = g("f_w2")
    c["f_w3"] = g("f_w3")
    c["f_wout"] = g("f_wout")
    c["f_vec"] = np.ascontiguousarray(np.stack([g("f_b1"), g("f_b2"), g("f_b3"), g("f_freq")], axis=-1))
    return c


def run_cfg(cfg, x_per_core, inp, trace=False):
    b = Builder(cfg)
    nc = b.build()
    consts = host_consts(inp)
    in_maps = []
    for xc in x_per_core:
        m = dict(consts)
        m["x_in"] = np.ascontiguousarray(xc, dtype=np.float32)
        in_maps.append(m)
    res = run_bass_kernel_spmd(nc, in_maps, core_ids=list(range(len(x_per_core))), trace=trace)
    return res


def kernel(**inputs):
    xp = np.asarray(inputs["x_prompt"], np.float32)
    xsm = np.asarray(inputs["x_sample"], np.float32)
    owners = {0: 0, 4: 1}
    zero = np.zeros_like(xp[0])
    x_per_core = []
    for c in range(8):
        second = xsm[owners[c]] if c in owners else zero
        x_per_core.append(np.stack([xp[c], second]))
    res = run_cfg({"nseq": 2}, x_per_core, inputs)
    y_prompt = np.stack([res.results[c]["y_out"][0] for c in range(8)]).astype(np.float32)
    y_sample = np.stack([res.results[c]["y_out"][1] for c in (0, 4)]).astype(np.float32)
    return (y_prompt, y_sample)
```

```python
import math
from contextlib import ExitStack

import numpy as np
import ml_dtypes

import concourse.bass as bass
import concourse.mybir as mybir
from concourse.bass_utils import run_bass_kernel_spmd

F32 = mybir.dt.float32
BF16 = mybir.dt.bfloat16
AF = mybir.ActivationFunctionType
ALU = mybir.AluOpType

D = 2048
L = 4096
KC = D // 128
T = 512
NT = L // T
DFF = 4 * D
DEPTH = 4
D_A = 1024
D_B = 1024
NH = 8
EPS = 1e-6
GELU_C = math.sqrt(2.0 / math.pi)


class Prog:
    COMPUTE = ("pe", "act", "dve", "pool")

    def __init__(self, nc):
        self.nc = nc
        self.ops = []
        self.lastw = {}
        self.readers = {}
        self.ps_next = 0
        self.last_by_src = {}
        self.fence_pending = {}
        self.pos = []
        self.nofence = set()

    def barrier(self):
        f = {k: v for k, v in self.last_by_src.items() if k not in self.nofence}
        for e in ("pe", "act", "dve", "pool", "sp"):
            d = dict(self.fence_pending.get(e, {}))
            d.update(f)
            self.fence_pending[e] = d

    def _src(self, idx):
        op = self.ops[idx]
        return ("dma", op[3]) if op[3] is not None else ("eng", op[0])

    def add(self, eng, fn, r=(), w=(), dma_grp=None, pos=None):
        idx = len(self.ops)
        self.pos.append(idx if pos is None else pos)
        deps = self.fence_pending.pop(eng, None) or {}

        def need(d):
            s = self._src(d)
            if deps.get(s, -1) < d:
                deps[s] = d

        for k in r:
            lw = self.lastw.get(k)
            if lw is not None:
                need(lw)
        for k in w:
            lw = self.lastw.get(k)
            if lw is not None:
                need(lw)
            rd = self.readers.get(k)
            if rd:
                for d in rd.values():
                    need(d)
        self.ops.append([eng, fn, deps, dma_grp])
        me = ("dma", dma_grp) if dma_grp is not None else ("eng", eng)
        self.last_by_src[me] = idx
        for k in w:
            self.lastw[k] = idx
            self.readers[k] = {}
        for k in r:
            self.readers.setdefault(k, {})[me] = idx
        return idx

    def mm(self, out, lhsT, rhs, start, stop, r, w):
        return self.add("pe", lambda e: e.matmul(out, lhsT, rhs, start=start, stop=stop), r, w)

    def tr(self, out, in_, ident, r, w):
        return self.add("pe", lambda e: e.transpose(out, in_, ident), r, w)

    def act(self, out, in_, func, r, w, bias=None, scale=None):
        kw = {}
        if bias is not None:
            kw["bias"] = bias
        if scale is not None:
            kw["scale"] = scale
        return self.add("act", lambda e: e.activation(out, in_, func, **kw), r, w)

    def v(self, eng, name, args, r, w, **kw):
        return self.add(eng, lambda e: getattr(e, name)(*args, **kw), r, w)

    def dma(self, q, out, in_, r, w, grp, pos=None, **kw):
        return self.add(q, lambda e: e.dma_start(out=out, in_=in_, **kw), r, w, dma_grp=grp, pos=pos)

    def emit(self, stack, same_engine_sync=True):
        nc = self.nc
        ops = self.ops
        signal = set()
        for idx, (eng, fn, deps, grp) in enumerate(ops):
            for s, d in deps.items():
                if s[0] == "eng":
                    if s[1] == eng and grp is None and (eng == "pe" or not same_engine_sync):
                        continue
                    signal.add(d)
        sem_eng = {e: stack.enter_context(nc.semaphore("sem_" + e)) for e in self.COMPUTE}
        grp_names = []
        for op in ops:
            if op[3] is not None and op[3] not in grp_names:
                grp_names.append(op[3])
        sem_grp = {g: stack.enter_context(nc.semaphore("dg_%d" % i)) for i, g in enumerate(grp_names)}
        cnt = {}
        run = {e: 0 for e in self.COMPUTE}
        rung = {g: 0 for g in grp_names}
        prevg = {}
        for idx, (eng, fn, deps, grp) in enumerate(ops):
            if grp is not None:
                prevg[idx] = rung[grp]
                rung[grp] += 16
                cnt[idx] = rung[grp]
            elif idx in signal:
                run[eng] += 1
                cnt[idx] = run[eng]
        per_eng = {e: [] for e in ("pe", "act", "dve", "pool", "sp")}
        for idx, op in enumerate(ops):
            per_eng[op[0]].append(idx)
        for e_ in per_eng:
            per_eng[e_].sort(key=lambda i: (self.pos[i], i))
        final_g = dict(rung)

        def emit_engine(ename, e):
            waited = {}
            for idx in per_eng[ename]:
                eng, fn, deps, grp = ops[idx]
                for s, d in deps.items():
                    if s[0] == "eng":
                        if s[1] == eng and grp is None and (eng == "pe" or not same_engine_sync):
                            continue
                        sem = sem_eng[s[1]]
                    else:
                        sem = sem_grp[s[1]]
                    c = cnt[d]
                    if waited.get(s, 0) >= c:
                        continue
                    e.wait_ge(sem, c)
                    waited[s] = c
                if grp is not None:
                    s = ("dma", grp)
                    if prevg[idx] > 0 and waited.get(s, 0) < prevg[idx]:
                        e.wait_ge(sem_grp[grp], prevg[idx])
                        waited[s] = prevg[idx]
                    fn(e).then_inc(sem_grp[grp], 16)
                else:
                    ins = fn(e)
                    if idx in signal:
                        ins.then_inc(sem_eng[eng], 1)
            if ename == "sp":
                for g, c in final_g.items():
                    if c > 0:
                        e.wait_ge(sem_grp[g], c)

        with nc.Block() as block:
            @block.tensor
            def _(e):
                emit_engine("pe", e)

            @block.scalar
            def _(e):
                emit_engine("act", e)

            @block.vector
            def _(e):
                emit_engine("dve", e)

            @block.gpsimd
            def _(e):
                emit_engine("pool", e)

            @block.sync
            def _(e):
                emit_engine("sp", e)


NSLOT = 21
NEG = -30000.0


class Builder:
    def __init__(self, cfg):
        self.cfg = cfg
        self.nseq = cfg.get("nseq", 2)
        self.tiles = cfg.get("tiles", list(range(NT)))
        self.layers = cfg.get("layers", list(range(DEPTH)))
        nc = bass.Bass("TRN2", target_bir_lowering=False)
        self.nc = nc
        self.P = Prog(nc)
        self.stack = ExitStack()
        self.pstack = None
        self.nm = 0
        self._decl_dram()

    def din(self, name, shape, dt=F32):
        return self.nc.dram_tensor(name, list(shape), dt, kind="ExternalInput").ap()

    def dscr(self, name, shape, dt=F32):
        if self.cfg.get("debug", False):
            return self.nc.dram_tensor(name, list(shape), dt, kind="ExternalOutput").ap()
        return self.nc.dram_tensor(name, list(shape), dt).ap()

    def _decl_dram(self):
        nc = self.nc
        ns = self.nseq
        self.x_in = self.din("x_in", [ns, L, D])
        self.y_out = nc.dram_tensor("y_out", [ns, L, D], F32, kind="ExternalOutput").ap()
        self.norm_mix = self.din("norm_mix", [DEPTH, 128, KC])
        self.norm_mlp = self.din("norm_mlp", [DEPTH, 128, KC])
        self.w_mlp1 = self.din("w_mlp1", [DEPTH, D, DFF])
        self.w_mlp2 = self.din("w_mlp2", [DEPTH, DFF, D])
        self.w_in_c = self.din("w_in_c", [2, D, 2 * D])
        self.v_gain = self.din("v_gain", [2, D])
        self.w_sT = self.din("w_sT", [2, 8, 128, 128])
        self.b_s = self.din("b_s", [2, 8 * 128])
        self.w_out_c = self.din("w_out_c", [2, D, D])
        self.ident = self.din("ident", [128, 128])
        self.w_in_ab = self.din("w_in_ab", [2, D, 6144])
        self.w_out_ab = self.din("w_out_ab", [2, D, D])
        self.qk_gain = self.din("qk_gain", [2, 128, 2])
        self.biasT = self.din("biasT", [2, NH, 128, NSLOT, 128], BF16)
        self.scw = self.din("scw", [2, 128, 4, 24])
        self.hbias = self.din("hbias", [2, 128, 8])
        self.f_w1 = self.din("f_w1", [2, 33, 64])
        self.f_w2 = self.din("f_w2", [2, 64, 64])
        self.f_w3 = self.din("f_w3", [2, 64, 64])
        self.f_wout = self.din("f_wout", [2, 64, 2048])
        self.f_vec = self.din("f_vec", [2, 64, 4])
        self.zfeat = self.din("zfeat", [33, L])
        self.deltas = self.din("deltas", [D_B])
        self.tcol = self.din("tcol", [128, 32])
        self.TF = self.din("TF", [2, 32, 128, 32, 128], BF16)
        self.TI = self.din("TI", [2, L, L], BF16)
        self.wb = {}
        for nm_, ap_ in (("w_mlp1", self.w_mlp1), ("w_mlp2", self.w_mlp2), ("w_in_c", self.w_in_c),
                         ("w_out_c", self.w_out_c), ("w_in_ab", self.w_in_ab), ("w_out_ab", self.w_out_ab)):
            self.wb[nm_] = self.nc.dram_tensor(nm_ + "_bf", list(ap_.shape), BF16).ap()
        self.xs = self.dscr("xs_scratch", [D, L])
        self.qT_s = self.dscr("qT_s", [D_A, L], BF16)
        self.kT_s = self.dscr("kT_s", [D_A, L], BF16)
        self.v_s = self.dscr("v_s", [L, D_A], BF16)
        self.zT_s = self.dscr("zT_s", [3 * D_B, L])
        self.uT_s = self.dscr("uT_s", [D_B, L])
        self.x0_s = self.dscr("x0_s", [D_B, L])
        self.u_s = self.dscr("u_s", [L, D_B], BF16)
        self.abT_s = self.dscr("abT_s", [D, L], BF16)
        self.Ksp = self.dscr("Ksp", [2, 2, L, D_B])

    def sb(self, name, shape, dt):
        return self.stack.enter_context(self.nc.sbuf_tensor(name, list(shape), dt))

    def sbp(self, name, shape, dt):
        self.nm += 1
        return self.pstack.enter_context(self.nc.sbuf_tensor("%s_%d" % (name, self.nm), list(shape), dt))

    def phase_begin(self):
        self.w_hist = []
        self.pstack = ExitStack()
        self.ph = self.nm

    def phase_end(self):
        self.P.barrier()
        self.pstack.close()
        self.pstack = None

    def bank(self, avoid=None):
        b = self.P.ps_next
        if avoid is not None and b == avoid:
            b = (b + 1) % 8
        self.P.ps_next = (b + 1) % 8
        return b

    def build(self):
        P = self.P
        nc = self.nc
        self.ps = [self.stack.enter_context(nc.psum_tensor("ps%d" % i, [128, 512], F32)) for i in range(8)]
        self.ones = self.sb("ones", [128, 128], F32)
        self.identf = self.sb("identf", [128, 128], F32)
        self.onesb = self.sb("onesb", [128, 128], BF16)
        self.identb = self.sb("identb", [128, 128], BF16)
        self.epsc = self.sb("epsc", [128, 4], F32)
        P.v("pool", "memset", (self.ones[:], 1.0), [], ["ones"])
        P.v("pool", "memset", (self.onesb[:], 1.0), [], ["onesb"])
        P.v("pool", "memset", (self.epsc[:, 0:1], D * EPS), [], ["epsc"])
        P.v("pool", "memset", (self.epsc[:, 1:2], EPS), [], ["epsc"])
        P.v("pool", "memset", (self.epsc[:, 2:3], 128 * EPS), [], ["epsc"])
        P.v("pool", "memset", (self.epsc[:, 3:4], 0.0), [], ["epsc"])
        self.epsD = self.epsc[:, 0:1]
        self.eps1 = self.epsc[:, 1:2]
        self.eps128 = self.epsc[:, 2:3]
        P.dma("sp", self.identf[:], self.ident, [], ["identf"], "c_ident")
        P.dma("pool", self.identb[:], self.ident, [], ["identb"], "c_identb")
        self.gvec = self.sb("gvec", [128, 2 * DEPTH, KC], F32)
        self.gsc = self.sb("gsc", [128, 2 * DEPTH, KC], F32)
        P.dma("sp", self.gvec[:, 0:DEPTH, :], self.norm_mix.rearrange("l p k -> p l k"), [], ["gvec"], "c_g1")
        P.dma("sp", self.gvec[:, DEPTH:2 * DEPTH, :], self.norm_mlp.rearrange("l p k -> p l k"), [], ["gvec"], "c_g2")
        P.v("dve", "tensor_scalar", (self.gsc[:], self.gvec[:], math.sqrt(D), None, ALU.mult), ["gvec"], ["gsc"])
        self.qkg = self.sb("qkg", [128, 2, 2], F32)
        P.dma("sp", self.qkg[:], self.qk_gain.rearrange("j p c -> p j c"), [], ["qkg"], "c_qk")
        P.v("dve", "tensor_scalar", (self.qkg[:, :, 1:2], self.qkg[:, :, 1:2], math.sqrt(128.0), None, ALU.mult),
            ["qkg"], ["qkg"])
        srcs = {"w_mlp1": self.w_mlp1, "w_mlp2": self.w_mlp2, "w_in_c": self.w_in_c, "w_out_c": self.w_out_c,
                "w_in_ab": self.w_in_ab, "w_out_ab": self.w_out_ab}
        order = []
        for l in self.layers:
            if l % 2 == 0:
                order += [("w_in_ab", l // 2), ("w_out_ab", l // 2)]
            else:
                order += [("w_in_c", l // 2), ("w_out_c", l // 2)]
            order += [("w_mlp1", l), ("w_mlp2", l)]
        self.wbkeys = {}
        ci = 0
        for nm_, li in order:
            src = srcs[nm_]
            rows = src.shape[1]
            keys = []
            for r0 in range(0, rows, 512):
                key = ("wbk", nm_, li, r0)
                keys.append(key)
                P.dma("pool", self.wb[nm_][li, r0:r0 + 512, :].rearrange("(p a) c -> p a c", p=128),
                      src[li, r0:r0 + 512, :].rearrange("(p a) c -> p a c", p=128), [], [key], "cv%d" % (ci % 4))
                ci += 1
            self.wbkeys[(nm_, li)] = keys
        for g_ in range(4):
            P.nofence.add(("dma", "cv%d" % g_))
        self.conv_pending = True

        even_layers = [l for l in self.layers if l % 2 == 0]
        for l in even_layers:
            if "F" in self.cfg.get("even_phases", ("F",)):
                self.filter_phase(l // 2)
        for s in range(self.nseq):
            i = 0
            first = True
            while i < len(self.layers):
                l = self.layers[i]
                if l % 2 == 0:
                    eph = self.cfg.get("even_phases", ("E1", "ATT", "HA", "HBC"))
                    if "E1" in eph:
                        self.e1_phase(s, l, from_input=first)
                    if "ATT" in eph:
                        self.attn_phase(l // 2)
                    if "HA" in eph:
                        self.hyena_a_phase(l // 2)
                    if "HBC" in eph:
                        self.hyena_bc_phase(l // 2)
                    if "TILE" not in self.cfg.get("even_phases", ("TILE",)):
                        i += 1
                        continue
                    nxt = self.layers[i + 1] if i + 1 < len(self.layers) and self.layers[i + 1] == l + 1 else None
                    last = (i + (2 if nxt is not None else 1)) >= len(self.layers)
                    self.tile_phase(s, even=l, odd=nxt, from_input=False, to_output=last)
                    i += 2 if nxt is not None else 1
                else:
                    last = (i + 1) >= len(self.layers)
                    self.tile_phase(s, even=None, odd=l, from_input=first, to_output=last)
                    i += 1
                first = False
        P.emit(self.stack, same_engine_sync=self.cfg.get("same_sync", True))
        self.stack.close()
        return nc

    def alloc_tile_bufs(self, odd_j=None, double_x=False):
        P = self.P
        self.xT = self.sbp("xT", [128, KC, T], F32)
        self.xk = "xT"
        self.xTbufs = [(self.xT, "xT")]
        if double_x:
            self.xTbufs.append((self.sbp("xTb", [128, KC, T], F32), "xTb"))
        self.xn = self.sbp("xn", [128, KC, T], BF16)
        self.big = self.sbp("big", [128, 32, T], BF16)
        self.wsl = [self.sbp("wsl%d" % i, [128, 16 * 512], BF16) for i in range(3)]
        self.wsl_next = 0
        self.sqb = [self.sbp("sqb%d" % i, [128, T], BF16) for i in range(4)]
        self.rstd = self.sbp("rstd", [128, T], F32)
        self.tmp = [self.sbp("tmp%d" % i, [128, T], F32) for i in range(6)]
        self.tmp_next = 0
        self.tb = [self.sbp("tb%d" % i, [128, T], BF16) for i in range(4)]
        self.tb_next = 0
        self.iot = self.sbp("iot", [128, D], F32)
        if odd_j is not None:
            j = odd_j
            self.vg = self.sbp("vg", [128, D], F32)
            self.bsb = self.sbp("bsb", [128, 8 * 128], F32)
            self.wst = self.sbp("wst", [128, 8, 128], BF16)
            P.dma("sp", self.vg[:], self.v_gain[j].partition_broadcast(128), [], ["vg"], "c_vg")
            P.dma("sp", self.bsb[:], self.b_s[j].partition_broadcast(128), [], ["bsb"], "c_bs")
            P.dma("pool", self.wst[:], self.w_sT[j].rearrange("g q p -> q g p"), [], ["wst"], "c_ws")

    def wslot(self):
        i = self.wsl_next
        self.wsl_next = (i + 1) % 3
        return i

    def tmpslot(self):
        i = self.tmp_next
        self.tmp_next = (i + 1) % len(self.tmp)
        return i

    def tbslot(self):
        i = self.tb_next
        self.tb_next = (i + 1) % len(self.tb)
        return i

    def load_tile(self, s, t, from_input):
        P = self.P
        if from_input:
            for sub in range(4):
                r0 = t * T + sub * 128
                P.dma("sp", self.iot[:], self.x_in[s, r0:r0 + 128, :], [], ["iot"], "g_iot")
                for kq in range(4):
                    b = self.bank()
                    for kk in range(4):
                        k = kq * 4 + kk
                        P.tr(self.ps[b][:, kk * 128:(kk + 1) * 128], self.iot[:, k * 128:(k + 1) * 128],
                             self.identf[:], ["iot", "identf"], [("ps", b)])
                    for kk in range(4):
                        k = kq * 4 + kk
                        P.v("dve", "tensor_copy", (self.xT[:, k, sub * 128:(sub + 1) * 128],
                                                   self.ps[b][:, kk * 128:(kk + 1) * 128]),
                            [("ps", b)], [(self.xk, k)])
        else:
            P.dma("sp", self.xT[:], self.xs.rearrange("(k p) n -> p k n", p=128)[:, :, t * T:(t + 1) * T],
                  [("xs", t)], [(self.xk, k) for k in range(KC)], "g_xT")

    def store_tile(self, s, t, to_output):
        P = self.P
        if to_output:
            for sub in range(4):
                r0 = t * T + sub * 128
                for kq in range(4):
                    b = self.bank()
                    for kk in range(4):
                        k = kq * 4 + kk
                        P.tr(self.ps[b][:, kk * 128:(kk + 1) * 128], self.xT[:, k, sub * 128:(sub + 1) * 128],
                             self.identf[:], [(self.xk, k), "identf"], [("ps", b)])
                    P.act(self.iot[:, kq * 512:(kq + 1) * 512], self.ps[b][:], AF.Copy, [("ps", b)], ["iot"])
                P.dma("sp", self.y_out[s, r0:r0 + 128, :], self.iot[:], ["iot"], [("yout", s, t, sub)], "g_iot")
        else:
            P.dma("sp", self.xs.rearrange("(k p) n -> p k n", p=128)[:, :, t * T:(t + 1) * T], self.xT[:],
                  [(self.xk, k) for k in range(KC)], [("xs", t)], "g_xT")

    def rmsnorm_tile(self, gi):
        P = self.P
        b = self.bank()
        for k in range(KC):
            sl = k % 4
            if k % 2 == 0 or self.conv_pending:
                P.act(self.sqb[sl][:], self.xT[:, k, :], AF.Square, [(self.xk, k)], [("sqb", sl)])
            else:
                P.v("pool", "tensor_tensor", (self.sqb[sl][:], self.xT[:, k, :], self.xT[:, k, :], ALU.mult),
                    [(self.xk, k)], [("sqb", sl)])
            P.mm(self.ps[b][:], self.onesb[:], self.sqb[sl][:], k == 0, k == KC - 1,
                 ["onesb", ("sqb", sl)], [("ps", b)])
        P.act(self.rstd[:], self.ps[b][:], AF.Sqrt, [("ps", b), "epsc"], ["rstd"], bias=self.epsD)
        P.v("dve", "reciprocal", (self.rstd[:], self.rstd[:]), ["rstd"], ["rstd"])
        for k in range(KC):
            P.v("dve", "scalar_tensor_tensor",
                (self.xn[:, k, :], self.xT[:, k, :], self.gsc[:, gi, k:k + 1], self.rstd[:], ALU.mult, ALU.mult),
                [(self.xk, k), "gsc", "rstd"], [("xn", k)])

    def load_w(self, wspec, nk, c0, ncols, r0=0):
        P = self.P
        wap = self.wb[wspec[0]][wspec[1]]
        rkeys = self.wbkeys[wspec]
        i = self.wslot()
        dst = self.wsl[i][:, 0:nk * ncols].rearrange("p (k n) -> p k n", n=ncols)
        src = wap[r0:r0 + nk * 128, :].rearrange("(k p) n -> p k n", p=128)[:, :, c0:c0 + ncols]
        hist = self.w_hist
        pos = None
        if len(hist) >= 2:
            pos = hist[-2] + 0.5
        idx = P.dma("sp", dst, src, rkeys, [("wsl", i)], "g_w%d" % i, pos=pos)
        hist.append(idx)
        return i, dst

    def mlp(self, l):
        P = self.P
        self.rmsnorm_tile(DEPTH + l)
        w1 = ("w_mlp1", l)
        w2 = ("w_mlp2", l)
        for hh in range(2):
            for jb in range(8):
                wi, wv = self.load_w(w1, KC, (hh * 8 + jb) * 512, 512)
                for jj in range(4):
                    j = jb * 4 + jj
                    b = self.bank()
                    for k in range(KC):
                        P.mm(self.ps[b][:], wv[:, k, jj * 128:(jj + 1) * 128], self.xn[:, k, :], k == 0, k == KC - 1,
                             [("wsl", wi), ("xn", k)], [("ps", b)])
                    ts = self.tmpslot()
                    P.act(self.tmp[ts][:], self.ps[b][:], AF.Relu, [("ps", b)], [("tmp", ts)])
                    P.v("pool", "tensor_tensor", (self.big[:, j, :], self.tmp[ts][:], self.tmp[ts][:], ALU.mult),
                        [("tmp", ts)], [("big", j)])
            for mp in range(8):
                i, dst = self.load_w(w2, 32, mp * 256, 256, r0=hh * 4096)
                for m2 in range(2):
                    m = mp * 2 + m2
                    b = self.bank()
                    for j in range(32):
                        P.mm(self.ps[b][:], dst[:, j, m2 * 128:(m2 + 1) * 128], self.big[:, j, :], j == 0, j == 31,
                             [("wsl", i), ("big", j)], [("ps", b)])
                    P.v("dve", "tensor_tensor", (self.xT[:, m, :], self.xT[:, m, :], self.ps[b][:], ALU.add),
                        [(self.xk, m), ("ps", b)], [(self.xk, m)])

    def gelu(self, out, b, wkeys):
        P = self.P
        a = self.tmpslot()
        c = self.tmpslot()
        xs_, t_ = self.tmp[a], self.tmp[c]
        P.act(xs_[:], self.ps[b][:], AF.Copy, [("ps", b)], [("tmp", a)])
        P.v("pool", "tensor_tensor", (t_[:], xs_[:], xs_[:], ALU.mult), [("tmp", a)], [("tmp", c)])
        P.v("pool", "tensor_scalar", (t_[:], t_[:], 0.044715, 1.0, ALU.mult, ALU.add), [("tmp", c)], [("tmp", c)])
        P.v("dve", "tensor_tensor", (t_[:], t_[:], xs_[:], ALU.mult), [("tmp", a), ("tmp", c)], [("tmp", c)])
        P.act(t_[:], t_[:], AF.Sigmoid, [("tmp", c)], [("tmp", c)], scale=2.0 * GELU_C)
        P.v("dve", "tensor_tensor", (out, t_[:], xs_[:], ALU.mult), [("tmp", a), ("tmp", c)], wkeys)

    def mixer_c(self, l):
        P = self.P
        j = l // 2
        self.rmsnorm_tile(l)
        w_in = ("w_in_c", j)
        for cb in range(4):
            wi, wv = self.load_w(w_in, KC, cb * 512, 512)
            for cc in range(4):
                c = cb * 4 + cc
                b = self.bank()
                for k in range(KC):
                    P.mm(self.ps[b][:], wv[:, k, cc * 128:(cc + 1) * 128], self.xn[:, k, :], k == 0, k == KC - 1,
                         [("wsl", wi), ("xn", k)], [("ps", b)])
                self.gelu(self.big[:, c, :], b, [("big", c)])
        vt = [self.big[:, 16 + 4 * sub:16 + 4 * sub + 4, :].rearrange("p a b -> p (a b)") for sub in range(4)]
        vkeys = [[("big", 16 + 4 * sub + i) for i in range(4)] for sub in range(4)]
        for cb in range(4):
            wi, wv = self.load_w(w_in, KC, D + cb * 512, 512)
            for sub in range(4):
                b = self.bank()
                for k in range(KC):
                    P.mm(self.ps[b][:], self.xn[:, k, sub * 128:(sub + 1) * 128], wv[:, k, :], k == 0, k == KC - 1,
                         [("wsl", wi), ("xn", k)], [("ps", b)])
                self.gelu(vt[sub][:, cb * 512:(cb + 1) * 512], b, [vkeys[sub][cb]])
        for sub in range(4):
            a = self.tmpslot()
            c = self.tmpslot()
            ssq = self.tmp[a]
            junk = self.tmp[c]
            for cb in range(4):
                P.act(junk[:], vt[sub][:, cb * 512:(cb + 1) * 512], AF.Square, [vkeys[sub][cb]], [("tmp", c)])
                P.v("dve", "reduce_sum", (ssq[:, cb:cb + 1], junk[:], mybir.AxisListType.X),
                    [("tmp", c)], [("tmp", a)])
            P.v("dve", "reduce_sum", (ssq[:, 4:5], ssq[:, 0:4], mybir.AxisListType.X), [("tmp", a)], [("tmp", a)])
            P.act(ssq[:, 5:6], ssq[:, 4:5], AF.Sqrt, [("tmp", a), "epsc"], [("tmp", a)], bias=self.eps1, scale=1.0 / D)
            P.v("dve", "reciprocal", (ssq[:, 6:7], ssq[:, 5:6]), [("tmp", a)], [("tmp", a)])
            for cb in range(4):
                P.v("dve", "scalar_tensor_tensor",
                    (vt[sub][:, cb * 512:(cb + 1) * 512], vt[sub][:, cb * 512:(cb + 1) * 512], ssq[:, 6:7],
                     self.vg[:, cb * 512:(cb + 1) * 512], ALU.mult, ALU.mult),
                    [vkeys[sub][cb], ("tmp", a), "vg"], [vkeys[sub][cb]])
        for sub in range(4):
            for cq in range(4):
                b = self.bank()
                for cc in range(4):
                    c = cq * 4 + cc
                    g = c // 2
                    P.mm(self.ps[b][:, cc * 128:(cc + 1) * 128], vt[sub][:, c * 128:(c + 1) * 128],
                         self.wst[:, g, :], True, True, [vkeys[sub][cq], "wst"], [("ps", b)])
                for cc in range(4):
                    c = cq * 4 + cc
                    g = c // 2
                    ts = self.tmpslot()
                    P.v("dve", "tensor_tensor", (self.tmp[ts][:, 0:128], self.ps[b][:, cc * 128:(cc + 1) * 128],
                                                 self.bsb[:, g * 128:(g + 1) * 128], ALU.add),
                        [("ps", b), "bsb"], [("tmp", ts)])
                    P.v("pool", "tensor_tensor", (self.big[:, c, sub * 128:(sub + 1) * 128], self.tmp[ts][:, 0:128],
                                                  self.big[:, c, sub * 128:(sub + 1) * 128], ALU.mult),
                        [("tmp", ts), ("big", c)], [("big", c)])
        self.proj_residual(("w_out_c", j))

    def proj_residual(self, w_o, src=None, skey="big"):
        P = self.P
        if src is None:
            src = self.big
        for mb in range(4):
            wi, wv = self.load_w(w_o, KC, mb * 512, 512)
            for m4 in range(4):
                m = mb * 4 + m4
                b = self.bank()
                for k in range(KC):
                    P.mm(self.ps[b][:], wv[:, k, m4 * 128:(m4 + 1) * 128], src[:, k, :], k == 0, k == KC - 1,
                         [("wsl", wi), (skey, k)], [("ps", b)])
                P.v("dve", "tensor_tensor", (self.xT[:, m, :], self.xT[:, m, :], self.ps[b][:], ALU.add),
                    [(self.xk, m), ("ps", b)], [(self.xk, m)])

    def tile_phase(self, s, even, odd, from_input, to_output):
        P = self.P
        if even is None and self.conv_pending:
            P.nofence.clear()
            P.barrier()
            self.conv_pending = False
        self.phase_begin()
        self.alloc_tile_bufs(odd_j=(odd // 2 if odd is not None else None))
        if even is not None:
            abuf = self.sbp("abuf", [128, KC, T], BF16)

            def fetch_ab(t):
                P.dma("sp", abuf[:], self.abT_s.rearrange("(k p) n -> p k n", p=128)[:, :, t * T:(t + 1) * T],
                      [("abT", t)], [("abuf", k) for k in range(KC)], "g_ab")

            fetch_ab(self.tiles[0])
        for ti, t in enumerate(self.tiles):
            self.load_tile(s, t, from_input=from_input)
            if even is not None:
                self.proj_residual(("w_out_ab", even // 2), src=abuf, skey="abuf")
                if ti + 1 < len(self.tiles):
                    fetch_ab(self.tiles[ti + 1])
                self.mlp(even)
            if odd is not None:
                self.mixer_c(odd)
                self.mlp(odd)
            self.store_tile(s, t, to_output=to_output)
        self.phase_end()

    def e1_phase(self, s, l, from_input):
        P = self.P
        j = l // 2
        self.phase_begin()
        self.alloc_tile_bufs(double_x=True)
        w_in = ("w_in_ab", j)

        def fetch(ti):
            self.xT, self.xk = self.xTbufs[ti % 2]
            self.load_tile(s, self.tiles[ti], from_input=from_input)
            if from_input:
                self.store_tile(s, self.tiles[ti], to_output=False)

        fetch(0)
        for ti, t in enumerate(self.tiles):
            self.xT, self.xk = self.xTbufs[ti % 2]
            self.rmsnorm_tile(l)
            if ti + 1 < len(self.tiles):
                fetch(ti + 1)
                self.xT, self.xk = self.xTbufs[ti % 2]
            tsl = slice(t * T, (t + 1) * T)
            pending = []

            def tail(b, a, isq, h):
                b2 = self.bank()
                P.mm(self.ps[b2][:], self.onesb[:], self.sqb[a][:], True, True, ["onesb", ("sqb", a)], [("ps", b2)])
                c = self.tmpslot()
                P.act(self.tmp[c][:], self.ps[b2][:], AF.Sqrt, [("ps", b2), "epsc"], [("tmp", c)], bias=self.eps128)
                P.v("dve", "reciprocal", (self.tmp[c][:], self.tmp[c][:]), [("tmp", c)], [("tmp", c)])
                o = self.tbslot()
                P.v("dve", "scalar_tensor_tensor",
                    (self.tb[o][:], self.ps[b][:], self.qkg[:, j, (0 if isq else 1):(1 if isq else 2)],
                     self.tmp[c][:], ALU.mult, ALU.mult), [("ps", b), "qkg", ("tmp", c)], [("tb", o)])
                dst = (self.qT_s if isq else self.kT_s)[h * 128:(h + 1) * 128, tsl]
                P.dma("sp", dst, self.tb[o][:], [("tb", o)], [("qk", isq, h, t)], "g_tb%d" % o)

            nq = 0
            for blk in range(4):
                wi, wv = self.load_w(w_in, KC, blk * 512, 512)
                isq = blk < 2
                for cc in range(4):
                    h = (blk % 2) * 4 + cc
                    b = self.bank()
                    for k in range(KC):
                        P.mm(self.ps[b][:], wv[:, k, cc * 128:(cc + 1) * 128], self.xn[:, k, :], k == 0, k == KC - 1,
                             [("wsl", wi), ("xn", k)], [("ps", b)])
                    a = nq % 4
                    nq += 1
                    P.act(self.sqb[a][:], self.ps[b][:], AF.Square, [("ps", b)], [("sqb", a)])
                    if pending:
                        tail(*pending.pop())
                    pending.append((b, a, isq, h))
            tail(*pending.pop())
            for blk in range(2):
                wi, wv = self.load_w(w_in, KC, 2048 + blk * 512, 512)
                for sub in range(4):
                    b = self.bank()
                    for k in range(KC):
                        P.mm(self.ps[b][:], self.xn[:, k, sub * 128:(sub + 1) * 128], wv[:, k, :], k == 0, k == KC - 1,
                             [("wsl", wi), ("xn", k)], [("ps", b)])
                    o = self.tbslot()
                    P.act(self.tb[o][:], self.ps[b][:], AF.Copy, [("ps", b)], [("tb", o)])
                    r0 = t * T + sub * 128
                    P.dma("sp", self.v_s[r0:r0 + 128, blk * 512:(blk + 1) * 512], self.tb[o][:],
                          [("tb", o)], [("vs", t, sub, blk)], "g_tb%d" % o)
            for blk in range(6):
                wi, wv = self.load_w(w_in, KC, 3072 + blk * 512, 512)
                for cc in range(4):
                    c = blk * 4 + cc
                    b = self.bank()
                    for k in range(KC):
                        P.mm(self.ps[b][:], wv[:, k, cc * 128:(cc + 1) * 128], self.xn[:, k, :], k == 0, k == KC - 1,
                             [("wsl", wi), ("xn", k)], [("ps", b)])
                    a = self.tmpslot()
                    P.act(self.tmp[a][:], self.ps[b][:], AF.Copy, [("ps", b)], [("tmp", a)])
                    P.dma("sp", self.zT_s[c * 128:(c + 1) * 128, tsl], self.tmp[a][:],
                          [("tmp", a)], [("zs", c, t)], "g_tmp%d" % a)
        self.conv_pending = False
        self.phase_end()

    def attn_phase(self, j):
        P = self.P
        self.phase_begin()
        qT = [self.sbp("qT", [128, L], BF16) for _ in range(2)]
        kT = [self.sbp("kT", [128, L], BF16) for _ in range(2)]
        vh = [self.sbp("vh", [128, 32, 128], BF16) for _ in range(2)]
        bh = [self.sbp("bh", [128, NSLOT, 128], BF16) for _ in range(2)]
        aT = [self.sbp("aT", [128, L], BF16) for _ in range(2)]
        eT = [self.sbp("eT", [128, 5, 128], BF16) for _ in range(3)]
        rd = [self.sbp("rd", [128, 128], F32) for _ in range(3)]
        for h in range(NH):
            sl = h % 2
            P.dma("sp", qT[sl][:], self.qT_s[h * 128:(h + 1) * 128, :], [("qk", True, h)], [("qT", sl)], "a_q%d" % sl)
            P.dma("sp", kT[sl][:], self.kT_s[h * 128:(h + 1) * 128, :], [("qk", False, h)], [("kT", sl)], "a_k%d" % sl)
            P.dma("sp", vh[sl][:], self.v_s.rearrange("(n p) c -> p n c", p=128)[:, :, h * 128:(h + 1) * 128],
                  [], [("vh", sl)], "a_v%d" % sl)
            P.dma("sp", bh[sl][:], self.biasT[j, h], [], [("bh", sl)], "a_b%d" % sl)
            def pv(nch, krow0, ei, qs, sl=sl):
                b3 = self.bank()
                for c in range(nch):
                    vi = krow0 // 2 + c
                    P.mm(self.ps[b3][:, 0:128], vh[sl][:, vi, :], eT[ei][:, c, :], c == 0, c == nch - 1,
                         [("vh", sl), ("eT", ei)], [("ps", b3)])
                for c in range(nch):
                    P.mm(self.ps[b3][:, 128:256], self.onesb[:], eT[ei][:, c, :], c == 0, c == nch - 1,
                         ["onesb", ("eT", ei)], [("ps", b3)])
                P.v("dve", "reciprocal", (rd[ei][:], self.ps[b3][:, 128:256]), [("ps", b3)], [("rd", ei)])
                P.v("dve", "tensor_tensor", (aT[sl][:, qs], self.ps[b3][:, 0:128], rd[ei][:], ALU.mult),
                    [("ps", b3), ("rd", ei)], [("aT", sl)])

            pend = None
            for m in range(32):
                if 2 <= m <= 29:
                    nch, krow0, slot0 = 5, 2 * m - 4, 0
                elif m < 2:
                    nch, krow0, slot0 = 4, 0, 5 + 4 * m
                else:
                    nch, krow0, slot0 = 4, 56, 13 + 4 * (m - 30)
                qs = slice(m * 128, (m + 1) * 128)
                b1 = self.bank()
                b2 = self.bank() if nch == 5 else None
                ei = (h * 32 + m) % 3
                for c in range(nch):
                    bb, co = (b1, c) if c < 4 else (b2, 0)
                    k0 = (krow0 + 2 * c) * 64
                    outp = self.ps[bb][:, co * 128:(co + 1) * 128]
                    P.mm(outp, kT[sl][:, k0:k0 + 128], qT[sl][:, qs], True, False,
                         [("kT", sl), ("qT", sl)], [("ps", bb)])
                    P.mm(outp, self.identb[:], bh[sl][:, slot0 + c, :], False, True,
                         ["identb", ("bh", sl)], [("ps", bb)])
                P.act(eT[ei][:, 0:4, :].rearrange("p a b -> p (a b)"), self.ps[b1][:], AF.Exp,
                      [("ps", b1)], [("eT", ei)])
                if nch == 5:
                    P.act(eT[ei][:, 4, :], self.ps[b2][:, 0:128], AF.Exp, [("ps", b2)], [("eT", ei)])
                if pend is not None:
                    pv(*pend)
                pend = (nch, krow0, ei, qs)
            pv(*pend)
            pend = None
            P.dma("sp", self.abT_s[h * 128:(h + 1) * 128, :], aT[sl][:], [("aT", sl)], [("abT", "a", h)], "a_o%d" % sl)
        self.phase_end()

    def hyena_a_phase(self, j):
        P = self.P
        self.phase_begin()
        zb = [self.sbp("zb", [128, L + 2], F32) for _ in range(3)]
        cv = [self.sbp("cv", [128, L], F32) for _ in range(3)]
        uT = self.sbp("uT", [128, L], F32)
        ust = self.sbp("ust", [128, 32, 128], BF16)
        scw = self.sbp("scw", [128, 4, 24], F32)
        P.dma("sp", scw[:], self.scw[j], [], ["scw"], "h_scw")
        for i in range(3):
            P.v("pool", "memset", (zb[i][:, 0:1], 0.0), [], [("zb", i)])
            P.v("pool", "memset", (zb[i][:, L + 1:L + 2], 0.0), [], [("zb", i)])
        for cc in range(8):
            for i in range(3):
                ci = i * 8 + cc
                eng = "dve"
                P.dma("sp", zb[i][:, 1:L + 1], self.zT_s[ci * 128:(ci + 1) * 128, :], [("zs", ci)], [("zb", i)], "h_z%d" % i)
                P.act(cv[i][:], zb[i][:, 1:L + 1], AF.Identity, [("zb", i), "scw"], [("cv", i)],
                      bias=scw[:, 3, ci:ci + 1], scale=scw[:, 1, ci:ci + 1])
                P.v(eng, "scalar_tensor_tensor", (cv[i][:], zb[i][:, 0:L], scw[:, 0, ci:ci + 1], cv[i][:],
                                                  ALU.mult, ALU.add), [("zb", i), "scw", ("cv", i)], [("cv", i)])
                P.v(eng, "scalar_tensor_tensor", (cv[i][:], zb[i][:, 2:L + 2], scw[:, 2, ci:ci + 1], cv[i][:],
                                                  ALU.mult, ALU.add), [("zb", i), "scw", ("cv", i)], [("cv", i)])
            P.v("pool", "tensor_tensor", (uT[:], cv[2][:], cv[1][:], ALU.mult), [("cv", 1), ("cv", 2)], ["uT"])
            P.dma("sp", self.uT_s[cc * 128:(cc + 1) * 128, :], uT[:], ["uT"], [("uTs", cc)], "h_u")
            P.dma("sp", self.x0_s[cc * 128:(cc + 1) * 128, :], cv[0][:], [("cv", 0)], [("x0s", cc)], "h_x0")
            for g in range(8):
                b = self.bank()
                for q in range(4):
                    n = g * 4 + q
                    P.tr(self.ps[b][:, q * 128:(q + 1) * 128], uT[:, n * 128:(n + 1) * 128], self.identf[:],
                         ["uT", "identf"], [("ps", b)])
                P.act(ust[:, g * 4:(g + 1) * 4, :].rearrange("p a b -> p (a b)"), self.ps[b][:], AF.Copy,
                      [("ps", b)], ["ust"])
            P.dma("sp", self.u_s.rearrange("(n p) c -> p n c", p=128)[:, :, cc * 128:(cc + 1) * 128], ust[:],
                  ["ust"], [("us", cc)], "h_us")
        self.phase_end()

    def dft_fwd(self, srcC, keysC, srcS, keysS, tf, consume):
        P = self.P
        for fi in range(32):
            banks = []
            for cs in range(2):
                sl = (fi * 2 + cs) % 4
                P.dma("sp", tf[sl][:], self.TF[cs, fi], [], [("tf", sl)], "d_tf%d" % sl)
                b = self.bank()
                src, keys = (srcC, keysC) if cs == 0 else (srcS, keysS)
                for n in range(32):
                    P.mm(self.ps[b][:], tf[sl][:, n, :], src[:, n, :], n == 0, n == 31,
                         [("tf", sl), keys(n)], [("ps", b)])
                banks.append(b)
            consume(fi, banks[0], banks[1])

    def hyena_bc_phase(self, j):
        P = self.P
        self.phase_begin()
        uti = self.sbp("uti", [128, 32, 512], BF16)
        Y = self.sbp("Y", [128, 64, 512], BF16)
        tf = [self.sbp("tf", [128, 32, 128], BF16) for _ in range(4)]
        kt = [self.sbp("kt", [128, 512], F32) for _ in range(4)]
        et = [self.sbp("et", [128, 512], F32) for _ in range(4)]
        tt = [self.sbp("tt", [128, 512], F32) for _ in range(4)]
        ob = [self.sbp("ob", [128, 512], BF16) for _ in range(2)]
        hb = self.sbp("hb", [128, 8], F32)
        P.dma("sp", hb[:], self.hbias[j], [], ["hb"], "d_hb")
        cnt = [0, 0, 0]
        for ch in range(2):
            csl = slice(ch * 512, (ch + 1) * 512)
            P.dma("sp", uti[:], self.u_s.rearrange("(n p) c -> p n c", p=128)[:, :, csl],
                  [("us", c) for c in range(8)], [("uti", 0), ("uti", 1)], "d_u")

            def consume(fi, bP, bQ, ch=ch, csl=csl):
                k0 = cnt[0] % 2
                cnt[0] += 1
                kp, kq = kt[2 * k0], kt[2 * k0 + 1]
                fsl = slice(fi * 128, (fi + 1) * 128)
                P.dma("sp", kp[:], self.Ksp[j, 0, fsl, csl], [("Ksp", j)], [("kt", 2 * k0)], "d_kp%d" % k0)
                P.dma("sp", kq[:], self.Ksp[j, 1, fsl, csl], [("Ksp", j)], [("kt", 2 * k0 + 1)], "d_kq%d" % k0)
                t0, t1, t2, t3 = tt
                base = (cnt[0] % 1) * 0
                P.v("dve", "tensor_tensor", (t0[:], self.ps[bP][:], kp[:], ALU.mult), [("ps", bP), ("kt", 2 * k0)], [("tt", 0)])
                P.v("dve", "tensor_tensor", (t1[:], self.ps[bQ][:], kq[:], ALU.mult), [("ps", bQ), ("kt", 2 * k0 + 1)], [("tt", 1)])
                P.v("dve", "tensor_tensor", (t2[:], self.ps[bP][:], kq[:], ALU.mult), [("ps", bP), ("kt", 2 * k0 + 1)], [("tt", 2)])
                P.v("dve", "tensor_tensor", (t3[:], self.ps[bQ][:], kp[:], ALU.mult), [("ps", bQ), ("kt", 2 * k0)], [("tt", 3)])
                P.v("pool", "tensor_tensor", (Y[:, fi, :], t0[:], t1[:], ALU.subtract), [("tt", 0), ("tt", 1)], [("Y", fi)])
                P.v("pool", "tensor_tensor", (Y[:, 32 + fi, :], t2[:], t3[:], ALU.add), [("tt", 2), ("tt", 3)], [("Y", 32 + fi)])

            self.dft_fwd(uti, lambda n: ("uti", n // 16), uti, lambda n: ("uti", n // 16), tf, consume)
            for tb in range(8):
                tsl = slice(tb * 512, (tb + 1) * 512)
                banks = [self.bank() for _ in range(4)]
                for kg in range(4):
                    cs, f0 = kg // 2, (kg % 2) * 2048
                    sl = cnt[1] % 2
                    cnt[1] += 1
                    ti = uti[:, sl * 16:(sl + 1) * 16, :]
                    P.dma("sp", ti, self.TI[cs, f0:f0 + 2048, :].rearrange("(k p) t -> p k t", p=128)[:, :, tsl],
                          [], [("uti", sl)], "d_ti%d" % sl)
                    for c in range(4):
                        for kk in range(16):
                            yk = cs * 32 + (kg % 2) * 16 + kk
                            P.mm(self.ps[banks[c]][:], Y[:, yk, c * 128:(c + 1) * 128], ti[:, kk, :],
                                 kg == 0 and kk == 0, kg == 3 and kk == 15,
                                 [("Y", yk), ("uti", sl)], [("ps", banks[c])])
                for c in range(4):
                    cc = ch * 4 + c
                    e0 = cnt[2] % 2
                    cnt[2] += 1
                    eu, ex = et[2 * e0], et[2 * e0 + 1]
                    P.dma("sp", eu[:], self.uT_s[cc * 128:(cc + 1) * 128, tsl], [("uTs", cc)], [("et", 2 * e0)], "d_eu%d" % e0)
                    P.dma("sp", ex[:], self.x0_s[cc * 128:(cc + 1) * 128, tsl], [("x0s", cc)], [("et", 2 * e0 + 1)], "d_ex%d" % e0)
                    P.v("dve", "scalar_tensor_tensor", (eu[:], eu[:], hb[:, cc:cc + 1], self.ps[banks[c]][:], ALU.mult, ALU.add),
                        [("et", 2 * e0), "hb", ("ps", banks[c])], [("et", 2 * e0)])
                    P.v("pool", "tensor_tensor", (ob[e0][:], eu[:], ex[:], ALU.mult),
                        [("et", 2 * e0), ("et", 2 * e0 + 1)], [("ob", e0)])
                    P.dma("sp", self.abT_s[1024 + cc * 128:1024 + (cc + 1) * 128, tsl], ob[e0][:],
                          [("ob", e0)], [("abT", "b", cc, tb)], "d_ob%d" % e0)
        self.phase_end()

    def filter_phase(self, j):
        P = self.P
        self.phase_begin()
        zf = self.sbp("zf", [33, L], F32)
        hA = self.sbp("hA", [64, L], F32)
        hB = self.sbp("hB", [64, L], F32)
        fw1 = self.sbp("fw1", [33, 64], F32)
        fw2 = self.sbp("fw2", [64, 64], F32)
        fw3 = self.sbp("fw3", [64, 64], F32)
        fwo = self.sbp("fwo", [64, 2048], F32)
        fv = self.sbp("fv", [64, 4], F32)
        fs = self.sbp("fs", [64, 4], F32)
        dl = self.sbp("dl", [128, D_B], F32)
        tc = self.sbp("tc", [128, 32], F32)
        hs = self.sbp("hs", [128, 32, 512], BF16)
        hd = self.sbp("hd", [128, 32, 512], BF16)
        dec = [self.sbp("dec", [128, 512], F32) for _ in range(2)]
        hf = [self.sbp("hf", [128, 512], F32) for _ in range(2)]
        hbw = [self.sbp("hbw", [128, 512], F32) for _ in range(2)]
        ab = [self.sbp("ab", [128, 512], F32) for _ in range(2)]
        ab2 = [self.sbp("ab2", [128, 512], F32) for _ in range(2)]
        rn = self.sbp("rn", [128, 512], F32)
        st = [self.sbp("st", [64, 512], F32) for _ in range(2)]
        s2 = [self.sbp("s2", [64, 512], F32) for _ in range(2)]
        tf = [self.sbp("tf", [128, 32, 128], BF16) for _ in range(4)]
        ko = [self.sbp("ko", [128, 512], F32) for _ in range(4)]
        P.dma("sp", zf[:], self.zfeat, [], ["zf"], "f_c0")
        P.dma("sp", fw1[:], self.f_w1[j], [], ["fw1"], "f_c1")
        P.dma("sp", fw2[:], self.f_w2[j], [], ["fw2"], "f_c2")
        P.dma("sp", fw3[:], self.f_w3[j], [], ["fw3"], "f_c3")
        P.dma("sp", fwo[:], self.f_wout[j], [], ["fwo"], "f_c4")
        P.dma("sp", fv[:], self.f_vec[j], [], ["fv"], "f_c5")
        P.dma("sp", dl[:], self.deltas.partition_broadcast(128), [], ["dl"], "f_c6")
        P.dma("sp", tc[:], self.tcol, [], ["tc"], "f_c7")
        P.v("dve", "tensor_scalar", (fs[:, 0:1], fv[:, 3:4], 1.0 / 3.0, None, ALU.mult), ["fv"], ["fs"])
        P.v("dve", "tensor_scalar", (fs[:, 1:4], fv[:, 0:3], fs[:, 0:1], None, ALU.mult), ["fv", "fs"], ["fs"])
        P.v("dve", "tensor_scalar", (tc[:], tc[:], -1.0, None, ALU.mult), ["tc"], ["tc"])

        def sin_layer(w, wkey, src, skey, dst, dkey, li, kdim):
            for nb in range(8):
                b = self.bank()
                P.mm(self.ps[b][0:64, :], w[0:kdim, :], src[0:kdim, nb * 512:(nb + 1) * 512], True, True,
                     [wkey, skey], [("ps", b)])
                a = nb % 2
                P.act(st[a][:], self.ps[b][0:64, :], AF.Sin, [("ps", b), "fs"], [("st", a)],
                      bias=fs[:, li:li + 1], scale=fs[:, 0:1])
                P.v("dve", "tensor_tensor", (s2[a][:], st[a][:], st[a][:], ALU.mult), [("st", a)], [("s2", a)])
                P.v("dve", "tensor_scalar", (s2[a][:], s2[a][:], -4.0, 3.0, ALU.mult, ALU.add), [("s2", a)], [("s2", a)])
                P.v("dve", "tensor_tensor", (dst[:, nb * 512:(nb + 1) * 512], s2[a][:], st[a][:], ALU.mult),
                    [("st", a), ("s2", a)], [dkey])

        sin_layer(fw1, "fw1", zf, "zf", hA, "hA", 1, 33)
        sin_layer(fw2, "fw2", hA, "hA", hB, "hB", 2, 64)
        sin_layer(fw3, "fw3", hB, "hB", hA, "hA", 3, 64)
        cnt = [0]
        for ch in range(2):
            csl = slice(ch * 512, (ch + 1) * 512)
            bn = self.bank()
            for n in range(32):
                a = n % 2
                bf_, bb_ = self.bank(avoid=bn), self.bank(avoid=bn)
                P.mm(self.ps[bf_][:], hA[:, n * 128:(n + 1) * 128], fwo[:, ch * 512:(ch + 1) * 512], True, True,
                     ["hA", "fwo"], [("ps", bf_)])
                P.mm(self.ps[bb_][:], hA[:, n * 128:(n + 1) * 128], fwo[:, D_B + ch * 512:D_B + (ch + 1) * 512], True, True,
                     ["hA", "fwo"], [("ps", bb_)])
                P.act(dec[a][:], dl[:, csl], AF.Exp, ["dl", "tc"], [("dec", a)], scale=tc[:, n:n + 1])
                P.v("dve", "tensor_tensor", (hf[a][:], self.ps[bf_][:], dec[a][:], ALU.mult),
                    [("ps", bf_), ("dec", a)], [("hf", a)])
                P.v("dve", "tensor_tensor", (hbw[a][:], self.ps[bb_][:], dec[a][:], ALU.mult),
                    [("ps", bb_), ("dec", a)], [("hbw", a)])
                if n == 0:
                    P.v("dve", "memset", (hbw[a][0:1, :], 0.0), [("hbw", a)], [("hbw", a)])
                P.act(ab[a][:], hf[a][:], AF.Abs, [("hf", a)], [("ab", a)])
                P.act(ab2[a][:], hbw[a][:], AF.Abs, [("hbw", a)], [("ab2", a)])
                P.v("dve", "tensor_tensor", (ab[a][:], ab[a][:], ab2[a][:], ALU.add), [("ab", a), ("ab2", a)], [("ab", a)])
                P.mm(self.ps[bn][:], self.ones[:], ab[a][:], n == 0, n == 31, ["ones", ("ab", a)], [("ps", bn)])
                P.v("dve", "tensor_tensor", (hs[:, n, :], hf[a][:], hbw[a][:], ALU.add), [("hf", a), ("hbw", a)], [("hs", n // 16)])
                P.v("dve", "tensor_tensor", (hd[:, n, :], hf[a][:], hbw[a][:], ALU.subtract), [("hf", a), ("hbw", a)], [("hd", n // 16)])
            P.v("dve", "reciprocal", (rn[:], self.ps[bn][:]), [("ps", bn)], ["rn"])

            def consume(fi, bP, bQ, ch=ch, csl=csl):
                k0 = cnt[0] % 2
                cnt[0] += 1
                fsl = slice(fi * 128, (fi + 1) * 128)
                for cs, bb in ((0, bP), (1, bQ)):
                    o = 2 * k0 + cs
                    P.v("dve", "tensor_tensor", (ko[o][:], self.ps[bb][:], rn[:], ALU.mult), [("ps", bb), "rn"], [("ko", o)])
                    P.dma("sp", self.Ksp[j, cs, fsl, csl], ko[o][:], [("ko", o)], [("Ksp", j, cs, fi, ch)], "f_ko%d" % o)

            self.dft_fwd(hs, lambda n: ("hs", n // 16), hd, lambda n: ("hd", n // 16), tf, consume)
        self.phase_end()


_CONST_CACHE = {}


def algo_consts():
    if _CONST_CACHE:
        return _CONST_CACHE
    f32 = np.float32
    c = {}
    c["ident"] = np.eye(128, dtype=f32)
    t = np.linspace(0.0, 1.0, L, dtype=f32)[:, None]
    w = (2.0 * np.pi * np.arange(L, dtype=f32)[:, None] / L).astype(f32)
    f = np.linspace(1e-4, 15, 16, dtype=f32)[None, :]
    fw = (f * w).astype(f32)
    z = np.concatenate([t, np.cos(fw), -np.sin(fw)], axis=-1).astype(f32)
    c["zfeat"] = np.ascontiguousarray(z.T)
    c["deltas"] = np.abs(np.linspace(math.log(1e-2) / 0.3, math.log(1e-2) / 1.5, D_B, dtype=f32)).astype(f32)
    c["tcol"] = np.ascontiguousarray(t[:, 0].reshape(32, 128).T)
    n = np.arange(L, dtype=np.float64)[:, None]
    fq = np.arange(L, dtype=np.float64)[None, :]
    ang = np.pi * (2 * fq + 1) * n / (2 * L)
    C = np.cos(ang)
    S = np.sin(ang)
    TF = np.stack([C, S])
    TF = TF.reshape(2, 32, 128, 32, 128).transpose(0, 3, 2, 1, 4)
    c["TF"] = np.ascontiguousarray(TF).astype(ml_dtypes.bfloat16)
    TI = np.stack([C.T, S.T]) / L
    c["TI"] = np.ascontiguousarray(TI).astype(ml_dtypes.bfloat16)
    _CONST_CACHE.update(c)
    return c


def bias_tiles(rpb):
    rpb = np.asarray(rpb, np.float32)
    specs = [(2, [0, 2, 4, 6, 8])] + [(m, [0, 2, 4, 6]) for m in (0, 1)] + [(m, [56, 58, 60, 62]) for m in (30, 31)]
    kc = np.arange(64)[:, None]
    qc = np.arange(64)[None, :]
    cs = np.clip(qc - 8, 0, 48)
    col_ok = (kc >= cs) & (kc < cs + 16)
    dc = np.clip(kc - qc, -15, 15) + 15
    out = np.full((2, NH, 128, NSLOT, 128), NEG, np.float32)
    slot = 0
    for m, krows in specs:
        for kr0 in krows:
            for pk in range(2):
                kr = kr0 + pk
                for pq in range(2):
                    r = 2 * m + pq
                    rs = min(max(r - 4, 0), 56)
                    if not (rs <= kr < rs + 8):
                        continue
                    dr = kr - r + 7
                    vals = rpb[:, :, dr, :][:, :, dc]
                    vals = np.where(col_ok[None, None], vals, NEG)
                    out[:, :, pk * 64:(pk + 1) * 64, slot, pq * 64:(pq + 1) * 64] = vals
            slot += 1
    return out.astype(ml_dtypes.bfloat16)


def host_consts(inp):
    f32 = np.float32
    g = lambda k: np.asarray(inp[k], f32)
    c = dict(algo_consts())
    c["norm_mix"] = np.ascontiguousarray(g("norm_mix").reshape(DEPTH, KC, 128).transpose(0, 2, 1))
    c["norm_mlp"] = np.ascontiguousarray(g("norm_mlp").reshape(DEPTH, KC, 128).transpose(0, 2, 1))
    c["w_mlp1"] = g("w_mlp1")
    c["w_mlp2"] = g("w_mlp2")
    c["w_in_c"] = g("w_in_c")
    c["v_gain"] = g("v_gain")
    c["w_sT"] = np.ascontiguousarray(g("w_s").transpose(0, 1, 3, 2))
    c["b_s"] = np.ascontiguousarray(g("b_s").reshape(2, 8 * 128))
    c["w_out_c"] = g("w_out_c")
    c["w_in_ab"] = g("w_in_ab")
    c["w_out_ab"] = g("w_out_ab")
    c["qk_gain"] = np.ascontiguousarray(np.stack([g("q_gain"), g("k_gain")], axis=-1))
    c["biasT"] = bias_tiles(inp["rpb"])
    scw = np.concatenate([g("sc_w"), g("sc_b")[:, None, :]], axis=1)
    c["scw"] = np.ascontiguousarray(scw.reshape(2, 4, 24, 128).transpose(0, 3, 1, 2))
    c["hbias"] = np.ascontiguousarray(g("h_bias").reshape(2, 8, 128).transpose(0, 2, 1))
    c["f_w1"] = g("f_w1")
    c["f_w2"] = g("f_w2")
    c["f_w3"] = g("f_w3")
    c["f_wout"] = g("f_wout")
    c["f_vec"] = np.ascontiguousarray(np.stack([g("f_b1"), g("f_b2"), g("f_b3"), g("f_freq")], axis=-1))
    return c


def run_cfg(cfg, x_per_core, inp, trace=False):
    b = Builder(cfg)
    nc = b.build()
    consts = host_consts(inp)
    in_maps = []
    for xc in x_per_core:
        m = dict(consts)
        m["x_in"] = np.ascontiguousarray(xc, dtype=np.float32)
        in_maps.append(m)
    res = run_bass_kernel_spmd(nc, in_maps, core_ids=list(range(len(x_per_core))), trace=trace)
    return res


def kernel(**inputs):
    xp = np.asarray(inputs["x_prompt"], np.float32)
    xsm = np.asarray(inputs["x_sample"], np.float32)
    owners = {0: 0, 4: 1}
    zero = np.zeros_like(xp[0])
    x_per_core = []
    for c in range(8):
        second = xsm[owners[c]] if c in owners else zero
        x_per_core.append(np.stack([xp[c], second]))
    res = run_cfg({"nseq": 2}, x_per_core, inputs)
    y_prompt = np.stack([res.results[c]["y_out"][0] for c in range(8)]).astype(np.float32)
    y_sample = np.stack([res.results[c]["y_out"][1] for c in (0, 4)]).astype(np.float32)
    return (y_prompt, y_sample)
```
